# Optimizing a Trainium2 kernel written in Bass

```python
import jax, jax.numpy as jnp
from jax import lax
import numpy as np

D_MODEL = 1024
BATCH = 8
SEQ = 2048
DEPTH = 2
DEC_BATCH = 128
DEC_SEQ = 1
PAST_LEN = 16384
PAGE_SIZE = 128

D_MIX = 2 * D_MODEL
W_A = D_MIX // 2
W_B = D_MIX - W_A
NB_A = 16
BW_A = W_A // NB_A
CONV_W = 4
LRU_C = 8.0
H_B = 4
DK = W_B // H_B
DV = W_B // H_B
CHUNK = 128
EPS = 1e-6
SPLIT_SIZES = (W_A, W_A, W_B, W_B, W_B, W_B, W_B, H_B, H_B)
D_IN = sum(SPLIT_SIZES)
SPLIT_IDX = tuple(int(v) for v in np.cumsum(SPLIT_SIZES)[:-1])

kernel_name = "hymba_rglru_mlstm_decoder_step"


def rmsnorm(x, g):
    xf = x.astype(jnp.float32)
    y = xf * lax.rsqrt(jnp.mean(xf * xf, axis=-1, keepdims=True) + EPS)
    return (y * g.astype(jnp.float32)).astype(x.dtype)


def _lin_combine(e1, e2):
    a1, b1 = e1
    a2, b2 = e2
    return a1 * a2, a2 * b1 + b2


def rglru_branch(xa, conv_buf, h0, conv_w, conv_b, w_r, b_r, w_i, b_i, lam):
    B, S, _ = xa.shape
    ext = jnp.concatenate([conv_buf.astype(xa.dtype), xa], axis=1)
    xc = conv_b.astype(jnp.float32) + sum(
        conv_w[j].astype(jnp.float32) * ext[:, j:j + S].astype(jnp.float32) for j in range(CONV_W))
    new_buf = ext[:, S:]
    blocks = xc.reshape(B, S, NB_A, BW_A)
    r = jax.nn.sigmoid(jnp.einsum('bsnc,ncd->bsnd', blocks, w_r.astype(jnp.float32)).reshape(B, S, W_A)
                       + b_r.astype(jnp.float32))
    i = jax.nn.sigmoid(jnp.einsum('bsnc,ncd->bsnd', blocks, w_i.astype(jnp.float32)).reshape(B, S, W_A)
                       + b_i.astype(jnp.float32))
    log_a = -LRU_C * r * jax.nn.softplus(-lam.astype(jnp.float32))
    a = jnp.exp(log_a)
    u = jnp.sqrt(-jnp.expm1(2.0 * log_a)) * (i * xc)
    A, Bc = lax.associative_scan(_lin_combine, (a, u), axis=1)
    h = A * h0.astype(jnp.float32)[:, None, :] + Bc
    return h, h[:, -1], new_buf


def mlstm_chunk(carry, inp):
    C, n, m = carry
    q, k, v, ig, lf = inp
    L = q.shape[1]
    b = jnp.cumsum(lf, axis=1).transpose(0, 2, 1)
    igh = ig.transpose(0, 2, 1)
    causal = jnp.tril(jnp.ones((L, L), dtype=bool))
    D = jnp.where(causal, b[:, :, :, None] - b[:, :, None, :] + igh[:, :, None, :], -jnp.inf)
    g = b + m[:, :, None]
    m_t = jnp.maximum(g, jnp.max(D, axis=-1))
    w = jnp.exp(D - m_t[..., None])
    ginter = jnp.exp(g - m_t)
    P = w * jnp.einsum('bthk,bshk->bhts', q, k)
    num = (jnp.einsum('bhts,bshv->bthv', P, v)
           + ginter.transpose(0, 2, 1)[..., None] * jnp.einsum('bthk,bhkv->bthv', q, C))
    den = jnp.sum(P, axis=-1) + ginter * jnp.einsum('bthk,bhk->bht', q, n)
    denom = jnp.maximum(jnp.abs(den), jnp.exp(-m_t)).transpose(0, 2, 1)[..., None]
    h = num / denom
    mL = m_t[:, :, -1]
    decay = jnp.exp(b[:, :, -1:] - b + igh - mL[:, :, None])
    carry_scale = jnp.exp(b[:, :, -1] + m - mL)
    C_new = carry_scale[..., None, None] * C + jnp.einsum('bhs,bshk,bshv->bhkv', decay, k, v)
    n_new = carry_scale[..., None] * n + jnp.einsum('bhs,bshk->bhk', decay, k)
    return (C_new, n_new, mL), h


def mlstm_seq(q, k, v, ig, lf, C0, n0, m0):
    B, S = q.shape[:2]
    L = CHUNK if S % CHUNK == 0 else S
    nc = S // L

    def to_chunks(t):
        return jnp.moveaxis(t.reshape((B, nc, L) + t.shape[2:]), 1, 0)

    carry0 = (C0.astype(jnp.float32), n0.astype(jnp.float32), m0.astype(jnp.float32))
    (C, n, m), hs = lax.scan(mlstm_chunk, carry0,
                             (to_chunks(q), to_chunks(k), to_chunks(v), to_chunks(ig), to_chunks(lf)))
    h = jnp.moveaxis(hs, 0, 1).reshape(B, S, H_B, DV)
    return h, C, n, m


def hybrid_layer(x, h0, conv0, C0, n0, m0, g_norm, w_in, conv_w, conv_b, w_rgate, b_rgate,
                 w_igate, b_igate, lru_lambda, b_mi, b_mf, g_mhead, w_out):
    B, S, _ = x.shape
    hn = rmsnorm(x, g_norm)
    u = hn @ w_in
    xa, za, uq, uk, uv, uo, zb, ui, uf = jnp.split(u, SPLIT_IDX, axis=-1)
    yA, hA, bufA = rglru_branch(xa, conv0, h0, conv_w, conv_b, w_rgate, b_rgate, w_igate, b_igate, lru_lambda)
    q = uq.astype(jnp.float32).reshape(B, S, H_B, DK)
    k = uk.astype(jnp.float32).reshape(B, S, H_B, DK) * (DK ** -0.5)
    v = uv.astype(jnp.float32).reshape(B, S, H_B, DV)
    ig = ui.astype(jnp.float32) + b_mi.astype(jnp.float32)
    lf = jax.nn.log_sigmoid(uf.astype(jnp.float32) + b_mf.astype(jnp.float32))
    hB, C, n, m = mlstm_seq(q, k, v, ig, lf, C0, n0, m0)
    yB = jax.nn.sigmoid(uo.astype(jnp.float32)).reshape(B, S, H_B, DV) * hB
    yB = yB * lax.rsqrt(jnp.mean(yB * yB, axis=-1, keepdims=True) + EPS)
    yB = yB.reshape(B, S, W_B) * g_mhead.astype(jnp.float32)
    merged = jnp.concatenate([yA * jax.nn.silu(za.astype(jnp.float32)),
                              yB * jax.nn.silu(zb.astype(jnp.float32))], axis=-1).astype(x.dtype)
    x = x + merged @ w_out
    return x, hA, bufA, C, n, m


def trunk(x, h_s, conv_s, C_s, n_s, m_s, g_norm, w_in, conv_w, conv_b, w_rgate, b_rgate,
          w_igate, b_igate, lru_lambda, b_mi, b_mf, g_mhead, w_out, g_final):
    hs, bufs, Cs, ns, ms = [], [], [], [], []
    for l in range(DEPTH):
        x, hA, bufA, C, n, m = hybrid_layer(
            x, h_s[l], conv_s[l], C_s[l], n_s[l], m_s[l], g_norm[l], w_in[l], conv_w[l], conv_b[l],
            w_rgate[l], b_rgate[l], w_igate[l], b_igate[l], lru_lambda[l], b_mi[l], b_mf[l],
            g_mhead[l], w_out[l])
        hs.append(hA.astype(h_s.dtype)); bufs.append(bufA.astype(conv_s.dtype))
        Cs.append(C.astype(C_s.dtype)); ns.append(n.astype(n_s.dtype)); ms.append(m.astype(m_s.dtype))
    y = rmsnorm(x, g_final)
    return y, jnp.stack(hs), jnp.stack(bufs), jnp.stack(Cs), jnp.stack(ns), jnp.stack(ms)


def setup_inputs(seed: int = 0) -> dict:
    key = jax.random.key(seed)
    ks = jax.random.split(key, 24)
    f32 = jnp.float32
    nrm = lambda k, shape, s: jax.random.normal(k, shape, f32) * s
    a_c = jax.random.uniform(ks[0], (DEPTH, W_A), f32, 0.9, 0.999)
    s = a_c ** (1.0 / LRU_C)
    lru_lambda = jnp.log(s) - jnp.log1p(-s)
    b_mf = jnp.linspace(3.0, 6.0, H_B, dtype=f32)[None, :] + nrm(ks[1], (DEPTH, H_B), 0.1)
    return {
        "x_prompt": nrm(ks[2], (BATCH, SEQ, D_MODEL), 1.0),
        "x_sample": nrm(ks[3], (DEC_BATCH, DEC_SEQ, D_MODEL), 1.0),
        "state_rglru_h": nrm(ks[4], (DEPTH, DEC_BATCH, W_A), 0.5),
        "state_rglru_conv": nrm(ks[5], (DEPTH, DEC_BATCH, CONV_W - 1, W_A), 1.0),
        "state_mlstm_C": nrm(ks[6], (DEPTH, DEC_BATCH, H_B, DK, DV), 0.1),
        "state_mlstm_n": nrm(ks[7], (DEPTH, DEC_BATCH, H_B, DK), 0.5),
        "state_mlstm_m": nrm(ks[8], (DEPTH, DEC_BATCH, H_B), 0.5),
        "g_norm": 1.0 + nrm(ks[9], (DEPTH, D_MODEL), 0.02),
        "w_in": nrm(ks[10], (DEPTH, D_MODEL, D_IN), D_MODEL ** -0.5),
        "conv_w": nrm(ks[11], (DEPTH, CONV_W, W_A), CONV_W ** -0.5),
        "conv_b": nrm(ks[12], (DEPTH, W_A), 0.05),
        "w_rgate": nrm(ks[13], (DEPTH, NB_A, BW_A, BW_A), BW_A ** -0.5),
        "b_rgate": nrm(ks[14], (DEPTH, W_A), 0.1),
        "w_igate": nrm(ks[15], (DEPTH, NB_A, BW_A, BW_A), BW_A ** -0.5),
        "b_igate": nrm(ks[16], (DEPTH, W_A), 0.1),
        "lru_lambda": lru_lambda,
        "b_mi": nrm(ks[17], (DEPTH, H_B), 0.1),
        "b_mf": b_mf,
        "g_mhead": 1.0 + nrm(ks[18], (DEPTH, W_B), 0.02),
        "w_out": nrm(ks[19], (DEPTH, W_A + W_B, D_MODEL), (W_A + W_B) ** -0.5),
        "g_final": 1.0 + nrm(ks[20], (D_MODEL,), 0.02),
    }


def reference(x_prompt, x_sample, state_rglru_h, state_rglru_conv, state_mlstm_C, state_mlstm_n,
              state_mlstm_m, g_norm, w_in, conv_w, conv_b, w_rgate, b_rgate, w_igate, b_igate,
              lru_lambda, b_mi, b_mf, g_mhead, w_out, g_final):
    Bp = x_prompt.shape[0]
    zh = jnp.zeros((DEPTH, Bp, W_A), state_rglru_h.dtype)
    zconv = jnp.zeros((DEPTH, Bp, CONV_W - 1, W_A), state_rglru_conv.dtype)
    zC = jnp.zeros((DEPTH, Bp, H_B, DK, DV), state_mlstm_C.dtype)
    zn = jnp.zeros((DEPTH, Bp, H_B, DK), state_mlstm_n.dtype)
    zm = jnp.zeros((DEPTH, Bp, H_B), state_mlstm_m.dtype)
    y_prompt, p_h, p_conv, p_C, p_n, p_m = trunk(
        x_prompt, zh, zconv, zC, zn, zm, g_norm, w_in, conv_w, conv_b, w_rgate, b_rgate,
        w_igate, b_igate, lru_lambda, b_mi, b_mf, g_mhead, w_out, g_final)
    y_sample, s_h, s_conv, s_C, s_n, s_m = trunk(
        x_sample, state_rglru_h, state_rglru_conv, state_mlstm_C, state_mlstm_n, state_mlstm_m,
        g_norm, w_in, conv_w, conv_b, w_rgate, b_rgate, w_igate, b_igate, lru_lambda, b_mi, b_mf,
        g_mhead, w_out, g_final)
    return (y_prompt, y_sample, p_h, p_conv, p_C, p_n, p_m, s_h, s_conv, s_C, s_n, s_m)
```

```python
import contextlib
import numpy as np
import concourse.bass as bass
import concourse.mybir as mybir
from concourse.bass_utils import run_bass_kernel_spmd

F32 = mybir.dt.float32
F32R = mybir.dt.float32r
BF16 = mybir.dt.bfloat16
AF = mybir.ActivationFunctionType
ALU = mybir.AluOpType

NCORES = 8
WHATIF = {}
USE_SCRATCH = True
D = 1024
KC = 8
SEQ = 2048
T = 512
NT = SEQ // T
DIN = 7176
DEPTH = 2
NH = 4
DK = 256
EPS = 1e-6
DB = 16

C_ID, C_ONE, C_MASK, C_SEL, C_ONES4, NCONST = 0, 128, 256, 384, 896, 1408
R_GN, R_CW, R_CB, R_BR, R_BI, R_LAM, R_GMH = 0, 1, 5, 6, 7, 8, 9
R_GF = 20
NR = 21


class _FakeIns:
    def then_inc(self, *a, **k):
        return self


class _FakeEng:
    def __init__(self):
        self.rec = None

    def __getattr__(self, name):
        def f(*a, **k):
            self.rec = (name, a, k)
            return _FakeIns()
        return f


def _free(ap):
    n = 1
    for d in ap.shape[1:]:
        n *= int(d)
    return n


def _cost(eng, fn):
    fe = _FakeEng()
    fn(fe)
    name, a, k = fe.rec
    if name == 'matmul':
        rhs = a[2] if len(a) > 2 else k['rhs']
        lhs = a[1] if len(a) > 1 else k['lhsT']
        n = _free(rhs)
        m = _free(lhs)
        dt = rhs.dtype
        passes = 4 if (dt == F32 or (dt == F32R and n < 256)) else 1
        return 0.015 + passes * max(n, m) / 2000.0
    if name == 'transpose':
        return 0.12
    if name == 'dma_start':
        out = k['out']
        src = k['in_']
        esz = max(2 if out.dtype == BF16 else 4, 2 if src.dtype == BF16 else 4)
        nbytes = _free(out) * int(out.shape[0]) * esz
        return 2.0 + nbytes / 330e3
    out = k.get('out', a[0] if a else None)
    n = _free(out) if out is not None else 64
    if eng == 'act':
        return 0.2 + n / 1400.0
    if name == 'tensor_tensor_scan':
        return 0.1 + 2 * n / 960.0
    if eng == 'pool':
        return 0.3 + n / 500.0
    return 0.08 + n / 1000.0


class Sched:
    ENGS = ('pe', 'act', 'dve', 'pool', 'sp')
    WINDOW = {'pe': 100, 'act': 160, 'dve': 160, 'pool': 1, 'sp': 1}
    PE_GROUPS = False
    PE_DECODE_INORDER = False
    CRIT = False
    DMA_OCC = 1.0
    SLACK = 0.0

    def __init__(self, nc, es, reorder=True):
        self.nc, self.es, self.reorder = nc, es, reorder
        self.ins = []
        self.lastw, self.readers, self.sems = {}, {}, {}

    def _add(self, eng, fn, reads, writes, sem, cost):
        i = len(self.ins)
        deps = set()
        for k in reads:
            if k in self.lastw:
                deps.add(self.lastw[k])
        for k in writes:
            if k in self.lastw:
                deps.add(self.lastw[k])
            deps.update(self.readers.get(k, ()))
        self.ins.append(dict(eng=eng, fn=fn, deps=deps, sem=sem, cost=cost, wkeys=tuple(writes)))
        for k in reads:
            self.readers.setdefault(k, set()).add(i)
        for k in writes:
            self.lastw[k] = i
            self.readers[k] = set()

    def op(self, eng, fn, reads=(), writes=()):
        self._add(eng, fn, reads, writes, None, _cost(eng, fn))

    def dma(self, eng, out, in_, reads=(), writes=(), sem=None, **kw):
        fn = (lambda e: e.dma_start(out=out, in_=in_, **kw))
        self._add(eng, fn, reads, writes, 'D_' + sem, _cost(eng, fn))

    def _schedule(self):
        ins = self.ins
        n = len(ins)
        if not self.reorder:
            return list(range(n))
        succ = [[] for _ in range(n)]
        for i, I in enumerate(ins):
            for d in I['deps']:
                succ[d].append(i)
        unit_of = [None] * n
        units = []
        per_eng = {e: [] for e in self.ENGS}
        last_pe = None
        for i, I in enumerate(ins):
            e = I['eng']
            if e == 'pe' and self.PE_GROUPS and last_pe is not None and units[last_pe]['wkeys'] == I['wkeys'] and len(units[last_pe]['members']) < 40:
                units[last_pe]['members'].append(i)
                unit_of[i] = last_pe
                continue
            u = len(units)
            units.append(dict(eng=e, members=[i], wkeys=I['wkeys'], pos=len(per_eng[e])))
            per_eng[e].append(u)
            unit_of[i] = u
            if e == 'pe':
                last_pe = u
        mark = getattr(self, 'mark', n) if self.PE_DECODE_INORDER else n
        ext = [0] * len(units)
        indeg = [0] * n
        for i, I in enumerate(ins):
            indeg[i] = len(I['deps'])
            for d in I['deps']:
                if unit_of[d] != unit_of[i]:
                    ext[unit_of[i]] += 1
        tail = [0.0] * n
        if self.CRIT:
            for i in range(n - 1, -1, -1):
                t = 0.0
                for s_ in succ[i]:
                    if tail[s_] > t:
                        t = tail[s_]
                tail[i] = t + ins[i]['cost'] + 0.1
        head = {e: 0 for e in self.ENGS}
        udone = [False] * len(units)
        finish = [0.0] * n
        rtime = [0.0] * n
        ready = {e: [] for e in self.ENGS}
        for u, U in enumerate(units):
            if ext[u] == 0:
                ready[U['eng']].append(u)
        free = {e: 0.0 for e in self.ENGS}
        order = []
        nsched = [0]
        open_unit = [None]

        def sched_ins(i, e, u):
            est = max(free[e], rtime[i])
            finish[i] = est + ins[i]['cost']
            free[e] = finish[i] if ins[i]['sem'] is None else est + 0.1 + (ins[i]['cost'] - 2.0) * self.DMA_OCC
            order.append(i)
            nsched[0] += 1
            for s_ in succ[i]:
                indeg[s_] -= 1
                su = unit_of[s_]
                lat = 0.05 if ins[s_]['eng'] == e else 0.2
                t = finish[i] + lat
                if t > rtime[s_]:
                    rtime[s_] = t
                if su != u:
                    ext[su] -= 1
                    if ext[su] == 0 and not udone[su]:
                        ready[units[su]['eng']].append(su)

        while nsched[0] < n:
            best = None
            for e in self.ENGS:
                pl = per_eng[e]
                h = head[e]
                while h < len(pl) and udone[pl[h]]:
                    h += 1
                head[e] = h
                if h >= len(pl):
                    continue
                if e == 'pe' and open_unit[0] is not None:
                    u, k = open_unit[0]
                    m = units[u]['members'][k]
                    if indeg[m] == 0:
                        cand = ((max(free[e], rtime[m]), -1), ('open', u, k))
                        if best is None or cand[0] < best[0]:
                            best = (cand[0], cand[1], e)
                    continue
                lim = h + self.WINDOW[e]
                if e == 'pe' and units[pl[h]]['members'][0] >= mark:
                    lim = h + 1
                cand = None
                for u in ready[e]:
                    U = units[u]
                    if U['pos'] >= lim or (e == 'pe' and U['members'][0] >= mark and U['pos'] != h):
                        continue
                    st, acc = 0.0, 0.0
                    for m in U['members']:
                        if rtime[m] - acc > st:
                            st = rtime[m] - acc
                        acc += ins[m]['cost']
                    est_ = max(free[e], st)
                    if self.CRIT:
                        key = (est_ if est_ > free[e] + self.SLACK else free[e], -tail[U['members'][0]], U['pos'])
                    else:
                        key = (est_, U['pos'])
                    if cand is None or key < cand[0]:
                        cand = (key, ('full', u, 0))
                if e == 'pe':
                    hu = pl[h]
                    if ext[hu] > 0:
                        m0 = units[hu]['members'][0]
                        if indeg[m0] == 0:
                            key = (max(free[e], rtime[m0]), units[hu]['pos'])
                            if cand is None or key < cand[0]:
                                cand = (key, ('head', hu, 0))
                if cand is not None and (best is None or cand[0] < best[0]):
                    best = (cand[0], cand[1], e)
            assert best is not None, "scheduler stuck"
            _, (kind, u, k), e = best
            if kind == 'full':
                ready[e].remove(u)
                udone[u] = True
                for i in units[u]['members']:
                    sched_ins(i, e, u)
            else:
                mem = units[u]['members']
                udone[u] = True
                if u in ready[e]:
                    ready[e].remove(u)
                sched_ins(mem[k], e, u)
                open_unit[0] = (u, k + 1) if k + 1 < len(mem) else None
        self.sim_time = max(finish)
        self.finish = finish
        return order

    def emit(self):
        nc = self.nc
        ins = self.ins
        order = self._schedule()
        stream = {e: [] for e in self.ENGS}
        cnt = {e: 0 for e in self.ENGS}
        dcnt = {}
        tok = [None] * len(ins)
        for i in order:
            I = ins[i]
            if I['sem'] is None:
                cnt[I['eng']] += 1
                tok[i] = ('E_' + I['eng'], cnt[I['eng']], 1)
            else:
                dcnt[I['sem']] = dcnt.get(I['sem'], 0) + 16
                tok[i] = (I['sem'], dcnt[I['sem']], 16)
        waited = {e: {} for e in self.ENGS}
        names = set()
        for i in order:
            I = ins[i]
            e = I['eng']
            deps = {}
            for d in I['deps']:
                s_, v, _ = tok[d]
                if deps.get(s_, 0) < v:
                    deps[s_] = v
            waits = []
            for s_, v in deps.items():
                if e == 'pe' and s_ == 'E_pe':
                    continue
                if waited[e].get(s_, 0) >= v:
                    continue
                waited[e][s_] = v
                waits.append((s_, v))
                names.add(s_)
            names.add(tok[i][0])
            stream[e].append((waits, I['fn'], (tok[i][0], tok[i][2])))
        for nme in sorted(names):
            self.sems[nme] = self.es.enter_context(nc.semaphore(nme))
        fin = list(dcnt.items())
        sems = self.sems

        def mk(engname, final):
            def body(e):
                for waits, fn, inc in stream[engname]:
                    for s_, v in waits:
                        e.wait_ge(sems[s_], v)
                    fn(e).then_inc(sems[inc[0]], inc[1])
                if final:
                    for nme, c in fin:
                        e.wait_ge(sems[nme], c)
            return body
        with nc.Block() as block:
            block.tensor(mk('pe', False))
            block.scalar(mk('act', False))
            block.vector(mk('dve', False))
            block.gpsimd(mk('pool', False))
            block.sync(mk('sp', True))


def build_program(do_decode=True):
    nc = bass.Bass("TRN2", target_bir_lowering=False)
    es = contextlib.ExitStack()
    S = Sched(nc, es)

    def din(name, shape):
        return nc.dram_tensor(name, list(shape), F32, kind="ExternalInput").ap()

    def dout(name, shape):
        return nc.dram_tensor(name, list(shape), F32, kind="ExternalOutput").ap()

    x_p = din("x_p", [SEQ, D])
    consts = din("consts", [128, NCONST])
    g_norm = din("g_norm", [DEPTH, D]); w_in = din("w_in", [DEPTH, D, DIN])
    conv_w = din("conv_w", [DEPTH, 4, D]); conv_b = din("conv_b", [DEPTH, D])
    w_rg = din("w_rgate", [DEPTH, 16, 64, 64]); b_rg = din("b_rgate", [DEPTH, D])
    w_ig = din("w_igate", [DEPTH, 16, 64, 64]); b_ig = din("b_igate", [DEPTH, D])
    lam = din("lru_lambda", [DEPTH, D]); b_mi = din("b_mi", [DEPTH, NH]); b_mf = din("b_mf", [DEPTH, NH])
    g_mh = din("g_mhead", [DEPTH, D]); w_out = din("w_out", [DEPTH, 2 * D, D]); g_fin = din("g_final", [D])
    y_p = dout("y_p", [SEQ, D])
    p_h = dout("p_h", [DEPTH, D]); p_conv = dout("p_conv", [DEPTH, 3, D])
    p_C = dout("p_C", [DEPTH, NH, DK, DK]); p_n = dout("p_n", [DEPTH, NH, DK]); p_m = dout("p_m", [DEPTH, NH])

    def sb(name, shape, dt=F32):
        return es.enter_context(nc.sbuf_tensor(name, list(shape), dt))

    S_ONES4 = 384
    cst = sb("cst", [128, 896])
    selr = sb("selr", [4, 512], F32R)
    ones_w = sb("ones_w", [128, 256], F32R)
    ones_r = ones_w[:, 0:128]
    prm = sb("prm", [128, KC, 32])
    nsp = sb("nsp", [128, DEPTH, KC])
    gcol = sb("gcol", [4, 8])
    wbd = sb("wbd", [128, 2 * DEPTH * KC, 128], BF16)
    wg = sb("wg", [128, DEPTH, KC, 8], BF16)
    xT = sb("xT", [128, KC, T])
    hn = sb("hn", [128, KC, T], BF16)
    mg = sb("mg", [128, 16, T], BF16)
    NS = 4
    wsl = [sb(f"wsl{i}", [128, KC * 512], BF16) for i in range(NS)]
    xin = sb("xin", [128, D])
    yout = sb("yout", [128, D])
    sq = [sb(f"sq{i}", [128, T], F32R) for i in range(2)]
    rsb = sb("rsb", [128, T])
    grow = [sb(f"grow{i}", [4, T], F32 if i < 3 else F32R) for i in range(6)]
    acol = sb("acol", [128, 16])
    NWK = 16
    work = sb("work", [128, NWK, T])
    workr = sb("workr", [128, 15, T], F32R)
    ext = [sb(f"ext{i}", [128, T + 4]) for i in range(2)]
    Cst = [sb(f"Cst{l}", [128, NH, 2, DK], F32R) for l in range(DEPTH)]
    nrep = [sb(f"nrep{l}", [128, NH, 2, 128], F32R) for l in range(DEPTH)]
    hst = sb("hst", [128, DEPTH, KC])
    convst = sb("convst", [128, DEPTH * KC, 3])
    mst = sb("mst", [4, DEPTH])
    dec = sb("dec", [128, 4])

    ps = [es.enter_context(nc.psum_tensor(f"ps{i}", [128, 512], F32)) for i in range(8)]
    psctr = [0]

    reserved = set()

    def bank():
        while psctr[0] % 8 in reserved:
            psctr[0] += 1
        i = psctr[0] % 8
        psctr[0] += 1
        if WHATIF.get('psum'):
            return ps[i], f"psv{psctr[0]}"
        return ps[i], f"ps{i}"

    ident = cst[:, C_ID:C_ID + 128]
    mask01 = cst[:, C_MASK:C_MASK + 128]
    ones4 = cst[0:4, S_ONES4:S_ONES4 + T]

    def wk(i):
        return work[:, i, :]

    def wkr(i):
        return workr[:, i, :]

    def wkrf(i):
        return workr[:].bitcast(F32)[:, i, :]

    S.dma('sp', cst[:, 0:384], consts[:, 0:384], writes=['cst'], sem='cst')
    S.dma('sp', cst[:, 384:896], consts[:, C_ONES4:C_ONES4 + 512], writes=['cst'], sem='cst')
    S.dma('sp', work[0:4, 2, :], consts[0:4, C_SEL:C_SEL + 512], writes=['wk2'], sem='selst')
    S.op('dve', lambda e: e.tensor_copy(out=selr[:], in_=work[0:4, 2, :]), reads=['wk2'], writes=['selr'])
    S.op('dve', lambda e: e.tensor_copy(out=ones_w[:, 0:128], in_=cst[:, C_ONE:C_ONE + 128]), reads=['cst'], writes=['ones_r'])
    S.op('dve', lambda e: e.tensor_copy(out=ones_w[:, 128:256], in_=cst[:, C_ONE:C_ONE + 128]), reads=['cst'], writes=['ones_r'])
    prow = work[0:32, 0:2, :]

    def prow_row(r):
        return work[r:r + 1, 0:2, :]
    S.op('dve', lambda e: e.memset(work[0:32, 0:2, :], 0.0), writes=['wk0', 'wk1'])
    rows = []
    for l in range(DEPTH):
        rows.append((10 * l + R_GN, g_norm[l:l + 1, :]))
        for j in range(4):
            rows.append((10 * l + R_CW + j, conv_w[l, j:j + 1, :]))
        rows.append((10 * l + R_CB, conv_b[l:l + 1, :]))
        rows.append((10 * l + R_BR, b_rg[l:l + 1, :]))
        rows.append((10 * l + R_BI, b_ig[l:l + 1, :]))
        rows.append((10 * l + R_LAM, lam[l:l + 1, :]))
        rows.append((10 * l + R_GMH, g_mh[l:l + 1, :]))
    rows.append((R_GF, g_fin.rearrange("(o d) -> o d", o=1)))
    for r, src in rows:
        S.dma('sp', work[r:r + 1, 0:2, :], src.rearrange("o (a b) -> o a b", a=2), reads=['wk0', 'wk1'], writes=[f'prow{r}'], sem='prow')
    for kc in range(KC):
        pt, pk = bank()
        a, b = divmod(kc * 128, 512)
        S.op('pe', (lambda e, pt=pt, a=a, b=b: e.transpose(out=pt[:, 0:32], in_=work[0:32, a, b:b + 128], identity=cst[0:32, C_ID:C_ID + 32])),
             reads=[f'prow{r}' for r, _ in rows] + ['cst', 'wk0', 'wk1'], writes=[pk])
        S.op('dve', (lambda e, pt=pt, kc=kc: e.tensor_copy(out=prm[:, kc, :], in_=pt[:, 0:32])), reads=[pk], writes=['prm'])
    for l in range(DEPTH):
        S.op('act', (lambda e, l=l: e.activation(out=nsp[:, l, :], in_=prm[:, :, 10 * l + R_LAM], func=AF.Exp, scale=-1.0)), reads=['prm'], writes=['nsp'])
    S.op('act', lambda e: e.activation(out=nsp[:], in_=nsp[:], func=AF.Ln, bias=1.0), reads=['nsp'], writes=['nsp'])
    S.op('dve', lambda e: e.tensor_scalar(out=nsp[:], in0=nsp[:], scalar1=-8.0, scalar2=None, op0=ALU.mult), reads=['nsp'], writes=['nsp'])
    S.dma('sp', gcol[:, 0:2], b_mi.rearrange("l h -> h l"), writes=['gcol'], sem='gcol', allow_slow_non_contiguous=True)
    S.dma('sp', gcol[:, 4:6], b_mf.rearrange("l h -> h l"), writes=['gcol'], sem='gcol', allow_slow_non_contiguous=True)
    S.op('dve', lambda e: e.tensor_scalar(out=gcol[:, 2:4], in0=gcol[:, 4:6], scalar1=-1.0, scalar2=None, op0=ALU.mult), reads=['gcol'], writes=['gcol'])
    S.op('pool', lambda e: e.memset(wbd[:], 0.0), writes=['wbd0'])
    for gi_, wsrc in enumerate((w_rg, w_ig)):
        for l in range(DEPTH):
            for half in range(2):
                src = wsrc[l].rearrange("(c two) i o -> two i c o", two=2)[half]
                base = (gi_ * DEPTH + l) * KC
                S.dma('pool', wbd[half * 64:(half + 1) * 64, base:base + KC, half * 64:(half + 1) * 64], src,
                      reads=['wbd0'], writes=[f'wbd_{gi_}{l}{half}'], sem='wbd')
    for l in range(DEPTH):
        S.dma('pool', wg[:, l, :, :], w_in[l].rearrange("(kc p) n -> p kc n", p=128)[:, :, 7168:7176], writes=['wg'], sem='wg')
    for l in range(DEPTH):
        for h in range(NH):
            S.op('dve', (lambda e, l=l, h=h: e.tensor_scalar(out=Cst[l][:, h, :, :], in0=cst[:, 0:512].rearrange("p (a b) -> p a b", a=2),
                                                           scalar1=0.0, scalar2=None, op0=ALU.mult)), reads=['cst'], writes=[f'C{l}'])
            S.op('dve', (lambda e, l=l, h=h: e.tensor_scalar(out=nrep[l][:, h, :, :], in0=cst[:, 0:256].rearrange("p (a b) -> p a b", a=2),
                                                           scalar1=0.0, scalar2=None, op0=ALU.mult)), reads=['cst'], writes=[f'n{l}'])
    S.op('dve', lambda e: e.memset(hst[:], 0.0), writes=['hst0', 'hst1'])
    S.op('dve', lambda e: e.memset(convst[:], 0.0), writes=['cv0', 'cv1'])
    S.op('dve', lambda e: e.memset(mst[:], 0.0), writes=['mst0', 'mst1'])

    wctr = [0]

    wscr = [nc.dram_tensor(f"wscr{l}", [28, 128, 4096], BF16, kind="Internal").ap() for l in range(DEPTH)]
    seen = {}

    def load_w(src, view):
        src_ap, gid = src
        i = wctr[0] % NS
        wctr[0] += 1
        a, b = view
        flat = wsl[i][:, 0:a * b]
        dst = flat.rearrange("p (a b) -> p a b", a=a)
        l = int(gid.split('_')[0])
        if (not USE_SCRATCH) or gid not in seen:
            S.dma('pool', dst, src_ap, writes=[f'W{i}'], sem=f'W{i}')
            if USE_SCRATCH:
                g = len([k for k in seen if k.startswith(f'{l}_')])
                seen[gid] = g
                S.dma('sp', wscr[l][g, :, 0:a * b], flat, reads=[f'W{i}'], writes=[f'scr{gid}'], sem=f'scrst{i}')
        else:
            g = seen[gid]
            S.dma('pool', flat, wscr[l][g, :, 0:a * b], reads=[f'scr{gid}'], writes=[f'W{i}'], sem=f'W{i}')
        return dst, f'W{i}'

    def win_cols(l, c0, n):
        return w_in[l].rearrange("(kc p) n -> p kc n", p=128)[:, :, c0:c0 + n], f'{l}_in_{c0}'

    def wout_cols(l, c0, n):
        return w_out[l].rearrange("(kc p) n -> p kc n", p=128)[:, :, c0:c0 + n], f'{l}_out_{c0}'

    def act_sig(opf, dst, src, reads, dkey, nbias=None):
        if nbias is None:
            opf('act', (lambda e: e.activation(out=dst, in_=src, func=AF.Exp, scale=-1.0)), reads=list(reads), writes=[dkey])
        else:
            opf('act', (lambda e: e.activation(out=dst, in_=src, func=AF.Exp, scale=-1.0, bias=nbias)), reads=list(reads) + ['nprm'], writes=[dkey])
        opf('act', (lambda e: e.activation(out=dst, in_=dst, func=AF.Ln, bias=1.0)), reads=[dkey], writes=[dkey])
        opf('act', (lambda e: e.activation(out=dst, in_=dst, func=AF.Exp, scale=-1.0)), reads=[dkey], writes=[dkey])

    def act_rpow(opf, dst, src, reads, dkey, p, scale=1.0, bias=None):
        if bias is None:
            opf('act', (lambda e: e.activation(out=dst, in_=src, func=AF.Ln, scale=scale)), reads=list(reads), writes=[dkey])
        else:
            opf('act', (lambda e: e.activation(out=dst, in_=src, func=AF.Ln, scale=scale, bias=bias)), reads=list(reads), writes=[dkey])
        opf('act', (lambda e: e.activation(out=dst, in_=dst, func=AF.Exp, scale=p)), reads=[dkey], writes=[dkey])

    def rmsnorm(src_key_fn, gcol_fn, dst_fn, dst_key_fn, ntok, src_fn=None, extra=()):
        if src_fn is None:
            src_fn = lambda kc: xT[:, kc, 0:ntok]
        extra = list(extra)
        pt, pk = bank()
        for kc in range(KC):
            s_ = sq[kc % 2]
            S.op('act', (lambda e, s_=s_, kc=kc: e.activation(out=s_[:, 0:ntok], in_=src_fn(kc), func=AF.Square)),
                 reads=[src_key_fn(kc)] + extra, writes=[f'sq{kc % 2}'])
            S.op('pe', (lambda e, s_=s_, kc=kc, pt=pt: e.matmul(pt[:, 0:ntok], ones_r, s_[:, 0:ntok], start=(kc == 0), stop=(kc == KC - 1))),
                 reads=[f'sq{kc % 2}', 'ones_r'] + extra, writes=[pk])
        act_rpow(S.op, rsb[:, 0:ntok], pt[:, 0:ntok], [pk, 'epsc'] + extra, 'rsb', -0.5, scale=1.0 / D, bias=epsc[:, 0:1])
        for kc in range(KC):
            S.op('dve', (lambda e, kc=kc: e.scalar_tensor_tensor(out=dst_fn(kc), in0=src_fn(kc), scalar=gcol_fn(kc),
                                                               in1=rsb[:, 0:ntok], op0=ALU.mult, op1=ALU.mult)),
                 reads=[src_key_fn(kc), 'rsb', 'prm'], writes=[dst_key_fn(kc)])

    epsc = sb("epsc", [128, 1])
    S.op('dve', lambda e: e.memset(epsc[:], EPS), writes=['epsc'])
    onep = sb("onep", [128, 1])
    S.op('dve', lambda e: e.memset(onep[:], 1.0), writes=['onep'])
    nprm = sb("nprm", [128, KC, 32])
    S.op('dve', lambda e: e.tensor_scalar(out=nprm[:], in0=prm[:], scalar1=-1.0, scalar2=None, op0=ALU.mult), reads=['prm'], writes=['nprm'])

    def proj_fm(wv, wkey, j, ntok, ncols=128, hn_fn=None, hnk_fn=None):
        if hn_fn is None:
            hn_fn = lambda kc: hn[:, kc, 0:ntok]
            hnk_fn = lambda kc: f'hn{kc}'
        pt, pk = bank()
        for kc in range(KC):
            S.op('pe', (lambda e, kc=kc, pt=pt: e.matmul(pt[0:ncols, 0:ntok], wv[:, kc, j * 128:j * 128 + ncols], hn_fn(kc),
                                                      start=(kc == 0), stop=(kc == KC - 1))),
                 reads=[wkey, hnk_fn(kc)], writes=[pk])
        return pt, pk

    def rglru_chunk(l, c, j, wa, wak, wz, wzk):
        st = c % 2
        b0 = 7 * st
        xc, xcb, r_, i_, m_, h_, sz = (wk(b0 + 0), work[:, b0 + 1, :].bitcast(BF16)[:, 0:T], wk(b0 + 2), wk(b0 + 3), wk(b0 + 4),
                                       wk(b0 + 5), wk(b0 + 6))
        a_, u_ = r_, i_
        KM = {0: 0, 1: 1, 2: 2, 3: 3, 4: 2, 5: 4, 6: 3, 7: 5, 8: 6}
        K = lambda n: (WHATIF.get('rg', 'wk') + f'{b0 + KM[n]}')
        ex = ext[st]
        exk = f'ext{st}'
        cvk = f'cv{l}'
        pX, pXk = proj_fm(wa, wak, j, T)
        pZ, pZk = proj_fm(wz, wzk, j, T)
        S.op('dve', (lambda e, ex=ex, c=c: e.tensor_copy(out=ex[:, 0:3], in_=convst[:, l * KC + c, :])), reads=[cvk], writes=[exk])
        S.op('act', (lambda e, ex=ex, pX=pX: e.activation(out=ex[:, 3:T + 3], in_=pX[:, :], func=AF.Copy)), reads=[pXk], writes=[exk])
        act_sig(S.op, sz, pZ[:, :], [pZk], K(8))
        S.op('dve', (lambda e, sz=sz, pZ=pZ: e.tensor_tensor(out=sz, in0=sz, in1=pZ[:, :], op=ALU.mult)), reads=[pZk, K(8)], writes=[K(8)])
        cw = lambda jj, c=c: prm[:, c, 10 * l + R_CW + jj:10 * l + R_CW + jj + 1]
        S.op('dve', (lambda e, ex=ex, xc=xc, cw=cw, c=c: e.tensor_scalar(out=xc, in0=ex[:, 0:T], scalar1=cw(0),
                                                                        scalar2=prm[:, c, 10 * l + R_CB:10 * l + R_CB + 1], op0=ALU.mult, op1=ALU.add)),
             reads=[exk, 'prm'], writes=[K(0)])
        for jj in range(1, 4):
            S.op('dve', (lambda e, ex=ex, xc=xc, cw=cw, jj=jj: e.scalar_tensor_tensor(out=xc, in0=ex[:, jj:jj + T], scalar=cw(jj), in1=xc,
                                                                                     op0=ALU.mult, op1=ALU.add)),
                 reads=[exk, 'prm', K(0)], writes=[K(0)])
        S.op('dve', (lambda e, ex=ex, c=c: e.tensor_copy(out=convst[:, l * KC + c, :], in_=ex[:, T:T + 3])), reads=[exk], writes=[cvk])
        S.op(WHATIF.get('xcb_eng', 'dve'), ((lambda e, xc=xc, xcb=xcb: e.activation(out=xcb, in_=xc, func=AF.Copy)) if WHATIF.get('xcb_eng', 'dve') == 'act' else (lambda e, xc=xc, xcb=xcb: e.tensor_copy(out=xcb, in_=xc))), reads=[K(0)], writes=[K(1)])
        pR, pRk = bank()
        S.op('pe', (lambda e, pR=pR, xcb=xcb, c=c: e.matmul(pR[:, :], wbd[:, (0 * DEPTH + l) * KC + c, :], xcb, start=True, stop=True)),
             reads=['wbd0'] + [f'wbd_{a_}{b_}{c_}' for a_ in range(2) for b_ in range(2) for c_ in range(2)] + [K(1)], writes=[pRk])
        pI, pIk = bank()
        S.op('pe', (lambda e, pI=pI, xcb=xcb, c=c: e.matmul(pI[:, :], wbd[:, (1 * DEPTH + l) * KC + c, :], xcb, start=True, stop=True)),
             reads=['wbd0'] + [f'wbd_{a_}{b_}{c_}' for a_ in range(2) for b_ in range(2) for c_ in range(2)] + [K(1)], writes=[pIk])
        act_sig(S.op, r_, pR[:, :], [pRk], K(2), nbias=nprm[:, c, 10 * l + R_BR:10 * l + R_BR + 1])
        act_sig(S.op, i_, pI[:, :], [pIk], K(3), nbias=nprm[:, c, 10 * l + R_BI:10 * l + R_BI + 1])
        S.op('act', (lambda e, a_=a_, r_=r_, c=c: e.activation(out=a_, in_=r_, func=AF.Exp, scale=nsp[:, l, c:c + 1])), reads=[K(2), 'nsp'], writes=[K(4)])
        S.op(WHATIF.get('sq_eng', 'dve'), ((lambda e, a_=a_, m_=m_: e.activation(out=m_, in_=a_, func=AF.Square)) if WHATIF.get('sq_eng', 'dve') == 'act' else (lambda e, a_=a_, m_=m_: e.tensor_tensor(out=m_, in0=a_, in1=a_, op=ALU.mult))), reads=[K(4)], writes=[K(5)])
        act_rpow(S.op, m_, m_, [K(5), 'onep'], K(5), 0.5, scale=-1.0, bias=onep[:, 0:1])
        S.op('dve', (lambda e, u_=u_, i_=i_, xc=xc: e.tensor_tensor(out=u_, in0=i_, in1=xc, op=ALU.mult)), reads=[K(3), K(0)], writes=[K(6)])
        S.op('dve', (lambda e, u_=u_, m_=m_: e.tensor_tensor(out=u_, in0=u_, in1=m_, op=ALU.mult)), reads=[K(6), K(5)], writes=[K(6)])
        S.op('dve', (lambda e, h_=h_, a_=a_, u_=u_, c=c: e.tensor_tensor_scan(out=h_, data0=a_, data1=u_, initial=hst[:, l, c:c + 1],
                                                                            op0=ALU.mult, op1=ALU.add)),
             reads=[K(4), K(6), f'hst{l}'], writes=[K(7)])
        S.op('dve', (lambda e, h_=h_, c=c: e.tensor_copy(out=hst[:, l, c:c + 1], in_=h_[:, T - 1:T])), reads=[K(7)], writes=[f'hst{l}'])
        S.op('dve', (lambda e, h_=h_, sz=sz, c=c: e.tensor_tensor(out=mg[:, c, :], in0=h_, in1=sz, op=ALU.mult)), reads=[K(7), K(8)], writes=[f'mg{c}'])


    def mlstm_head(l, h):
        ig, l1, bcs, Mx, gi, en = [g_[:] for g_ in grow]
        LM = {0: 'R0', 1: 'R1', 2: 'R2', 3: 'R3', 4: 'R4', 5: 'R5', 6: 'F0', 7: 'F1', 8: 'F2', 9: 'F3', 10: 'F4', 11: 'F5', 12: 'F6',
              13: 'F7', 14: 'F8', 15: 'F9', 16: 'R9', 17: 'R10', 18: 'R11', 19: 'R12', 20: 'F10', 21: 'F11', 22: 'F12', 23: 'R13', 24: 'R14',
              25: 'F13', 26: 'R6', 27: 'R7', 28: 'R8'}
        W = lambda n: ('wk' if LM[n][0] == 'F' else 'wr') + LM[n][1:] + (f'_h{h % 2}' if (WHATIF.get('mlproj') and n < 10) or WHATIF.get('mlall') else '')

        def B(n, f32=False):
            i = int(LM[n][1:])
            if LM[n][0] == 'F':
                return work[:, i, :]
            return wkrf(i) if f32 else workr[:, i, :]
        qT = lambda dkc: B(0 + dkc)
        kT = lambda dkc: B(2 + dkc)
        vtok = lambda j: B(4 + j // 2)[:, (j % 2) * 256:(j % 2) * 256 + 256]
        so = lambda dvc: B(6 + dvc)
        szb = lambda dvc: B(8 + dvc)
        Mb, gib, enb = B(10), B(11), B(12)
        wTb = [(13, 0), (14, 0), (15, 0), (15, 256)]
        PTb = [(26, 0), (27, 0), (28, 0), (28, 256)]
        wT = lambda j: B(wTb[j][0])[:, wTb[j][1]:wTb[j][1] + (T - 128 * j)]
        PT = lambda j: B(PTb[j][0])[:, PTb[j][1]:PTb[j][1] + (T - 128 * j)]
        qg = lambda dkc: B(16 + dkc)
        kd = lambda j: B(18 + j // 2)[:, (j % 2) * 256:(j % 2) * 256 + 256]
        rden = B(20)
        yB = lambda dvc: B(21 + dvc)
        ysq = lambda dvc: B(23 + dvc)
        rsh = B(25)

        wq, wqk = load_w(win_cols(l, 2048 + 256 * h, 256), (KC, 256))
        wkk_, wkk = load_w(win_cols(l, 3072 + 256 * h, 256), (KC, 256))
        for dkc in range(2):
            pt, pk = proj_fm(wq, wqk, dkc, T)
            S.op('dve', (lambda e, pt=pt, dkc=dkc: e.tensor_copy(out=qT(dkc), in_=pt[:, :])), reads=[pk], writes=[W(0 + dkc)])
        for dkc in range(2):
            pt, pk = proj_fm(wkk_, wkk, dkc, T)
            S.op('act', (lambda e, pt=pt, dkc=dkc: e.activation(out=kT(dkc), in_=pt[:, :], func=AF.Copy, scale=DK ** -0.5)), reads=[pk], writes=[W(2 + dkc)])
        wv_, wvk = load_w(win_cols(l, 4096 + 256 * h, 256), (KC, 256))
        for j in range(4):
            pt, pk = bank()
            for kc in range(KC):
                S.op('pe', (lambda e, kc=kc, pt=pt, j=j: e.matmul(pt[:, 0:256], hn[:, kc, j * 128:(j + 1) * 128], wv_[:, kc, :],
                                                               start=(kc == 0), stop=(kc == KC - 1))),
                     reads=[wvk, f'hn{kc}'], writes=[pk])
            S.op('act', (lambda e, pt=pt, j=j: e.activation(out=vtok(j), in_=pt[:, 0:256], func=AF.Copy)), reads=[pk], writes=[W(4 + j // 2)])
        wo_, wok = load_w(win_cols(l, 5120 + 256 * h, 256), (KC, 256))
        wzb_, wzbk = load_w(win_cols(l, 6144 + 256 * h, 256), (KC, 256))
        for dvc in range(2):
            pt, pk = proj_fm(wo_, wok, dvc, T)
            act_sig(S.op, so(dvc), pt[:, :], [pk], W(6 + dvc))
        for dvc in range(2):
            pt, pk = proj_fm(wzb_, wzbk, dvc, T)
            act_sig(S.op, szb(dvc), pt[:, :], [pk], W(8 + dvc))
            S.op('dve', (lambda e, pt=pt, dvc=dvc: e.tensor_tensor(out=szb(dvc), in0=szb(dvc), in1=pt[:, :], op=ALU.mult)), reads=[pk, W(8 + dvc)], writes=[W(8 + dvc)])
        sel = selr[:, 128 * h:128 * h + 128]
        for src, sk, dst, dkey in ((Mx, 'g3', Mb, W(10)), (gi, 'g4', gib, W(11)), (en, 'g5', enb, W(12))):
            pt, pk = bank()
            S.op('pe', (lambda e, pt=pt, src=src: e.matmul(pt[:, :], sel, src, start=True, stop=True)), reads=[sk, 'selr'], writes=[pk])
            S.op('dve', (lambda e, pt=pt, dst=dst: e.tensor_copy(out=dst, in_=pt[:, :])), reads=[pk], writes=[dkey])
        psS = []
        for j in range(4):
            Nj = T - 128 * j
            t0 = 128 * j
            pt, pk = bank()
            for dkc in range(2):
                S.op('pe', (lambda e, pt=pt, dkc=dkc, t0=t0, Nj=Nj: e.matmul(pt[:, 0:Nj], kT(dkc)[:, t0:t0 + 128], qT(dkc)[:, t0:T],
                                                                           start=(dkc == 0), stop=(dkc == 1))),
                     reads=[W(0 + dkc), W(2 + dkc)], writes=[pk])
            wkey = W(wTb[j][0])
            S.op('act', (lambda e, j=j, t0=t0: e.activation(out=wT(j), in_=Mb[:, t0:T], func=AF.Exp, scale=-1.0, bias=acol[:, 4 * j + h:4 * j + h + 1])),
                 reads=[W(10), 'acol'], writes=[wkey])
            S.op('dve', (lambda e, j=j: e.tensor_tensor(out=wT(j)[:, 0:128], in0=wT(j)[:, 0:128], in1=mask01, op=ALU.mult)),
                 reads=[wkey, 'cst'], writes=[wkey])
            S.op('dve', (lambda e, j=j, Nj=Nj: e.tensor_copy(out=dec[:, j:j + 1], in_=wT(j)[:, Nj - 1:Nj])), reads=[wkey], writes=['dec'])
            S.op('dve', (lambda e, j=j, pt=pt, Nj=Nj: e.tensor_tensor(out=PT(j), in0=pt[:, 0:Nj], in1=wT(j), op=ALU.mult)),
                 reads=[pk, wkey], writes=[W(PTb[j][0])])
        for dkc in range(2):
            S.op('dve', (lambda e, dkc=dkc: e.tensor_tensor(out=qg(dkc), in0=B(0 + dkc, True), in1=gib, op=ALU.mult)),
                 reads=[W(0 + dkc), W(11)], writes=[W(16 + dkc)])
        Ck, nk = f'C{l}', f'n{l}'
        pN = []
        for dvc in range(2):
            pt, pk = bank()
            pN.append((pt, pk))
            for dkc in range(2):
                S.op('pe', (lambda e, pt=pt, dkc=dkc, dvc=dvc: e.matmul(pt[:, :], Cst[l][:, h, dkc, dvc * 128:(dvc + 1) * 128], qg(dkc),
                                                                      start=(dkc == 0), stop=False)),
                     reads=[Ck, W(16 + dkc)], writes=[pk])
            for j in range(4):
                S.op('pe', (lambda e, pt=pt, j=j, dvc=dvc: e.matmul(pt[:, 128 * j:T], vtok(j)[:, dvc * 128:(dvc + 1) * 128], PT(j),
                                                                  start=False, stop=(j == 3))),
                     reads=[W(4 + j // 2), W(PTb[j][0])], writes=[pk])
        pD, pDk = bank()
        for dkc in range(2):
            S.op('pe', (lambda e, dkc=dkc: e.matmul(pD[:, :], nrep[l][:, h, dkc, :], qg(dkc), start=(dkc == 0), stop=False)),
                 reads=[nk, W(16 + dkc)], writes=[pDk])
        for j in range(4):
            S.op('pe', (lambda e, j=j: e.matmul(pD[:, 128 * j:T], ones_r, PT(j), start=False, stop=(j == 3))),
                 reads=['ones_r', W(PTb[j][0])], writes=[pDk])
        S.op('dve', lambda e: e.tensor_tensor(out=rden, in0=pD[:, :], in1=enb, op=ALU.max), reads=[pDk, W(12)], writes=[W(20)])
        S.op('dve', lambda e: e.scalar_tensor_tensor(out=rden, in0=pD[:, :], scalar=-1.0, in1=rden, op0=ALU.mult, op1=ALU.max), reads=[pDk, W(20)], writes=[W(20)])
        act_rpow(S.op, rden, rden, [W(20)], W(20), -1.0)
        for dvc in range(2):
            pt, pk = pN[dvc]
            S.op('dve', (lambda e, pt=pt, dvc=dvc: e.tensor_tensor(out=yB(dvc), in0=pt[:, :], in1=rden, op=ALU.mult)), reads=[pk, W(20)], writes=[W(21 + dvc)])
            S.op('dve', (lambda e, dvc=dvc: e.tensor_tensor(out=yB(dvc), in0=yB(dvc), in1=so(dvc), op=ALU.mult)), reads=[W(21 + dvc), W(6 + dvc)], writes=[W(21 + dvc)])
            S.op('act', (lambda e, dvc=dvc: e.activation(out=ysq(dvc), in_=yB(dvc), func=AF.Square)), reads=[W(21 + dvc)], writes=[W(23 + dvc)])
        pQ, pQk = bank()
        for dvc in range(2):
            S.op('pe', (lambda e, dvc=dvc: e.matmul(pQ[:, :], ones_r, ysq(dvc), start=(dvc == 0), stop=(dvc == 1))), reads=['ones_r', W(23 + dvc)], writes=[pQk])
        act_rpow(S.op, rsh, pQ[:, :], [pQk, 'epsc'], W(25), -0.5, scale=1.0 / DK, bias=epsc[:, 0:1])
        for dvc in range(2):
            fc = 2 * h + dvc
            S.op('dve', (lambda e, dvc=dvc, fc=fc: e.scalar_tensor_tensor(out=yB(dvc), in0=yB(dvc), scalar=prm[:, fc, 10 * l + R_GMH:10 * l + R_GMH + 1],
                                                                        in1=rsh, op0=ALU.mult, op1=ALU.mult)),
                 reads=[W(21 + dvc), W(25), 'prm'], writes=[W(21 + dvc)])
            S.op('dve', (lambda e, dvc=dvc, fc=fc: e.tensor_tensor(out=mg[:, 8 + fc, :], in0=yB(dvc), in1=szb(dvc), op=ALU.mult)),
                 reads=[W(21 + dvc), W(8 + dvc)], writes=[f'mg{8 + fc}'])
        for j in range(4):
            pt, pk = bank()
            for dkc in range(2):
                S.op('pe', (lambda e, pt=pt, j=j, dkc=dkc: e.transpose(out=pt[:, dkc * 128:(dkc + 1) * 128], in_=B(2 + dkc, True)[:, j * 128:(j + 1) * 128],
                                                                     identity=ident)),
                     reads=[W(2 + dkc), 'cst'], writes=[pk])
            S.op('dve', (lambda e, pt=pt, j=j: e.tensor_scalar(out=kd(j), in0=pt[:, 0:256], scalar1=dec[:, j:j + 1], scalar2=None, op0=ALU.mult)),
                 reads=[pk, 'dec'], writes=[W(18 + j // 2)])
        pC, pCk = bank()
        for dkc in range(2):
            for j in range(4):
                S.op('pe', (lambda e, j=j, dkc=dkc: e.matmul(pC[:, dkc * 256:(dkc + 1) * 256], kd(j)[:, dkc * 128:(dkc + 1) * 128], vtok(j),
                                                           start=(j == 0), stop=(j == 3))),
                     reads=[W(18 + j // 2), W(4 + j // 2)], writes=[pCk])
        pNn, pNnk = bank()
        for dkc in range(2):
            for j in range(4):
                S.op('pe', (lambda e, j=j, dkc=dkc: e.matmul(pNn[:, dkc * 256:(dkc + 1) * 256], kd(j)[:, dkc * 128:(dkc + 1) * 128], ones_w[:, :],
                                                           start=(j == 0), stop=(j == 3))),
                     reads=[W(18 + j // 2), 'ones_r'], writes=[pNnk])
        for dkc in range(2):
            S.op('dve', (lambda e, dkc=dkc: e.scalar_tensor_tensor(out=Cst[l][:, h, dkc, :], in0=Cst[l][:].bitcast(F32)[:, h, dkc, :], scalar=gib[:, T - 1:T],
                                                                  in1=pC[:, dkc * 256:(dkc + 1) * 256], op0=ALU.mult, op1=ALU.add)),
                 reads=[Ck, W(11), pCk], writes=[Ck])
            S.op('dve', (lambda e, dkc=dkc: e.scalar_tensor_tensor(out=nrep[l][:, h, dkc, :], in0=nrep[l][:].bitcast(F32)[:, h, dkc, :], scalar=gib[:, T - 1:T],
                                                                  in1=pNn[:, dkc * 256:dkc * 256 + 128], op0=ALU.mult, op1=ALU.add)),
                 reads=[nk, W(11), pNnk], writes=[nk])


    def layer_prompt(l, tt):
        xk = lambda kc: f'x{kc}'
        rmsnorm(xk, lambda kc: prm[:, kc, 10 * l + R_GN:10 * l + R_GN + 1], lambda kc: hn[:, kc, :], lambda kc: f'hn{kc}', T)
        ig, l1, bcs, Mx, gi, en = [g[:] for g in grow]
        pgi, pgik = bank()
        for kc in range(KC):
            S.op('pe', (lambda e, kc=kc: e.matmul(pgi[0:4, :], wg[:, l, kc, 0:4], hn[:, kc, :], start=(kc == 0), stop=(kc == KC - 1))),
                 reads=['wg', f'hn{kc}'], writes=[pgik])
        pgf, pgfk = bank()
        for kc in range(KC):
            S.op('pe', (lambda e, kc=kc: e.matmul(pgf[0:4, :], wg[:, l, kc, 4:8], hn[:, kc, :], start=(kc == 0), stop=(kc == KC - 1))),
                 reads=['wg', f'hn{kc}'], writes=[pgfk])
        S.op('act', lambda e: e.activation(out=ig, in_=pgi[0:4, :], func=AF.Identity, bias=gcol[:, l:l + 1]), reads=[pgik, 'gcol'], writes=['g0'])
        S.op('act', lambda e: e.activation(out=l1, in_=pgf[0:4, :], func=AF.Exp, scale=-1.0, bias=gcol[:, 2 + l:3 + l]), reads=[pgfk, 'gcol'], writes=['g1'])
        S.op('act', lambda e: e.activation(out=l1, in_=l1, func=AF.Ln, bias=1.0), reads=['g1'], writes=['g1'])
        S.op('dve', lambda e: e.tensor_tensor_scan(out=bcs, data0=ones4, data1=l1, initial=0.0, op0=ALU.mult, op1=ALU.subtract),
             reads=['g1', 'cst'], writes=['g2'])
        S.op('dve', lambda e: e.tensor_tensor(out=ig, in0=ig, in1=bcs, op=ALU.subtract), reads=['g0', 'g2'], writes=['g0'])
        S.op('dve', lambda e: e.tensor_tensor_scan(out=Mx, data0=ig, data1=ig, initial=mst[:, l:l + 1], op0=ALU.max, op1=ALU.max),
             reads=['g0', f'mst{l}'], writes=['g3'])
        MxF = grow[3][:].bitcast(F32)
        S.op('act', lambda e: e.activation(out=gi, in_=MxF, func=AF.Exp, scale=-1.0, bias=mst[:, l:l + 1]), reads=['g3', f'mst{l}'], writes=['g4'])
        S.op('dve', lambda e: e.tensor_tensor(out=bcs, in0=bcs, in1=MxF, op=ALU.add), reads=['g2', 'g3'], writes=['g2'])
        S.op('act', lambda e: e.activation(out=en, in_=bcs, func=AF.Exp, scale=-1.0), reads=['g2'], writes=['g5'])
        S.op('dve', lambda e: e.tensor_copy(out=mst[:, l:l + 1], in_=bcs[:, T - 1:T]), reads=['g2'], writes=[f'mst{l}'])
        pa, pak = bank()
        for j in range(4):
            S.op('pe', (lambda e, j=j: e.transpose(out=pa[:, 4 * j:4 * j + 4], in_=ig[:, j * 128:(j + 1) * 128], identity=cst[0:4, C_ID:C_ID + 4])),
                 reads=['g0', 'cst'], writes=[pak])
        S.op('dve', lambda e: e.tensor_copy(out=acol[:], in_=pa[:, 0:16]), reads=[pak], writes=['acol'])

        for g in range(2):
            wa, wak = load_w(win_cols(l, 512 * g, 512), (KC, 512))
            wz, wzk = load_w(win_cols(l, 1024 + 512 * g, 512), (KC, 512))
            for j in range(4):
                rglru_chunk(l, 4 * g + j, j, wa, wak, wz, wzk)
        for h in range(NH):
            mlstm_head(l, h)
        for g4 in range(4):
            wo4, wo4k = load_w(wout_cols(l, 256 * g4, 256), (16, 256))
            for dd in range(2):
                dc = 2 * g4 + dd
                pt, pk = bank()
                for kc in range(16):
                    S.op('pe', (lambda e, pt=pt, kc=kc, dd=dd, wo4=wo4: e.matmul(pt[:, :], wo4[:, kc, dd * 128:(dd + 1) * 128], mg[:, kc, :],
                                                                      start=(kc == 0), stop=(kc == 15))),
                         reads=[wo4k, f'mg{kc}'], writes=[pk])
                S.op('dve', (lambda e, pt=pt, dc=dc: e.tensor_tensor(out=xT[:, dc, :], in0=xT[:, dc, :], in1=pt[:, :], op=ALU.add)),
                     reads=[pk, f'x{dc}'], writes=[f'x{dc}'])

    def x_load(tt):
        for j in range(4):
            r0 = tt * T + j * 128
            for half in range(2):
                key = f'xin{half}'
                S.dma('sp', xin[:, half * 512:(half + 1) * 512], x_p[r0:r0 + 128, half * 512:(half + 1) * 512], writes=[key], sem=key)
                pt, pk = bank()
                for q in range(4):
                    kc = half * 4 + q
                    S.op('pe', (lambda e, pt=pt, q=q, kc=kc: e.transpose(out=pt[:, q * 128:(q + 1) * 128], in_=xin[:, kc * 128:(kc + 1) * 128], identity=ident)),
                         reads=[key, 'cst'], writes=[pk])
                S.op('act', (lambda e, pt=pt, half=half, j=j: e.activation(out=xT[:, half * 4:half * 4 + 4, j * 128:(j + 1) * 128],
                                                                         in_=pt[:, :].rearrange("p (a b) -> p a b", a=4), func=AF.Copy)),
                     reads=[pk], writes=[f'x{half * 4 + q}' for q in range(4)])

    def y_store(tt):
        for j in range(4):
            r0 = tt * T + j * 128
            for half in range(2):
                key = f'yout{half}'
                pt, pk = bank()
                for q in range(4):
                    kc = half * 4 + q
                    S.op('pe', (lambda e, pt=pt, q=q, kc=kc, j=j: e.transpose(out=pt[:, q * 128:(q + 1) * 128], in_=wk(kc)[:, j * 128:(j + 1) * 128], identity=ident)),
                         reads=[f'wk{kc}', 'cst'], writes=[pk])
                S.op('act', (lambda e, pt=pt, half=half: e.activation(out=yout[:, half * 512:(half + 1) * 512], in_=pt[:, :], func=AF.Copy)),
                     reads=[pk], writes=[key])
                S.dma('act', y_p[r0:r0 + 128, half * 512:(half + 1) * 512], yout[:, half * 512:(half + 1) * 512], reads=[key], sem=key)

    for tt in range(NT):
        if tt == 0:
            x_load(0)
        for l in range(DEPTH):
            layer_prompt(l, tt)
        rmsnorm(lambda kc: f'x{kc}', lambda kc: prm[:, kc, R_GF:R_GF + 1], lambda kc: wk(kc), lambda kc: f'wk{kc}', T)
        if tt + 1 < NT:
            x_load(tt + 1)
        y_store(tt)

    for l in range(DEPTH):
        S.dma('sp', p_h[l].rearrange("(kc p) -> p kc", p=128), hst[:, l, :], reads=[f'hst{l}'], sem='ost', allow_slow_non_contiguous=True)
        for j in range(3):
            S.dma('sp', p_conv[l, j].rearrange("(kc p) -> p kc", p=128), convst[:, l * KC:(l + 1) * KC, j], reads=[f'cv{l}'], sem='ost', allow_slow_non_contiguous=True)
        S.dma('sp', p_C[l].rearrange("h (dkc p) v -> p h dkc v", p=128), Cst[l][:].bitcast(F32), reads=[f'C{l}'], sem='ost')
        S.dma('sp', p_n[l].rearrange("h (dkc p) -> p h dkc", p=128), nrep[l][:].bitcast(F32)[:, :, :, 0], reads=[f'n{l}'], sem='ost', allow_slow_non_contiguous=True)
        S.dma('sp', p_m[l].rearrange("(h o) -> h o", o=1), mst[:, l:l + 1], reads=[f'mst{l}'], sem='ost', allow_slow_non_contiguous=True)

    x_s = din("x_s", [DB, D]); st_h = din("st_h", [DEPTH, DB, D]); st_conv = din("st_conv", [DEPTH, DB, 3, D])
    st_C = din("st_C", [DEPTH, DB, NH, DK, DK]); st_n = din("st_n", [DEPTH, DB, NH, DK]); st_m = din("st_m", [DEPTH, DB, NH])
    gb_t = din("gb_t", [DB, 16])
    emask = din("emask", [128, 256])
    y_s = dout("y_s", [DB, D]); s_h = dout("s_h", [DEPTH, DB, D]); s_conv = dout("s_conv", [DEPTH, DB, 3, D])
    s_C = dout("s_C", [DEPTH, DB, NH, DK, DK]); s_n = dout("s_n", [DEPTH, DB, NH, DK]); s_m = dout("s_m", [DEPTH, DB, NH])

    N = DB
    x_d = sb("x_d", [128, KC, N])
    hn_d = sb("hn_d", [128, KC, N], BF16)
    mg_d = sb("mg_d", [128, 16, N], BF16)
    xcbd = sb("xcbd", [128, 2, N], BF16)
    gt = sb("gt", [N, 64])
    gbt = sb("gbt", [N, 16])

    XF = xT[:].rearrange("p a b -> p (a b)")
    MF = mg[:].bitcast(F32).rearrange("p a b -> p (a b)")
    HF = hn[:].bitcast(F32).rearrange("p a b -> p (a b)")
    cs_tok = XF[0:N, 0:3072]
    h0_tok = XF[0:N, 3072:4096]
    k_tok = MF[0:N, 0:1024]
    v_tok = MF[0:N, 1024:2048]
    n_tok = MF[0:N, 2048:3072]
    otok = MF[0:N, 3072:4096]
    rhs3 = HF[0:N, 512:560]
    Em = HF[:, 560:816].rearrange("p (b c) -> p b c", b=N)
    qm = xT[:].bitcast(BF16).rearrange("p a b -> p (a b)")[:, 0:2048].rearrange("p (g b c) -> p g b c", g=8, b=N)
    cq_tok = MF[0:N, 1024:2048]
    A_ = xin[:]
    B_ = yout[:]
    convT = A_[:, 0:384].rearrange("p (j c b) -> p j c b", j=3, c=KC)
    h0T = A_[:, 384:512].rearrange("p (c b) -> p c b", c=KC)
    xaT = A_[:, 512:640].rearrange("p (c b) -> p c b", c=KC)
    hT = A_[:, 640:768].rearrange("p (c b) -> p c b", c=KC)
    tmp = lambda i, n=1: A_[:, 768 + 16 * i:768 + 16 * (i + n)]
    qTd = B_[:, 0:128].rearrange("p (h c b) -> p h c b", h=NH, c=2)
    kTd = B_[:, 128:256].rearrange("p (h c b) -> p h c b", h=NH, c=2)
    vTd = B_[:, 256:384].rearrange("p (h c b) -> p h c b", h=NH, c=2)
    nTd = B_[:, 384:512].rearrange("p (h c b) -> p h c b", h=NH, c=2)
    sod = B_[:, 512:640].rearrange("p (h c b) -> p h c b", h=NH, c=2)
    szbd = B_[:, 640:768].rearrange("p (h c b) -> p h c b", h=NH, c=2)
    CqT = B_[:, 768:896].rearrange("p (h c b) -> p h c b", h=NH, c=2)
    bc = rsb[:, 64:384].rearrange("p (h c) -> p h c", h=NH)
    id16 = cst[0:N, C_ID:C_ID + N]
    ones16 = cst[0:N, C_ONE:C_ONE + 128]
    onesf = cst[:, C_ONE:C_ONE + 128]
    NCB = 3
    Cin = [work[:, 4 * i:4 * i + 4, :].rearrange("p a (c v) -> p (a c) v", c=2).rearrange("p (h c) v -> p h c v", h=NH) for i in range(NCB)]
    Cbf = [work[:, 12 + 2 * i:14 + 2 * i, :].bitcast(BF16).rearrange("p a (c v) -> p (a c) v", c=4).rearrange("p (h c) v -> p h c v", h=NH) for i in range(2)]
    kw_r = workr[0:N, 0:2, :].rearrange("p a b -> p (a b)")
    kw_f = workr[:].bitcast(F32)[0:N, 0:2, :].rearrange("p a b -> p (a b)")
    vm = [workr[0:N, 2 + i // 2, (i % 2) * 256:(i % 2) * 256 + 256] for i in range(4)]

    allkeys = ([f'x{k}' for k in range(KC)] + [f'hn{k}' for k in range(KC)] + [f'mg{k}' for k in range(16)] +
               [f'wk{i}' for i in range(NWK)] + [f'wr{i}' for i in range(15)] + ['xin0', 'xin1', 'yout0', 'yout1', 'ext0', 'ext1', 'rsb'])
    S.mark = len(S.ins)
    S.op('dve', lambda e: e.memset(gt[:, 0:1], 0.0), writes=allkeys + ['dbar'])

    def dop(eng, fn, reads=(), writes=()):
        S.op(eng, fn, list(reads) + ['dbar'], writes)

    def ddma(eng, out, in_, reads=(), writes=(), sem=None, **kw):
        S.dma(eng, out, in_, list(reads) + ['dbar'], writes, sem, **kw)

    hnd_fn = lambda kc: hn_d[:, kc, :]
    hndk_fn = lambda kc: 'hnd'

    def to_fm(src_tok, skey, dst3, dkey, nch=KC):
        pt, pk = bank()
        for kc in range(nch):
            dop('pe', (lambda e, kc=kc, pt=pt: e.transpose(out=pt[:, kc * N:(kc + 1) * N], in_=src_tok[:, kc * 128:(kc + 1) * 128], identity=id16)),
                reads=[skey, 'cst'], writes=[pk])
        dop('act', (lambda e, pt=pt: e.activation(out=dst3, in_=pt[:, 0:nch * N].rearrange("p (c b) -> p c b", c=nch), func=AF.Copy)),
            reads=[pk], writes=[dkey])

    def to_tok(src3, skey, dst_tok, dkey):
        for half in range(2):
            pt, pk = bank()
            for q in range(4):
                dop('pe', (lambda e, q=q, pt=pt, half=half: e.transpose(out=pt[0:N, q * 128:(q + 1) * 128], in_=src3[:, half * 4 + q, :], identity=ident)),
                    reads=[skey, 'cst'], writes=[pk])
            dop('act', (lambda e, pt=pt, half=half: e.activation(out=dst_tok[:, half * 512:(half + 1) * 512], in_=pt[0:N, :], func=AF.Copy)),
                reads=[pk], writes=[dkey])

    def decode_layer(l):
        rmsnorm(lambda kc: 'xd', lambda kc: prm[:, kc, 10 * l + R_GN:10 * l + R_GN + 1], hnd_fn, hndk_fn, N,
                src_fn=lambda kc: x_d[:, kc, :], extra=['dbar'])
        ddma('sp', cs_tok.rearrange("p (j d) -> p j d", j=3), st_conv[l], writes=['d_cs'] + [f'd_qm{b_}_{g_}' for b_ in range(N) for g_ in range(8)], sem='d_cs')
        ddma('sp', h0_tok, st_h[l], writes=['d_h0'], sem='d_h0')
        ddma('sp', n_tok.rearrange("p (h k) -> p h k", h=NH), st_n[l], writes=['d_n'], sem='d_n')
        ddma('sp', gt[:, 0:4], st_m[l], writes=['d_mp'], sem='d_mp')
        ddma('sp', s_conv[l, :, 0:2, :], st_conv[l, :, 1:3, :], sem='d_d2d')
        for j in range(3):
            to_fm(cs_tok[:, j * D:(j + 1) * D], 'd_cs', convT[:, j, :, :], 'd_convT')
        to_fm(h0_tok, 'd_h0', h0T, 'd_h0T')
        to_fm(n_tok, 'd_n', nTd.rearrange("p h c b -> p (h c) b"), 'd_nT')
        pG, pGk = bank()
        for kc in range(KC):
            dop('pe', (lambda e, kc=kc: e.matmul(pG[0:N, 0:8], hn_d[:, kc, :], wg[:, l, kc, :], start=(kc == 0), stop=(kc == KC - 1))),
                reads=['hnd', 'wg'], writes=[pGk])
        G = lambda a: gt[:, a:a + 4]
        mp, ig, l1, gg, mt, w_, gi_, en_, t_ = G(0), G(4), G(8), G(12), G(16), G(20), G(24), G(28), G(32)
        dop('dve', lambda e: e.tensor_tensor(out=ig, in0=pG[0:N, 0:4], in1=gbt[:, 4 * l:4 * l + 4], op=ALU.add), reads=[pGk, 'd_gbt'], writes=['d_gt'])
        dop('dve', lambda e: e.tensor_tensor(out=l1, in0=pG[0:N, 4:8], in1=gbt[:, 8 + 4 * l:12 + 4 * l], op=ALU.add), reads=[pGk, 'd_gbt', 'd_gt'], writes=['d_gt'])
        dop('act', lambda e: e.activation(out=l1, in_=l1, func=AF.Exp, scale=-1.0), reads=['d_gt'], writes=['d_gt'])
        dop('act', lambda e: e.activation(out=l1, in_=l1, func=AF.Ln, bias=1.0), reads=['d_gt'], writes=['d_gt'])
        dop('dve', lambda e: e.tensor_tensor(out=gg, in0=mp, in1=l1, op=ALU.subtract), reads=['d_gt', 'd_mp'], writes=['d_gt'])
        dop('dve', lambda e: e.tensor_tensor(out=mt, in0=gg, in1=ig, op=ALU.max), reads=['d_gt'], writes=['d_gt'])
        dop('dve', lambda e: e.tensor_tensor(out=t_, in0=ig, in1=mt, op=ALU.subtract), reads=['d_gt'], writes=['d_gt'])
        dop('act', lambda e: e.activation(out=w_, in_=t_, func=AF.Exp), reads=['d_gt'], writes=['d_gt'])
        dop('dve', lambda e: e.tensor_tensor(out=t_, in0=gg, in1=mt, op=ALU.subtract), reads=['d_gt'], writes=['d_gt'])
        dop('act', lambda e: e.activation(out=gi_, in_=t_, func=AF.Exp), reads=['d_gt'], writes=['d_gt'])
        dop('act', lambda e: e.activation(out=en_, in_=mt, func=AF.Exp, scale=-1.0), reads=['d_gt'], writes=['d_gt'])
        ddma('sp', s_m[l], mt, reads=['d_gt'], sem='d_sm')

        def rg_chunk(c, j, wa, wak, wz, wzk):
            tb = 7 * (c % 2)
            TK = lambda i: f'd_t{tb + i}'
            sz, xc, r_, i_, m_ = tmp(tb + 0), tmp(tb + 1), tmp(tb + 2), tmp(tb + 3), tmp(tb + 4)
            xb = xcbd[:, c % 2, :]
            xbk = f'd_xcb{c % 2}'
            pX, pXk = proj_fm(wa, wak, j, N, hn_fn=hnd_fn, hnk_fn=hndk_fn)
            pZ, pZk = proj_fm(wz, wzk, j, N, hn_fn=hnd_fn, hnk_fn=hndk_fn)
            dop('act', lambda e: e.activation(out=xaT[:, c, :], in_=pX[:, 0:N], func=AF.Copy), reads=[pXk], writes=['d_xaT'])
            act_sig(dop, sz, pZ[:, 0:N], [pZk], TK(0))
            dop('dve', lambda e: e.tensor_tensor(out=sz, in0=sz, in1=pZ[:, 0:N], op=ALU.mult), reads=[pZk, TK(0)], writes=[TK(0)])
            cw = lambda jj: prm[:, c, 10 * l + R_CW + jj:10 * l + R_CW + jj + 1]
            dop('dve', lambda e: e.tensor_scalar(out=xc, in0=convT[:, 0, c, :], scalar1=cw(0), scalar2=prm[:, c, 10 * l + R_CB:10 * l + R_CB + 1],
                                                op0=ALU.mult, op1=ALU.add), reads=['d_convT', 'prm'], writes=[TK(1)])
            for jj in (1, 2):
                dop('dve', (lambda e, jj=jj: e.scalar_tensor_tensor(out=xc, in0=convT[:, jj, c, :], scalar=cw(jj), in1=xc, op0=ALU.mult, op1=ALU.add)),
                    reads=['d_convT', 'prm', TK(1)], writes=[TK(1)])
            dop('dve', lambda e: e.scalar_tensor_tensor(out=xc, in0=xaT[:, c, :], scalar=cw(3), in1=xc, op0=ALU.mult, op1=ALU.add),
                reads=['d_xaT', 'prm', TK(1)], writes=[TK(1)])
            dop('act', lambda e: e.activation(out=xb, in_=xc, func=AF.Copy), reads=[TK(1)], writes=[xbk])
            pR, pRk = bank()
            dop('pe', lambda e: e.matmul(pR[:, 0:N], wbd[:, (0 * DEPTH + l) * KC + c, :], xb, start=True, stop=True), reads=['wbd0'] + [f'wbd_{a_}{b_}{c_}' for a_ in range(2) for b_ in range(2) for c_ in range(2)] + [xbk], writes=[pRk])
            pI, pIk = bank()
            dop('pe', lambda e: e.matmul(pI[:, 0:N], wbd[:, (1 * DEPTH + l) * KC + c, :], xb, start=True, stop=True), reads=['wbd0'] + [f'wbd_{a_}{b_}{c_}' for a_ in range(2) for b_ in range(2) for c_ in range(2)] + [xbk], writes=[pIk])
            act_sig(dop, r_, pR[:, 0:N], [pRk], TK(2), nbias=nprm[:, c, 10 * l + R_BR:10 * l + R_BR + 1])
            act_sig(dop, i_, pI[:, 0:N], [pIk], TK(3), nbias=nprm[:, c, 10 * l + R_BI:10 * l + R_BI + 1])
            dop('act', lambda e: e.activation(out=r_, in_=r_, func=AF.Exp, scale=nsp[:, l, c:c + 1]), reads=[TK(2), 'nsp'], writes=[TK(2)])
            dop('act', lambda e: e.activation(out=m_, in_=r_, func=AF.Square), reads=[TK(2)], writes=[TK(4)])
            act_rpow(dop, m_, m_, [TK(4), 'onep'], TK(4), 0.5, scale=-1.0, bias=onep[:, 0:1])
            dop('dve', lambda e: e.tensor_tensor(out=i_, in0=i_, in1=xc, op=ALU.mult), reads=[TK(3), TK(1)], writes=[TK(3)])
            dop('dve', lambda e: e.tensor_tensor(out=i_, in0=i_, in1=m_, op=ALU.mult), reads=[TK(3), TK(4)], writes=[TK(3)])
            dop('dve', lambda e: e.tensor_tensor(out=hT[:, c, :], in0=r_, in1=h0T[:, c, :], op=ALU.mult), reads=[TK(2), 'd_h0T'], writes=['d_hT'])
            dop('dve', lambda e: e.tensor_tensor(out=hT[:, c, :], in0=hT[:, c, :], in1=i_, op=ALU.add), reads=['d_hT', TK(3)], writes=['d_hT'])
            dop('dve', lambda e: e.tensor_tensor(out=mg_d[:, c, :], in0=hT[:, c, :], in1=sz, op=ALU.mult), reads=['d_hT', TK(0)], writes=['mgd'])

        for g in range(2):
            wa, wak = load_w(win_cols(l, 512 * g, 512), (KC, 512))
            wz, wzk = load_w(win_cols(l, 1024 + 512 * g, 512), (KC, 512))
            for j in range(4):
                rg_chunk(4 * g + j, j, wa, wak, wz, wzk)
        to_tok(xaT, 'd_xaT', otok, 'd_otok')
        ddma('sp', s_conv[l, :, 2, :], otok, reads=['d_otok'], sem='d_otok')
        to_tok(hT, 'd_hT', otok, 'd_otok')
        ddma('sp', s_h[l], otok, reads=['d_otok'], sem='d_otok')

        def ml_head_proj(h):
            wq, wqk = load_w(win_cols(l, 2048 + 256 * h, 256), (KC, 256))
            wk_, wkk = load_w(win_cols(l, 3072 + 256 * h, 256), (KC, 256))
            for dkc in range(2):
                pt, pk = proj_fm(wq, wqk, dkc, N, hn_fn=hnd_fn, hnk_fn=hndk_fn)
                dop('dve', (lambda e, pt=pt, dkc=dkc: e.tensor_copy(out=qTd[:, h, dkc, :], in_=pt[:, 0:N])), reads=[pk], writes=['d_qT'])
                pt, pk = proj_fm(wk_, wkk, dkc, N, hn_fn=hnd_fn, hnk_fn=hndk_fn)
                dop('act', (lambda e, pt=pt, dkc=dkc: e.activation(out=kTd[:, h, dkc, :], in_=pt[:, 0:N], func=AF.Copy, scale=DK ** -0.5)), reads=[pk], writes=['d_kT'])
            wv_, wvk = load_w(win_cols(l, 4096 + 256 * h, 256), (KC, 256))
            wo_, wok = load_w(win_cols(l, 5120 + 256 * h, 256), (KC, 256))
            wzb_, wzbk = load_w(win_cols(l, 6144 + 256 * h, 256), (KC, 256))
            for dvc in range(2):
                pt, pk = proj_fm(wv_, wvk, dvc, N, hn_fn=hnd_fn, hnk_fn=hndk_fn)
                dop('act', (lambda e, pt=pt, dvc=dvc: e.activation(out=vTd[:, h, dvc, :], in_=pt[:, 0:N], func=AF.Copy)), reads=[pk], writes=['d_vT'])
                pt, pk = proj_fm(wo_, wok, dvc, N, hn_fn=hnd_fn, hnk_fn=hndk_fn)
                act_sig(dop, sod[:, h, dvc, :], pt[:, 0:N], [pk], 'd_so')
                pt, pk = proj_fm(wzb_, wzbk, dvc, N, hn_fn=hnd_fn, hnk_fn=hndk_fn)
                act_sig(dop, szbd[:, h, dvc, :], pt[:, 0:N], [pk], 'd_szb')
                dop('dve', (lambda e, pt=pt, dvc=dvc: e.tensor_tensor(out=szbd[:, h, dvc, :], in0=szbd[:, h, dvc, :], in1=pt[:, 0:N], op=ALU.mult)),
                    reads=[pk, 'd_szb'], writes=['d_szb'])
            pt, pk = bank()
            for dkc in range(2):
                dop('pe', (lambda e, dkc=dkc: e.transpose(out=pt[0:N, dkc * 128:(dkc + 1) * 128], in_=kTd[:, h, dkc, :], identity=ident)), reads=['d_kT', 'cst'], writes=[pk])
            for dvc in range(2):
                dop('pe', (lambda e, dvc=dvc: e.transpose(out=pt[0:N, 256 + dvc * 128:256 + (dvc + 1) * 128], in_=vTd[:, h, dvc, :], identity=ident)), reads=['d_vT', 'cst'], writes=[pk])
            dop('dve', lambda e: e.tensor_scalar(out=kw_r[:, h * 256:(h + 1) * 256], in0=pt[0:N, 0:256], scalar1=gt[:, 20 + h:21 + h], scalar2=None, op0=ALU.mult),
                reads=[pk, 'd_gt'], writes=['d_ktok'])
            dop('dve', lambda e: e.tensor_copy(out=v_tok[:, h * 256:(h + 1) * 256], in_=pt[0:N, 256:512]), reads=[pk], writes=['d_vtok'])
            dop('dve', lambda e: e.scalar_tensor_tensor(out=n_tok[:, h * 256:(h + 1) * 256], in0=n_tok[:, h * 256:(h + 1) * 256], scalar=gt[:, 24 + h:25 + h],
                                                       in1=kw_f[:, h * 256:(h + 1) * 256], op0=ALU.mult, op1=ALU.add),
                reads=['d_n', 'd_nT', 'd_gt', 'd_ktok'], writes=['d_n'])
            for q in range(3):
                dop('dve', (lambda e, q=q: e.tensor_scalar(out=rhs3[:, 16 * q:16 * q + 16], in0=id16, scalar1=gt[:, 20 + 4 * q + h:21 + 4 * q + h], scalar2=None, op0=ALU.mult)),
                    reads=['cst', 'd_gt'], writes=['d_rhs3'])
            pB, pBk = bank()
            dop('pe', lambda e: e.matmul(pB[:, 0:48], ones16, rhs3, start=True, stop=True), reads=['cst', 'd_rhs3'], writes=[pBk])
            dop('dve', lambda e: e.tensor_copy(out=bc[:, h, 0:48], in_=pB[:, 0:48]), reads=[pBk], writes=['d_bc'])
            t0 = tmp(0, 2).rearrange("p (c b) -> p c b", c=2)
            t1 = tmp(2, 2).rearrange("p (c b) -> p c b", c=2)
            dop('dve', lambda e: e.tensor_tensor(out=t0, in0=qTd[:, h, :, :], in1=kTd[:, h, :, :], op=ALU.mult), reads=['d_qT', 'd_kT'], writes=['d_t0', 'd_t1'])
            dop('dve', lambda e: e.tensor_tensor(out=t1, in0=qTd[:, h, :, :], in1=nTd[:, h, :, :], op=ALU.mult), reads=['d_qT', 'd_nT'], writes=['d_t2', 'd_t3'])
            pQ, pQk = bank()
            for dkc in range(2):
                dop('pe', (lambda e, dkc=dkc: e.matmul(pQ[:, 0:N], onesf, t0[:, dkc, :], start=(dkc == 0), stop=(dkc == 1))), reads=['cst', 'd_t0', 'd_t1'], writes=[pQk])
            for dkc in range(2):
                dop('pe', (lambda e, dkc=dkc: e.matmul(pQ[:, N:2 * N], onesf, t1[:, dkc, :], start=(dkc == 0), stop=(dkc == 1))), reads=['cst', 'd_t2', 'd_t3'], writes=[pQk])
            dop('dve', lambda e: e.tensor_copy(out=bc[:, h, 48:80], in_=pQ[:, 0:2 * N]), reads=[pQk], writes=['d_bc'])

        for h in range(NH):
            ml_head_proj(h)
        ddma('sp', s_n[l], n_tok.rearrange("p (h k) -> p h k", h=NH), reads=['d_n'], sem='d_n')

        dop('dve', lambda e: e.memset(gt[:, 40:41], 0.0), reads=['d_convT', 'd_h0T'], writes=['d_cs', 'd_qmf'])
        for b in range(N):
            for h in range(NH):
                for dkc in range(2):
                    g = h * 2 + dkc
                    if (g + b) % 2 == 0:
                        dop('dve', (lambda e, g=g, b=b, h=h, dkc=dkc: e.tensor_scalar(out=qm[:, g, b, :], in0=Em[:, b, :], scalar1=qTd[:, h, dkc, b:b + 1],
                                                                                   scalar2=None, op0=ALU.mult)),
                            reads=['d_em', 'd_qT', 'd_qmf'], writes=[f'd_qm{b}_{g}'])
                    else:
                        dop('act', (lambda e, g=g, b=b, h=h, dkc=dkc: e.activation(out=qm[:, g, b, :], in_=Em[:, b, :], func=AF.Copy, scale=qTd[:, h, dkc, b:b + 1])),
                            reads=['d_em', 'd_qT', 'd_qmf'], writes=[f'd_qm{b}_{g}'])
        pCqs = [bank(), bank()]
        pCis = [ps.index(p_) for p_, _ in pCqs]
        for i_ in pCis:
            reserved.add(i_)
        def c_load(b):
            ddma('sp', Cin[b % NCB], st_C[l, b].rearrange("h (c p) v -> p h c v", p=128), writes=[f'd_Cin{b % NCB}'], sem=f'd_Cin{b % NCB}')
        for b in range(NCB - 1):
            c_load(b)
        for b in range(N):
            ci, cik = Cin[b % NCB], f'd_Cin{b % NCB}'
            cb_, cbk = Cbf[b % 2], f'd_Cbf{b % 2}'
            if b + NCB - 1 < N:
                c_load(b + NCB - 1)
            for h2 in range(2):
                dop('dve' if h2 == 0 else 'act',
                    ((lambda e, ci=ci, cb_=cb_, h2=h2: e.tensor_copy(out=cb_[:, 2 * h2:2 * h2 + 2, :, :], in_=ci[:, 2 * h2:2 * h2 + 2, :, :])) if h2 == 0 else
                     (lambda e, ci=ci, cb_=cb_, h2=h2: e.activation(out=cb_[:, 2 * h2:2 * h2 + 2, :, :], in_=ci[:, 2 * h2:2 * h2 + 2, :, :], func=AF.Copy))),
                    reads=[cik], writes=[cbk])
            for h in range(NH):
                pq, pqk = pCqs[h // 2]
                for dkc in range(2):
                    first = (b == 0 and h % 2 == 0 and dkc == 0)
                    dop('pe', (lambda e, h=h, dkc=dkc, cb_=cb_, b=b, pq=pq, first=first: e.matmul(pq[0:N, (h % 2) * 256:(h % 2) * 256 + 256], qm[:, h * 2 + dkc, b, :],
                                                                                         cb_[:, h, dkc, :], start=first, stop=(b == N - 1 and dkc == 1),
                                                                                         skip_group_check=True)),
                        reads=[cbk, f'd_qm{b}_{h * 2 + dkc}'], writes=[pqk])
            for h in range(NH):
                vi = (b * NH + h) % 4
                dop('dve', (lambda e, h=h, vi=vi, b=b: e.tensor_scalar(out=vm[vi], in0=v_tok[:, h * 256:(h + 1) * 256], scalar1=cst[0:N, C_ID + b:C_ID + b + 1],
                                                                   scalar2=None, op0=ALU.mult)), reads=['d_vtok', 'cst'], writes=[f'd_vm{vi}'])
                for dkc in range(2):
                    pU, pUk = bank()
                    dop('pe', (lambda e, h=h, dkc=dkc, vi=vi, pU=pU: e.matmul(pU[:, 0:256], kw_r[:, h * 256 + dkc * 128:h * 256 + (dkc + 1) * 128], vm[vi],
                                                                          start=True, stop=True)), reads=['d_ktok', f'd_vm{vi}'], writes=[pUk])
                    dop('dve', (lambda e, h=h, dkc=dkc, pU=pU, ci=ci, b=b: e.scalar_tensor_tensor(out=ci[:, h, dkc, :], in0=ci[:, h, dkc, :],
                                                                                         scalar=bc[:, h, 16 + b:17 + b], in1=pU[:, 0:256],
                                                                                         op0=ALU.mult, op1=ALU.add)),
                        reads=[cik, 'd_bc', pUk], writes=[cik])
            ddma('pool', s_C[l, b].rearrange("h (c p) v -> p h c v", p=128), ci, reads=[cik], sem=f'd_Cst{b % NCB}')

        for i_, (pq, pqk) in enumerate(pCqs):
            dop('act', (lambda e, pq=pq, i_=i_: e.activation(out=cq_tok[:, i_ * 512:(i_ + 1) * 512], in_=pq[0:N, :], func=AF.Copy)),
                reads=[pqk, 'd_ktok'], writes=['d_vtok'])
        to_fm(cq_tok, 'd_vtok', CqT.rearrange("p h c b -> p (h c) b"), 'd_CqT')
        for i_ in pCis:
            reserved.discard(i_)
        def ml_head_out(h):
            Pb, den, dn2, rs_ = tmp(4), tmp(5), tmp(6), tmp(11)
            wb, gib, enb, qkb, qnb = bc[:, h, 0:16], bc[:, h, 16:32], bc[:, h, 32:48], bc[:, h, 48:64], bc[:, h, 64:80]
            dop('dve', lambda e: e.tensor_tensor(out=Pb, in0=wb, in1=qkb, op=ALU.mult), reads=['d_bc'], writes=['d_t4'])
            dop('dve', lambda e: e.tensor_tensor(out=den, in0=gib, in1=qnb, op=ALU.mult), reads=['d_bc'], writes=['d_t5'])
            dop('dve', lambda e: e.tensor_tensor(out=den, in0=den, in1=Pb, op=ALU.add), reads=['d_t5', 'd_t4'], writes=['d_t5'])
            dop('dve', lambda e: e.tensor_tensor(out=dn2, in0=den, in1=enb, op=ALU.max), reads=['d_t5', 'd_bc'], writes=['d_t6'])
            dop('dve', lambda e: e.scalar_tensor_tensor(out=dn2, in0=den, scalar=-1.0, in1=dn2, op0=ALU.mult, op1=ALU.max), reads=['d_t5', 'd_t6'], writes=['d_t6'])
            act_rpow(dop, dn2, dn2, ['d_t6'], 'd_t6', -1.0)
            pQ2, pQ2k = bank()
            for dvc in range(2):
                y, t2, ysq_ = tmp(7 + dvc), tmp(9), tmp(12 + dvc)
                yk, ysk = f'd_t{7 + dvc}', f'd_t{12 + dvc}'
                col = (h * 2 + dvc) * N
                dop('dve', (lambda e, y=y, dvc=dvc: e.tensor_tensor(out=y, in0=CqT[:, h, dvc, :], in1=gib, op=ALU.mult)), reads=['d_CqT', 'd_bc'], writes=[yk])
                dop('dve', (lambda e, t2=t2, dvc=dvc: e.tensor_tensor(out=t2, in0=vTd[:, h, dvc, :], in1=Pb, op=ALU.mult)), reads=['d_vT', 'd_t4'], writes=['d_t9'])
                dop('dve', (lambda e, y=y, t2=t2: e.tensor_tensor(out=y, in0=y, in1=t2, op=ALU.add)), reads=[yk, 'd_t9'], writes=[yk])
                dop('dve', (lambda e, y=y: e.tensor_tensor(out=y, in0=y, in1=dn2, op=ALU.mult)), reads=[yk, 'd_t6'], writes=[yk])
                dop('dve', (lambda e, y=y, dvc=dvc: e.tensor_tensor(out=y, in0=y, in1=sod[:, h, dvc, :], op=ALU.mult)), reads=[yk, 'd_so'], writes=[yk])
                dop('act', (lambda e, y=y, ysq_=ysq_: e.activation(out=ysq_, in_=y, func=AF.Square)), reads=[yk], writes=[ysk])
                dop('pe', (lambda e, ysq_=ysq_, dvc=dvc: e.matmul(pQ2[:, 0:N], onesf, ysq_, start=(dvc == 0), stop=(dvc == 1))), reads=['cst', ysk], writes=[pQ2k])
            act_rpow(dop, rs_, pQ2[:, 0:N], [pQ2k, 'epsc'], 'd_t11', -0.5, scale=1.0 / DK, bias=epsc[:, 0:1])
            for dvc in range(2):
                y = tmp(7 + dvc)
                yk = f'd_t{7 + dvc}'
                fc = 2 * h + dvc
                dop('dve', (lambda e, y=y, fc=fc: e.scalar_tensor_tensor(out=y, in0=y, scalar=prm[:, fc, 10 * l + R_GMH:10 * l + R_GMH + 1], in1=rs_,
                                                                       op0=ALU.mult, op1=ALU.mult)), reads=[yk, 'd_t11', 'prm'], writes=[yk])
                dop('dve', (lambda e, y=y, fc=fc, dvc=dvc: e.tensor_tensor(out=mg_d[:, 8 + fc, :], in0=y, in1=szbd[:, h, dvc, :], op=ALU.mult)),
                    reads=[yk, 'd_szb'], writes=['mgd'])

        for h in range(NH):
            ml_head_out(h)

        for g4 in range(4):
            wo4, wo4k = load_w(wout_cols(l, 256 * g4, 256), (16, 256))
            for dd in range(2):
                dc = 2 * g4 + dd
                pt, pk = bank()
                for kc in range(16):
                    dop('pe', (lambda e, pt=pt, kc=kc, dd=dd, wo4=wo4: e.matmul(pt[:, 0:N], wo4[:, kc, dd * 128:(dd + 1) * 128], mg_d[:, kc, :],
                                                                            start=(kc == 0), stop=(kc == 15))), reads=[wo4k, 'mgd'], writes=[pk])
                dop('dve', (lambda e, pt=pt, dc=dc: e.tensor_tensor(out=x_d[:, dc, :], in0=x_d[:, dc, :], in1=pt[:, 0:N], op=ALU.add)), reads=[pk, 'xd'], writes=['xd'])

    if do_decode:
        ddma('sp', otok, x_s, writes=['d_otok'], sem='d_otok')
        ddma('sp', gbt[:], gb_t, writes=['d_gbt'], sem='d_gbt')
        ddma('sp', HF[:, 560:816], emask, writes=['d_em'], sem='d_em')
        to_fm(otok, 'd_otok', x_d[:], 'xd')
        for l in range(DEPTH):
            decode_layer(l)
        rmsnorm(lambda kc: 'xd', lambda kc: prm[:, kc, R_GF:R_GF + 1], lambda kc: hT[:, kc, :], lambda kc: 'd_hT', N,
                src_fn=lambda kc: x_d[:, kc, :], extra=['dbar'])
        to_tok(hT, 'd_hT', otok, 'd_otok')
        ddma('sp', y_s, otok, reads=['d_otok'], sem='d_otok')

    S.emit()
    return nc, es


def make_consts():
    c = np.zeros((128, NCONST), np.float32)
    c[:, C_ID:C_ID + 128] = np.eye(128, dtype=np.float32)
    c[:, C_ONE:C_ONE + 128] = 1.0
    s = np.arange(128)
    c[:, C_MASK:C_MASK + 128] = (s[:, None] <= s[None, :]).astype(np.float32)
    for h in range(NH):
        c[h, C_SEL + 128 * h:C_SEL + 128 * h + 128] = 1.0
    c[:, C_ONES4:C_ONES4 + T] = 1.0
    return c


_CACHE = {}


def kernel(**inputs):
    f = lambda k: np.ascontiguousarray(np.asarray(inputs[k], dtype=np.float32))
    if 'nc' not in _CACHE:
        _CACHE['nc'] = build_program()
    nc, _es = _CACHE['nc']
    consts = make_consts()
    shared = {k: f(k) for k in ("g_norm", "w_in", "conv_w", "conv_b", "w_rgate", "b_rgate", "w_igate", "b_igate",
                                "lru_lambda", "b_mi", "b_mf", "g_mhead", "w_out", "g_final")}
    xp = f("x_prompt")
    xs = f("x_sample").reshape(128, D)
    sth, stc, stC, stn, stm = f("state_rglru_h"), f("state_rglru_conv"), f("state_mlstm_C"), f("state_mlstm_n"), f("state_mlstm_m")
    gb = np.ascontiguousarray(np.tile(np.concatenate([shared["b_mi"].reshape(-1), shared["b_mf"].reshape(-1)])[None, :], (DB, 1)))
    em = np.ascontiguousarray(np.tile(np.eye(DB, dtype=np.float32).reshape(1, DB * DB), (128, 1)))
    in_maps = []
    for i in range(NCORES):
        m = dict(shared)
        sl = slice(i * DB, (i + 1) * DB)
        m["consts"] = consts
        m["x_p"] = xp[i]
        m["x_s"] = np.ascontiguousarray(xs[sl])
        m["st_h"] = np.ascontiguousarray(sth[:, sl]); m["st_conv"] = np.ascontiguousarray(stc[:, sl])
        m["st_C"] = np.ascontiguousarray(stC[:, sl]); m["st_n"] = np.ascontiguousarray(stn[:, sl]); m["st_m"] = np.ascontiguousarray(stm[:, sl])
        m["gb_t"] = gb
        m["emask"] = em
        in_maps.append(m)
    res = run_bass_kernel_spmd(nc, in_maps, core_ids=list(range(NCORES)))
    R = res.results
    g = lambda k: np.stack([np.asarray(R[i][k], dtype=np.float32) for i in range(NCORES)], axis=0)
    gc = lambda k, ax: np.ascontiguousarray(np.concatenate([np.asarray(R[i][k], dtype=np.float32) for i in range(NCORES)], axis=ax))
    y_prompt = g("y_p")
    p_h = np.ascontiguousarray(g("p_h").transpose(1, 0, 2))
    p_conv = np.ascontiguousarray(g("p_conv").transpose(1, 0, 2, 3))
    p_C = np.ascontiguousarray(g("p_C").transpose(1, 0, 2, 3, 4))
    p_n = np.ascontiguousarray(g("p_n").transpose(1, 0, 2, 3))
    p_m = np.ascontiguousarray(g("p_m").transpose(1, 0, 2))
    y_s = gc("y_s", 0).reshape(128, 1, D)
    s_h = gc("s_h", 1); s_conv = gc("s_conv", 1); s_C = gc("s_C", 1); s_n = gc("s_n", 1); s_m = gc("s_m", 1)
    return (y_prompt, y_s, p_h, p_conv, p_C, p_n, p_m, s_h, s_conv, s_C, s_n, s_m)
```

```python
import contextlib
import numpy as np
import concourse.bass as bass
import concourse.mybir as mybir
from concourse.bass_utils import run_bass_kernel_spmd

F32 = mybir.dt.float32
F32R = mybir.dt.float32r
BF16 = mybir.dt.bfloat16
AF = mybir.ActivationFunctionType
ALU = mybir.AluOpType

NCORES = 8
WHATIF = {}
USE_SCRATCH = True
PAIR_LOADS = True
D = 1024
KC = 8
SEQ = 2048
T = 512
NT = SEQ // T
DIN = 7176
DEPTH = 2
NH = 4
DK = 256
EPS = 1e-6
DB = 16

C_ID, C_ONE, C_MASK, C_SEL, C_ONES4, NCONST = 0, 128, 256, 384, 896, 1408
R_GN, R_CW, R_CB, R_BR, R_BI, R_LAM, R_GMH = 0, 1, 5, 6, 7, 8, 9
R_GF = 20
NR = 21


class _FakeIns:
    def then_inc(self, *a, **k):
        return self


class _FakeEng:
    def __init__(self):
        self.rec = None

    def __getattr__(self, name):
        def f(*a, **k):
            self.rec = (name, a, k)
            return _FakeIns()
        return f


def _free(ap):
    n = 1
    for d in ap.shape[1:]:
        n *= int(d)
    return n


def _cost(eng, fn):
    fe = _FakeEng()
    fn(fe)
    name, a, k = fe.rec
    if name == 'matmul':
        rhs = a[2] if len(a) > 2 else k['rhs']
        lhs = a[1] if len(a) > 1 else k['lhsT']
        n = _free(rhs)
        m = _free(lhs)
        dt = rhs.dtype
        passes = 4 if (dt == F32 or (dt == F32R and n < 256)) else 1
        return 0.015 + passes * max(n, m) / 2000.0
    if name == 'transpose':
        return 0.12
    if name == 'dma_start':
        out = k['out']
        src = k['in_']
        esz = max(2 if out.dtype == BF16 else 4, 2 if src.dtype == BF16 else 4)
        nbytes = _free(out) * int(out.shape[0]) * esz
        return 2.0 + nbytes / 330e3
    out = k.get('out', a[0] if a else None)
    n = _free(out) if out is not None else 64
    if eng == 'act':
        return 0.2 + n / 1400.0
    if name == 'tensor_tensor_scan':
        return 0.1 + 2 * n / 960.0
    if eng == 'pool':
        return 0.3 + n / 500.0
    return 0.08 + n / 1000.0


class Sched:
    ENGS = ('pe', 'act', 'dve', 'pool', 'sp')
    WINDOW = {'pe': 100, 'act': 160, 'dve': 160, 'pool': 1, 'sp': 1}
    PE_GROUPS = False
    PE_DECODE_INORDER = False
    CRIT = False
    DMA_OCC = 1.0
    SLACK = 0.0

    def __init__(self, nc, es, reorder=True):
        self.nc, self.es, self.reorder = nc, es, reorder
        self.ins = []
        self.lastw, self.readers, self.sems = {}, {}, {}

    def _add(self, eng, fn, reads, writes, sem, cost):
        i = len(self.ins)
        deps = set()
        for k in reads:
            if k in self.lastw:
                deps.add(self.lastw[k])
        for k in writes:
            if k in self.lastw:
                deps.add(self.lastw[k])
            deps.update(self.readers.get(k, ()))
        self.ins.append(dict(eng=eng, fn=fn, deps=deps, sem=sem, cost=cost, wkeys=tuple(writes)))
        for k in reads:
            self.readers.setdefault(k, set()).add(i)
        for k in writes:
            self.lastw[k] = i
            self.readers[k] = set()

    def op(self, eng, fn, reads=(), writes=()):
        self._add(eng, fn, reads, writes, None, _cost(eng, fn))

    def dma(self, eng, out, in_, reads=(), writes=(), sem=None, **kw):
        fn = (lambda e: e.dma_start(out=out, in_=in_, **kw))
        self._add(eng, fn, reads, writes, 'D_' + sem, _cost(eng, fn))

    def _schedule(self):
        ins = self.ins
        n = len(ins)
        if not self.reorder:
            return list(range(n))
        succ = [[] for _ in range(n)]
        for i, I in enumerate(ins):
            for d in I['deps']:
                succ[d].append(i)
        unit_of = [None] * n
        units = []
        per_eng = {e: [] for e in self.ENGS}
        last_pe = None
        for i, I in enumerate(ins):
            e = I['eng']
            if e == 'pe' and self.PE_GROUPS and last_pe is not None and units[last_pe]['wkeys'] == I['wkeys'] and len(units[last_pe]['members']) < 40:
                units[last_pe]['members'].append(i)
                unit_of[i] = last_pe
                continue
            u = len(units)
            units.append(dict(eng=e, members=[i], wkeys=I['wkeys'], pos=len(per_eng[e])))
            per_eng[e].append(u)
            unit_of[i] = u
            if e == 'pe':
                last_pe = u
        mark = getattr(self, 'mark', n) if self.PE_DECODE_INORDER else n
        ext = [0] * len(units)
        indeg = [0] * n
        for i, I in enumerate(ins):
            indeg[i] = len(I['deps'])
            for d in I['deps']:
                if unit_of[d] != unit_of[i]:
                    ext[unit_of[i]] += 1
        tail = [0.0] * n
        if self.CRIT:
            for i in range(n - 1, -1, -1):
                t = 0.0
                for s_ in succ[i]:
                    if tail[s_] > t:
                        t = tail[s_]
                tail[i] = t + ins[i]['cost'] + 0.1
        head = {e: 0 for e in self.ENGS}
        udone = [False] * len(units)
        finish = [0.0] * n
        rtime = [0.0] * n
        ready = {e: [] for e in self.ENGS}
        for u, U in enumerate(units):
            if ext[u] == 0:
                ready[U['eng']].append(u)
        free = {e: 0.0 for e in self.ENGS}
        order = []
        nsched = [0]
        open_unit = [None]

        def sched_ins(i, e, u):
            est = max(free[e], rtime[i])
            finish[i] = est + ins[i]['cost']
            free[e] = finish[i] if ins[i]['sem'] is None else est + 0.1 + (ins[i]['cost'] - 2.0) * self.DMA_OCC
            order.append(i)
            nsched[0] += 1
            for s_ in succ[i]:
                indeg[s_] -= 1
                su = unit_of[s_]
                lat = 0.05 if ins[s_]['eng'] == e else 0.2
                t = finish[i] + lat
                if t > rtime[s_]:
                    rtime[s_] = t
                if su != u:
                    ext[su] -= 1
                    if ext[su] == 0 and not udone[su]:
                        ready[units[su]['eng']].append(su)

        while nsched[0] < n:
            best = None
            for e in self.ENGS:
                pl = per_eng[e]
                h = head[e]
                while h < len(pl) and udone[pl[h]]:
                    h += 1
                head[e] = h
                if h >= len(pl):
                    continue
                if e == 'pe' and open_unit[0] is not None:
                    u, k = open_unit[0]
                    m = units[u]['members'][k]
                    if indeg[m] == 0:
                        cand = ((max(free[e], rtime[m]), -1), ('open', u, k))
                        if best is None or cand[0] < best[0]:
                            best = (cand[0], cand[1], e)
                    continue
                lim = h + self.WINDOW[e]
                if e == 'pe' and units[pl[h]]['members'][0] >= mark:
                    lim = h + 1
                cand = None
                for u in ready[e]:
                    U = units[u]
                    if U['pos'] >= lim or (e == 'pe' and U['members'][0] >= mark and U['pos'] != h):
                        continue
                    st, acc = 0.0, 0.0
                    for m in U['members']:
                        if rtime[m] - acc > st:
                            st = rtime[m] - acc
                        acc += ins[m]['cost']
                    est_ = max(free[e], st)
                    if self.CRIT:
                        key = (est_ if est_ > free[e] + self.SLACK else free[e], -tail[U['members'][0]], U['pos'])
                    else:
                        key = (est_, U['pos'])
                    if cand is None or key < cand[0]:
                        cand = (key, ('full', u, 0))
                if e == 'pe':
                    hu = pl[h]
                    if ext[hu] > 0:
                        m0 = units[hu]['members'][0]
                        if indeg[m0] == 0:
                            key = (max(free[e], rtime[m0]), units[hu]['pos'])
                            if cand is None or key < cand[0]:
                                cand = (key, ('head', hu, 0))
                if cand is not None and (best is None or cand[0] < best[0]):
                    best = (cand[0], cand[1], e)
            assert best is not None, "scheduler stuck"
            _, (kind, u, k), e = best
            if kind == 'full':
                ready[e].remove(u)
                udone[u] = True
                for i in units[u]['members']:
                    sched_ins(i, e, u)
            else:
                mem = units[u]['members']
                udone[u] = True
                if u in ready[e]:
                    ready[e].remove(u)
                sched_ins(mem[k], e, u)
                open_unit[0] = (u, k + 1) if k + 1 < len(mem) else None
        self.sim_time = max(finish)
        self.finish = finish
        return order

    def emit(self):
        nc = self.nc
        ins = self.ins
        order = self._schedule()
        stream = {e: [] for e in self.ENGS}
        cnt = {e: 0 for e in self.ENGS}
        dcnt = {}
        tok = [None] * len(ins)
        for i in order:
            I = ins[i]
            if I['sem'] is None:
                cnt[I['eng']] += 1
                tok[i] = ('E_' + I['eng'], cnt[I['eng']], 1)
            else:
                dcnt[I['sem']] = dcnt.get(I['sem'], 0) + 16
                tok[i] = (I['sem'], dcnt[I['sem']], 16)
        waited = {e: {} for e in self.ENGS}
        names = set()
        for i in order:
            I = ins[i]
            e = I['eng']
            deps = {}
            for d in I['deps']:
                s_, v, _ = tok[d]
                if deps.get(s_, 0) < v:
                    deps[s_] = v
            waits = []
            for s_, v in deps.items():
                if e == 'pe' and s_ == 'E_pe':
                    continue
                if waited[e].get(s_, 0) >= v:
                    continue
                waited[e][s_] = v
                waits.append((s_, v))
                names.add(s_)
            names.add(tok[i][0])
            stream[e].append((waits, I['fn'], (tok[i][0], tok[i][2])))
        for nme in sorted(names):
            self.sems[nme] = self.es.enter_context(nc.semaphore(nme))
        fin = list(dcnt.items())
        sems = self.sems

        def mk(engname, final):
            def body(e):
                for waits, fn, inc in stream[engname]:
                    for s_, v in waits:
                        e.wait_ge(sems[s_], v)
                    fn(e).then_inc(sems[inc[0]], inc[1])
                if final:
                    for nme, c in fin:
                        e.wait_ge(sems[nme], c)
            return body
        with nc.Block() as block:
            block.tensor(mk('pe', False))
            block.scalar(mk('act', False))
            block.vector(mk('dve', False))
            block.gpsimd(mk('pool', False))
            block.sync(mk('sp', True))


def build_program(do_decode=True):
    nc = bass.Bass("TRN2", target_bir_lowering=False)
    es = contextlib.ExitStack()
    S = Sched(nc, es)

    def din(name, shape):
        return nc.dram_tensor(name, list(shape), F32, kind="ExternalInput").ap()

    def dout(name, shape):
        return nc.dram_tensor(name, list(shape), F32, kind="ExternalOutput").ap()

    x_p = din("x_p", [SEQ, D])
    consts = din("consts", [128, NCONST])
    g_norm = din("g_norm", [DEPTH, D]); w_in = din("w_in", [DEPTH, D, DIN])
    conv_w = din("conv_w", [DEPTH, 4, D]); conv_b = din("conv_b", [DEPTH, D])
    w_rg = din("w_rgate", [DEPTH, 16, 64, 64]); b_rg = din("b_rgate", [DEPTH, D])
    w_ig = din("w_igate", [DEPTH, 16, 64, 64]); b_ig = din("b_igate", [DEPTH, D])
    lam = din("lru_lambda", [DEPTH, D]); b_mi = din("b_mi", [DEPTH, NH]); b_mf = din("b_mf", [DEPTH, NH])
    g_mh = din("g_mhead", [DEPTH, D]); w_out = din("w_out", [DEPTH, 2 * D, D]); g_fin = din("g_final", [D])
    y_p = dout("y_p", [SEQ, D])
    p_h = dout("p_h", [DEPTH, D]); p_conv = dout("p_conv", [DEPTH, 3, D])
    p_C = dout("p_C", [DEPTH, NH, DK, DK]); p_n = dout("p_n", [DEPTH, NH, DK]); p_m = dout("p_m", [DEPTH, NH])

    def sb(name, shape, dt=F32):
        return es.enter_context(nc.sbuf_tensor(name, list(shape), dt))

    S_ONES4 = 384
    cst = sb("cst", [128, 896])
    selr = sb("selr", [4, 512], F32R)
    ones_w = sb("ones_w", [128, 256], F32R)
    ones_r = ones_w[:, 0:128]
    prm = sb("prm", [128, KC, 32])
    nsp = sb("nsp", [128, DEPTH, KC])
    gcol = sb("gcol", [4, 8])
    wbd = sb("wbd", [128, 2 * DEPTH * KC, 128], BF16)
    wg = sb("wg", [128, DEPTH, KC, 8], BF16)
    xT = sb("xT", [128, KC, T])
    hn = sb("hn", [128, KC, T], BF16)
    mg = sb("mg", [128, 16, T], BF16)
    NS = 4
    wsl = [sb(f"wsl{i}", [128, KC * 512], BF16) for i in range(NS)]
    xin = sb("xin", [128, D])
    yout = sb("yout", [128, D])
    sq = [sb(f"sq{i}", [128, T], F32R) for i in range(2)]
    rsb = sb("rsb", [128, T])
    grow = [sb(f"grow{i}", [4, T], F32 if i < 3 else F32R) for i in range(6)]
    acol = sb("acol", [128, 16])
    NWK = 16
    work = sb("work", [128, NWK, T])
    workr = sb("workr", [128, 15, T], F32R)
    ext = [sb(f"ext{i}", [128, T + 4]) for i in range(2)]
    Cst = [sb(f"Cst{l}", [128, NH, 2, DK], F32R) for l in range(DEPTH)]
    nrep = [sb(f"nrep{l}", [128, NH, 2, 128], F32R) for l in range(DEPTH)]
    hst = sb("hst", [128, DEPTH, KC])
    convst = sb("convst", [128, DEPTH * KC, 3])
    mst = sb("mst", [4, DEPTH])
    dec = sb("dec", [128, 4])

    ps = [es.enter_context(nc.psum_tensor(f"ps{i}", [128, 512], F32)) for i in range(8)]
    psctr = [0]

    reserved = set()

    def bank():
        while psctr[0] % 8 in reserved:
            psctr[0] += 1
        i = psctr[0] % 8
        psctr[0] += 1
        if WHATIF.get('psum'):
            return ps[i], f"psv{psctr[0]}"
        return ps[i], f"ps{i}"

    ident = cst[:, C_ID:C_ID + 128]
    mask01 = cst[:, C_MASK:C_MASK + 128]
    ones4 = cst[0:4, S_ONES4:S_ONES4 + T]

    def wk(i):
        return work[:, i, :]

    def wkr(i):
        return workr[:, i, :]

    def wkrf(i):
        return workr[:].bitcast(F32)[:, i, :]

    S.dma('sp', cst[:, 0:384], consts[:, 0:384], writes=['cst'], sem='cst')
    S.dma('sp', cst[:, 384:896], consts[:, C_ONES4:C_ONES4 + 512], writes=['cst'], sem='cst')
    S.dma('sp', work[0:4, 2, :], consts[0:4, C_SEL:C_SEL + 512], writes=['wk2'], sem='selst')
    S.op('dve', lambda e: e.tensor_copy(out=selr[:], in_=work[0:4, 2, :]), reads=['wk2'], writes=['selr'])
    def x_load(tt):
        for j in range(4):
            r0 = tt * T + j * 128
            for half in range(2):
                key = f'xin{half}'
                S.dma('sp', xin[:, half * 512:(half + 1) * 512], x_p[r0:r0 + 128, half * 512:(half + 1) * 512], writes=[key], sem=key)
                pt, pk = bank()
                for q in range(4):
                    kc = half * 4 + q
                    S.op('pe', (lambda e, pt=pt, q=q, kc=kc: e.transpose(out=pt[:, q * 128:(q + 1) * 128], in_=xin[:, kc * 128:(kc + 1) * 128], identity=ident)),
                         reads=[key, 'cst'], writes=[pk])
                S.op('act', (lambda e, pt=pt, half=half, j=j: e.activation(out=xT[:, half * 4:half * 4 + 4, j * 128:(j + 1) * 128],
                                                                         in_=pt[:, :].rearrange("p (a b) -> p a b", a=4), func=AF.Copy)),
                     reads=[pk], writes=[f'x{half * 4 + q}' for q in range(4)])

    x_load(0)
    S.op('dve', lambda e: e.tensor_copy(out=ones_w[:, 0:128], in_=cst[:, C_ONE:C_ONE + 128]), reads=['cst'], writes=['ones_r'])
    S.op('dve', lambda e: e.tensor_copy(out=ones_w[:, 128:256], in_=cst[:, C_ONE:C_ONE + 128]), reads=['cst'], writes=['ones_r'])
    prow = work[0:32, 0:2, :]

    def prow_row(r):
        return work[r:r + 1, 0:2, :]
    S.op('dve', lambda e: e.memset(work[0:32, 0:2, :], 0.0), writes=['wk0', 'wk1'])
    rows = []
    for l in range(DEPTH):
        rows.append((10 * l + R_GN, g_norm[l:l + 1, :]))
        for j in range(4):
            rows.append((10 * l + R_CW + j, conv_w[l, j:j + 1, :]))
        rows.append((10 * l + R_CB, conv_b[l:l + 1, :]))
        rows.append((10 * l + R_BR, b_rg[l:l + 1, :]))
        rows.append((10 * l + R_BI, b_ig[l:l + 1, :]))
        rows.append((10 * l + R_LAM, lam[l:l + 1, :]))
        rows.append((10 * l + R_GMH, g_mh[l:l + 1, :]))
    rows.append((R_GF, g_fin.rearrange("(o d) -> o d", o=1)))
    for r, src in rows:
        S.dma('sp', work[r:r + 1, 0:2, :], src.rearrange("o (a b) -> o a b", a=2), reads=['wk0', 'wk1'], writes=[f'prow{r}'], sem='prow')
    for kc in range(KC):
        pt, pk = bank()
        a, b = divmod(kc * 128, 512)
        S.op('pe', (lambda e, pt=pt, a=a, b=b: e.transpose(out=pt[:, 0:32], in_=work[0:32, a, b:b + 128], identity=cst[0:32, C_ID:C_ID + 32])),
             reads=[f'prow{r}' for r, _ in rows] + ['cst', 'wk0', 'wk1'], writes=[pk])
        S.op('dve', (lambda e, pt=pt, kc=kc: e.tensor_copy(out=prm[:, kc, :], in_=pt[:, 0:32])), reads=[pk], writes=['prm'])
    for l in range(DEPTH):
        S.op('act', (lambda e, l=l: e.activation(out=nsp[:, l, :], in_=prm[:, :, 10 * l + R_LAM], func=AF.Exp, scale=-1.0)), reads=['prm'], writes=['nsp'])
    S.op('act', lambda e: e.activation(out=nsp[:], in_=nsp[:], func=AF.Ln, bias=1.0), reads=['nsp'], writes=['nsp'])
    S.op('dve', lambda e: e.tensor_scalar(out=nsp[:], in0=nsp[:], scalar1=-8.0, scalar2=None, op0=ALU.mult), reads=['nsp'], writes=['nsp'])
    S.dma('sp', gcol[:, 0:2], b_mi.rearrange("l h -> h l"), writes=['gcol'], sem='gcol', allow_slow_non_contiguous=True)
    S.dma('sp', gcol[:, 4:6], b_mf.rearrange("l h -> h l"), writes=['gcol'], sem='gcol', allow_slow_non_contiguous=True)
    S.op('dve', lambda e: e.tensor_scalar(out=gcol[:, 2:4], in0=gcol[:, 4:6], scalar1=-1.0, scalar2=None, op0=ALU.mult), reads=['gcol'], writes=['gcol'])
    S.op('pool', lambda e: e.memset(wbd[:], 0.0), writes=['wbd0'])
    for gi_, wsrc in enumerate((w_rg, w_ig)):
        for l in range(DEPTH):
            for half in range(2):
                src = wsrc[l].rearrange("(c two) i o -> two i c o", two=2)[half]
                base = (gi_ * DEPTH + l) * KC
                S.dma('pool', wbd[half * 64:(half + 1) * 64, base:base + KC, half * 64:(half + 1) * 64], src,
                      reads=['wbd0'], writes=[f'wbd_{gi_}{l}{half}'], sem='wbd')
    for l in range(DEPTH):
        S.dma('pool', wg[:, l, :, :], w_in[l].rearrange("(kc p) n -> p kc n", p=128)[:, :, 7168:7176], writes=['wg'], sem='wg')
    for l in range(DEPTH):
        for h in range(NH):
            S.op('dve', (lambda e, l=l, h=h: e.tensor_scalar(out=Cst[l][:, h, :, :], in0=cst[:, 0:512].rearrange("p (a b) -> p a b", a=2),
                                                           scalar1=0.0, scalar2=None, op0=ALU.mult)), reads=['cst'], writes=[f'C{l}'])
            S.op('dve', (lambda e, l=l, h=h: e.tensor_scalar(out=nrep[l][:, h, :, :], in0=cst[:, 0:256].rearrange("p (a b) -> p a b", a=2),
                                                           scalar1=0.0, scalar2=None, op0=ALU.mult)), reads=['cst'], writes=[f'n{l}'])
    S.op('dve', lambda e: e.memset(hst[:], 0.0), writes=['hst0', 'hst1'])
    S.op('dve', lambda e: e.memset(convst[:], 0.0), writes=['cv0', 'cv1'])
    S.op('dve', lambda e: e.memset(mst[:], 0.0), writes=['mst0', 'mst1'])

    wctr = [0]

    wscr = [nc.dram_tensor(f"wscr{l}", [28, 128, 4096], BF16, kind="Internal").ap() for l in range(DEPTH)]
    seen = {}

    def load_w(src, view):
        src_ap, gid = src
        i = wctr[0] % NS
        wctr[0] += 1
        a, b = view
        flat = wsl[i][:, 0:a * b]
        dst = flat.rearrange("p (a b) -> p a b", a=a)
        l = int(gid.split('_')[0])
        if (not USE_SCRATCH) or gid not in seen:
            S.dma('pool', dst, src_ap, writes=[f'W{i}', f'W{i}b'], sem=f'W{i}')
            if USE_SCRATCH:
                g = len([k for k in seen if k.startswith(f'{l}_')])
                seen[gid] = g
                S.dma('sp', wscr[l][g, :, 0:a * b], flat, reads=[f'W{i}'], writes=[f'scr{gid}'], sem=f'scrst{i}')
        else:
            g = seen[gid]
            S.dma('pool', flat, wscr[l][g, :, 0:a * b], reads=[f'scr{gid}'], writes=[f'W{i}', f'W{i}b'], sem=f'W{i}')
        return dst, f'W{i}'

    def load_w2(srcA, srcB):
        (apA, gidA), (apB, gidB) = srcA, srcB
        if not (USE_SCRATCH and PAIR_LOADS):
            return load_w(srcA, (KC, 256)), load_w(srcB, (KC, 256))
        l = int(gidA.split('_')[0])
        gid = gidA + '+' + gidB
        v3 = lambda ap_: ap_.rearrange("p (a b) -> p a b", a=KC)
        if gid not in seen:
            g = len([k for k in seen if k.startswith(f'{l}_')])
            seen[gid] = g
            outs = []
            for half, ap_ in enumerate((apA, apB)):
                i = wctr[0] % NS
                wctr[0] += 1
                flat = wsl[i][:, 0:2048]
                S.dma('pool', v3(flat), ap_, writes=[f'W{i}', f'W{i}b'], sem=f'W{i}')
                S.dma('sp', wscr[l][g, :, half * 2048:(half + 1) * 2048], flat, reads=[f'W{i}'], writes=[f'scr{gid}_{half}'], sem=f'scrst{i}')
                outs.append((v3(flat), f'W{i}'))
            return outs[0], outs[1]
        g = seen[gid]
        i = wctr[0] % NS
        wctr[0] += 1
        S.dma('pool', wsl[i][:, 0:4096], wscr[l][g, :, 0:4096], reads=[f'scr{gid}_0', f'scr{gid}_1'], writes=[f'W{i}', f'W{i}b'], sem=f'W{i}')
        return (v3(wsl[i][:, 0:2048]), f'W{i}'), (v3(wsl[i][:, 2048:4096]), f'W{i}b')

    def win_cols(l, c0, n):
        return w_in[l].rearrange("(kc p) n -> p kc n", p=128)[:, :, c0:c0 + n], f'{l}_in_{c0}'

    def wout_cols(l, c0, n):
        return w_out[l].rearrange("(kc p) n -> p kc n", p=128)[:, :, c0:c0 + n], f'{l}_out_{c0}'

    def act_sig(opf, dst, src, reads, dkey, nbias=None):
        if nbias is None:
            opf('act', (lambda e: e.activation(out=dst, in_=src, func=AF.Exp, scale=-1.0)), reads=list(reads), writes=[dkey])
        else:
            opf('act', (lambda e: e.activation(out=dst, in_=src, func=AF.Exp, scale=-1.0, bias=nbias)), reads=list(reads) + ['nprm'], writes=[dkey])
        opf('act', (lambda e: e.activation(out=dst, in_=dst, func=AF.Ln, bias=1.0)), reads=[dkey], writes=[dkey])
        opf('act', (lambda e: e.activation(out=dst, in_=dst, func=AF.Exp, scale=-1.0)), reads=[dkey], writes=[dkey])

    def act_rpow(opf, dst, src, reads, dkey, p, scale=1.0, bias=None):
        if bias is None:
            opf('act', (lambda e: e.activation(out=dst, in_=src, func=AF.Ln, scale=scale)), reads=list(reads), writes=[dkey])
        else:
            opf('act', (lambda e: e.activation(out=dst, in_=src, func=AF.Ln, scale=scale, bias=bias)), reads=list(reads), writes=[dkey])
        opf('act', (lambda e: e.activation(out=dst, in_=dst, func=AF.Exp, scale=p)), reads=[dkey], writes=[dkey])

    def rmsnorm(src_key_fn, gcol_fn, dst_fn, dst_key_fn, ntok, src_fn=None, extra=()):
        if src_fn is None:
            src_fn = lambda kc: xT[:, kc, 0:ntok]
        extra = list(extra)
        pt, pk = bank()
        for kc in range(KC):
            s_ = sq[kc % 2]
            S.op('act', (lambda e, s_=s_, kc=kc: e.activation(out=s_[:, 0:ntok], in_=src_fn(kc), func=AF.Square)),
                 reads=[src_key_fn(kc)] + extra, writes=[f'sq{kc % 2}'])
            S.op('pe', (lambda e, s_=s_, kc=kc, pt=pt: e.matmul(pt[:, 0:ntok], ones_r, s_[:, 0:ntok], start=(kc == 0), stop=(kc == KC - 1))),
                 reads=[f'sq{kc % 2}', 'ones_r'] + extra, writes=[pk])
        act_rpow(S.op, rsb[:, 0:ntok], pt[:, 0:ntok], [pk, 'epsc'] + extra, 'rsb', -0.5, scale=1.0 / D, bias=epsc[:, 0:1])
        for kc in range(KC):
            S.op('dve', (lambda e, kc=kc: e.scalar_tensor_tensor(out=dst_fn(kc), in0=src_fn(kc), scalar=gcol_fn(kc),
                                                               in1=rsb[:, 0:ntok], op0=ALU.mult, op1=ALU.mult)),
                 reads=[src_key_fn(kc), 'rsb', 'prm'], writes=[dst_key_fn(kc)])

    epsc = sb("epsc", [128, 1])
    S.op('dve', lambda e: e.memset(epsc[:], EPS), writes=['epsc'])
    onep = sb("onep", [128, 1])
    S.op('dve', lambda e: e.memset(onep[:], 1.0), writes=['onep'])
    nprm = sb("nprm", [128, KC, 32])
    S.op('dve', lambda e: e.tensor_scalar(out=nprm[:], in0=prm[:], scalar1=-1.0, scalar2=None, op0=ALU.mult), reads=['prm'], writes=['nprm'])

    def proj_fm(wv, wkey, j, ntok, ncols=128, hn_fn=None, hnk_fn=None):
        if hn_fn is None:
            hn_fn = lambda kc: hn[:, kc, 0:ntok]
            hnk_fn = lambda kc: f'hn{kc}'
        pt, pk = bank()
        for kc in range(KC):
            S.op('pe', (lambda e, kc=kc, pt=pt: e.matmul(pt[0:ncols, 0:ntok], wv[:, kc, j * 128:j * 128 + ncols], hn_fn(kc),
                                                      start=(kc == 0), stop=(kc == KC - 1))),
                 reads=[wkey, hnk_fn(kc)], writes=[pk])
        return pt, pk

    def rglru_chunk(l, c, j, wa, wak, wz, wzk):
        st = c % 2
        b0 = 7 * st
        xc, xcb, r_, i_, m_, h_, sz = (wk(b0 + 0), work[:, b0 + 1, :].bitcast(BF16)[:, 0:T], wk(b0 + 2), wk(b0 + 3), wk(b0 + 4),
                                       wk(b0 + 5), wk(b0 + 6))
        a_, u_ = r_, i_
        KM = {0: 0, 1: 1, 2: 2, 3: 3, 4: 2, 5: 4, 6: 3, 7: 5, 8: 6}
        K = lambda n: (WHATIF.get('rg', 'wk') + f'{b0 + KM[n]}')
        ex = ext[st]
        exk = f'ext{st}'
        cvk = f'cv{l}'
        pX, pXk = proj_fm(wa, wak, j, T)
        pZ, pZk = proj_fm(wz, wzk, j, T)
        S.op('dve', (lambda e, ex=ex, c=c: e.tensor_copy(out=ex[:, 0:3], in_=convst[:, l * KC + c, :])), reads=[cvk], writes=[exk])
        S.op('act', (lambda e, ex=ex, pX=pX: e.activation(out=ex[:, 3:T + 3], in_=pX[:, :], func=AF.Copy)), reads=[pXk], writes=[exk])
        act_sig(S.op, sz, pZ[:, :], [pZk], K(8))
        S.op('dve', (lambda e, sz=sz, pZ=pZ: e.tensor_tensor(out=sz, in0=sz, in1=pZ[:, :], op=ALU.mult)), reads=[pZk, K(8)], writes=[K(8)])
        cw = lambda jj, c=c: prm[:, c, 10 * l + R_CW + jj:10 * l + R_CW + jj + 1]
        S.op('dve', (lambda e, ex=ex, xc=xc, cw=cw, c=c: e.tensor_scalar(out=xc, in0=ex[:, 0:T], scalar1=cw(0),
                                                                        scalar2=prm[:, c, 10 * l + R_CB:10 * l + R_CB + 1], op0=ALU.mult, op1=ALU.add)),
             reads=[exk, 'prm'], writes=[K(0)])
        for jj in range(1, 4):
            S.op('dve', (lambda e, ex=ex, xc=xc, cw=cw, jj=jj: e.scalar_tensor_tensor(out=xc, in0=ex[:, jj:jj + T], scalar=cw(jj), in1=xc,
                                                                                     op0=ALU.mult, op1=ALU.add)),
                 reads=[exk, 'prm', K(0)], writes=[K(0)])
        S.op('dve', (lambda e, ex=ex, c=c: e.tensor_copy(out=convst[:, l * KC + c, :], in_=ex[:, T:T + 3])), reads=[exk], writes=[cvk])
        S.op(WHATIF.get('xcb_eng', 'dve'), ((lambda e, xc=xc, xcb=xcb: e.activation(out=xcb, in_=xc, func=AF.Copy)) if WHATIF.get('xcb_eng', 'dve') == 'act' else (lambda e, xc=xc, xcb=xcb: e.tensor_copy(out=xcb, in_=xc))), reads=[K(0)], writes=[K(1)])
        pR, pRk = bank()
        S.op('pe', (lambda e, pR=pR, xcb=xcb, c=c: e.matmul(pR[:, :], wbd[:, (0 * DEPTH + l) * KC + c, :], xcb, start=True, stop=True)),
             reads=['wbd0'] + [f'wbd_{a_}{b_}{c_}' for a_ in range(2) for b_ in range(2) for c_ in range(2)] + [K(1)], writes=[pRk])
        pI, pIk = bank()
        S.op('pe', (lambda e, pI=pI, xcb=xcb, c=c: e.matmul(pI[:, :], wbd[:, (1 * DEPTH + l) * KC + c, :], xcb, start=True, stop=True)),
             reads=['wbd0'] + [f'wbd_{a_}{b_}{c_}' for a_ in range(2) for b_ in range(2) for c_ in range(2)] + [K(1)], writes=[pIk])
        act_sig(S.op, r_, pR[:, :], [pRk], K(2), nbias=nprm[:, c, 10 * l + R_BR:10 * l + R_BR + 1])
        act_sig(S.op, i_, pI[:, :], [pIk], K(3), nbias=nprm[:, c, 10 * l + R_BI:10 * l + R_BI + 1])
        S.op('act', (lambda e, a_=a_, r_=r_, c=c: e.activation(out=a_, in_=r_, func=AF.Exp, scale=nsp[:, l, c:c + 1])), reads=[K(2), 'nsp'], writes=[K(4)])
        S.op(WHATIF.get('sq_eng', 'dve'), ((lambda e, a_=a_, m_=m_: e.activation(out=m_, in_=a_, func=AF.Square)) if WHATIF.get('sq_eng', 'dve') == 'act' else (lambda e, a_=a_, m_=m_: e.tensor_tensor(out=m_, in0=a_, in1=a_, op=ALU.mult))), reads=[K(4)], writes=[K(5)])
        act_rpow(S.op, m_, m_, [K(5), 'onep'], K(5), 0.5, scale=-1.0, bias=onep[:, 0:1])
        S.op('dve', (lambda e, u_=u_, i_=i_, xc=xc: e.tensor_tensor(out=u_, in0=i_, in1=xc, op=ALU.mult)), reads=[K(3), K(0)], writes=[K(6)])
        S.op('dve', (lambda e, u_=u_, m_=m_: e.tensor_tensor(out=u_, in0=u_, in1=m_, op=ALU.mult)), reads=[K(6), K(5)], writes=[K(6)])
        S.op('dve', (lambda e, h_=h_, a_=a_, u_=u_, c=c: e.tensor_tensor_scan(out=h_, data0=a_, data1=u_, initial=hst[:, l, c:c + 1],
                                                                            op0=ALU.mult, op1=ALU.add)),
             reads=[K(4), K(6), f'hst{l}'], writes=[K(7)])
        S.op('dve', (lambda e, h_=h_, c=c: e.tensor_copy(out=hst[:, l, c:c + 1], in_=h_[:, T - 1:T])), reads=[K(7)], writes=[f'hst{l}'])
        S.op('dve', (lambda e, h_=h_, sz=sz, c=c: e.tensor_tensor(out=mg[:, c, :], in0=h_, in1=sz, op=ALU.mult)), reads=[K(7), K(8)], writes=[f'mg{c}'])


    def mlstm_head(l, h):
        ig, l1, bcs, Mx, gi, en = [g_[:] for g_ in grow]
        LM = {0: 'R0', 1: 'R1', 2: 'R2', 3: 'R3', 4: 'R4', 5: 'R5', 6: 'F0', 7: 'F1', 8: 'F2', 9: 'F3', 10: 'F4', 11: 'F5', 12: 'F6',
              13: 'F7', 14: 'F8', 15: 'F9', 16: 'R9', 17: 'R10', 18: 'R11', 19: 'R12', 20: 'F10', 21: 'F11', 22: 'F12', 23: 'R13', 24: 'R14',
              25: 'F13', 26: 'R6', 27: 'R7', 28: 'R8'}
        W = lambda n: ('wk' if LM[n][0] == 'F' else 'wr') + LM[n][1:] + (f'_h{h % 2}' if (WHATIF.get('mlproj') and n < 10) or WHATIF.get('mlall') else '')

        def B(n, f32=False):
            i = int(LM[n][1:])
            if LM[n][0] == 'F':
                return work[:, i, :]
            return wkrf(i) if f32 else workr[:, i, :]
        qT = lambda dkc: B(0 + dkc)
        kT = lambda dkc: B(2 + dkc)
        vtok = lambda j: B(4 + j // 2)[:, (j % 2) * 256:(j % 2) * 256 + 256]
        so = lambda dvc: B(6 + dvc)
        szb = lambda dvc: B(8 + dvc)
        Mb, gib, enb = B(10), B(11), B(12)
        wTb = [(13, 0), (14, 0), (15, 0), (15, 256)]
        PTb = [(26, 0), (27, 0), (28, 0), (28, 256)]
        wT = lambda j: B(wTb[j][0])[:, wTb[j][1]:wTb[j][1] + (T - 128 * j)]
        PT = lambda j: B(PTb[j][0])[:, PTb[j][1]:PTb[j][1] + (T - 128 * j)]
        qg = lambda dkc: B(16 + dkc)
        kd = lambda j: B(18 + j // 2)[:, (j % 2) * 256:(j % 2) * 256 + 256]
        rden = B(20)
        yB = lambda dvc: B(21 + dvc)
        ysq = lambda dvc: B(23 + dvc)
        rsh = B(25)

        (wq, wqk), (wkk_, wkk) = load_w2(win_cols(l, 2048 + 256 * h, 256), win_cols(l, 3072 + 256 * h, 256))
        for dkc in range(2):
            pt, pk = proj_fm(wq, wqk, dkc, T)
            S.op('dve', (lambda e, pt=pt, dkc=dkc: e.tensor_copy(out=qT(dkc), in_=pt[:, :])), reads=[pk], writes=[W(0 + dkc)])
        for dkc in range(2):
            pt, pk = proj_fm(wkk_, wkk, dkc, T)
            S.op('act', (lambda e, pt=pt, dkc=dkc: e.activation(out=kT(dkc), in_=pt[:, :], func=AF.Copy, scale=DK ** -0.5)), reads=[pk], writes=[W(2 + dkc)])
        (wv_, wvk), (wo_, wok) = load_w2(win_cols(l, 4096 + 256 * h, 256), win_cols(l, 5120 + 256 * h, 256))
        for j in range(4):
            pt, pk = bank()
            for kc in range(KC):
                S.op('pe', (lambda e, kc=kc, pt=pt, j=j: e.matmul(pt[:, 0:256], hn[:, kc, j * 128:(j + 1) * 128], wv_[:, kc, :],
                                                               start=(kc == 0), stop=(kc == KC - 1))),
                     reads=[wvk, f'hn{kc}'], writes=[pk])
            S.op('act', (lambda e, pt=pt, j=j: e.activation(out=vtok(j), in_=pt[:, 0:256], func=AF.Copy)), reads=[pk], writes=[W(4 + j // 2)])
        wzb_, wzbk = load_w(win_cols(l, 6144 + 256 * h, 256), (KC, 256))
        for dvc in range(2):
            pt, pk = proj_fm(wo_, wok, dvc, T)
            act_sig(S.op, so(dvc), pt[:, :], [pk], W(6 + dvc))
        for dvc in range(2):
            pt, pk = proj_fm(wzb_, wzbk, dvc, T)
            act_sig(S.op, szb(dvc), pt[:, :], [pk], W(8 + dvc))
            S.op('dve', (lambda e, pt=pt, dvc=dvc: e.tensor_tensor(out=szb(dvc), in0=szb(dvc), in1=pt[:, :], op=ALU.mult)), reads=[pk, W(8 + dvc)], writes=[W(8 + dvc)])
        sel = selr[:, 128 * h:128 * h + 128]
        for src, sk, dst, dkey in ((Mx, 'g3', Mb, W(10)), (gi, 'g4', gib, W(11)), (en, 'g5', enb, W(12))):
            pt, pk = bank()
            S.op('pe', (lambda e, pt=pt, src=src: e.matmul(pt[:, :], sel, src, start=True, stop=True)), reads=[sk, 'selr'], writes=[pk])
            S.op('dve', (lambda e, pt=pt, dst=dst: e.tensor_copy(out=dst, in_=pt[:, :])), reads=[pk], writes=[dkey])
        psS = []
        for j in range(4):
            Nj = T - 128 * j
            t0 = 128 * j
            pt, pk = bank()
            for dkc in range(2):
                S.op('pe', (lambda e, pt=pt, dkc=dkc, t0=t0, Nj=Nj: e.matmul(pt[:, 0:Nj], kT(dkc)[:, t0:t0 + 128], qT(dkc)[:, t0:T],
                                                                           start=(dkc == 0), stop=(dkc == 1))),
                     reads=[W(0 + dkc), W(2 + dkc)], writes=[pk])
            wkey = W(wTb[j][0])
            S.op('act', (lambda e, j=j, t0=t0: e.activation(out=wT(j), in_=Mb[:, t0:T], func=AF.Exp, scale=-1.0, bias=acol[:, 4 * j + h:4 * j + h + 1])),
                 reads=[W(10), 'acol'], writes=[wkey])
            S.op('dve', (lambda e, j=j: e.tensor_tensor(out=wT(j)[:, 0:128], in0=wT(j)[:, 0:128], in1=mask01, op=ALU.mult)),
                 reads=[wkey, 'cst'], writes=[wkey])
            S.op('dve', (lambda e, j=j, Nj=Nj: e.tensor_copy(out=dec[:, j:j + 1], in_=wT(j)[:, Nj - 1:Nj])), reads=[wkey], writes=['dec'])
            S.op('dve', (lambda e, j=j, pt=pt, Nj=Nj: e.tensor_tensor(out=PT(j), in0=pt[:, 0:Nj], in1=wT(j), op=ALU.mult)),
                 reads=[pk, wkey], writes=[W(PTb[j][0])])
        for dkc in range(2):
            S.op('dve', (lambda e, dkc=dkc: e.tensor_tensor(out=qg(dkc), in0=B(0 + dkc, True), in1=gib, op=ALU.mult)),
                 reads=[W(0 + dkc), W(11)], writes=[W(16 + dkc)])
        Ck, nk = f'C{l}', f'n{l}'
        pN = []
        for dvc in range(2):
            pt, pk = bank()
            pN.append((pt, pk))
            for dkc in range(2):
                S.op('pe', (lambda e, pt=pt, dkc=dkc, dvc=dvc: e.matmul(pt[:, :], Cst[l][:, h, dkc, dvc * 128:(dvc + 1) * 128], qg(dkc),
                                                                      start=(dkc == 0), stop=False)),
                     reads=[Ck, W(16 + dkc)], writes=[pk])
            for j in range(4):
                S.op('pe', (lambda e, pt=pt, j=j, dvc=dvc: e.matmul(pt[:, 128 * j:T], vtok(j)[:, dvc * 128:(dvc + 1) * 128], PT(j),
                                                                  start=False, stop=(j == 3))),
                     reads=[W(4 + j // 2), W(PTb[j][0])], writes=[pk])
        pD, pDk = bank()
        for dkc in range(2):
            S.op('pe', (lambda e, dkc=dkc: e.matmul(pD[:, :], nrep[l][:, h, dkc, :], qg(dkc), start=(dkc == 0), stop=False)),
                 reads=[nk, W(16 + dkc)], writes=[pDk])
        for j in range(4):
            S.op('pe', (lambda e, j=j: e.matmul(pD[:, 128 * j:T], ones_r, PT(j), start=False, stop=(j == 3))),
                 reads=['ones_r', W(PTb[j][0])], writes=[pDk])
        S.op('dve', lambda e: e.tensor_tensor(out=rden, in0=pD[:, :], in1=enb, op=ALU.max), reads=[pDk, W(12)], writes=[W(20)])
        S.op('dve', lambda e: e.scalar_tensor_tensor(out=rden, in0=pD[:, :], scalar=-1.0, in1=rden, op0=ALU.mult, op1=ALU.max), reads=[pDk, W(20)], writes=[W(20)])
        act_rpow(S.op, rden, rden, [W(20)], W(20), -1.0)
        for dvc in range(2):
            pt, pk = pN[dvc]
            S.op('dve', (lambda e, pt=pt, dvc=dvc: e.tensor_tensor(out=yB(dvc), in0=pt[:, :], in1=rden, op=ALU.mult)), reads=[pk, W(20)], writes=[W(21 + dvc)])
            S.op('dve', (lambda e, dvc=dvc: e.tensor_tensor(out=yB(dvc), in0=yB(dvc), in1=so(dvc), op=ALU.mult)), reads=[W(21 + dvc), W(6 + dvc)], writes=[W(21 + dvc)])
            S.op('act', (lambda e, dvc=dvc: e.activation(out=ysq(dvc), in_=yB(dvc), func=AF.Square)), reads=[W(21 + dvc)], writes=[W(23 + dvc)])
        pQ, pQk = bank()
        for dvc in range(2):
            S.op('pe', (lambda e, dvc=dvc: e.matmul(pQ[:, :], ones_r, ysq(dvc), start=(dvc == 0), stop=(dvc == 1))), reads=['ones_r', W(23 + dvc)], writes=[pQk])
        act_rpow(S.op, rsh, pQ[:, :], [pQk, 'epsc'], W(25), -0.5, scale=1.0 / DK, bias=epsc[:, 0:1])
        for dvc in range(2):
            fc = 2 * h + dvc
            S.op('dve', (lambda e, dvc=dvc, fc=fc: e.scalar_tensor_tensor(out=yB(dvc), in0=yB(dvc), scalar=prm[:, fc, 10 * l + R_GMH:10 * l + R_GMH + 1],
                                                                        in1=rsh, op0=ALU.mult, op1=ALU.mult)),
                 reads=[W(21 + dvc), W(25), 'prm'], writes=[W(21 + dvc)])
            S.op('dve', (lambda e, dvc=dvc, fc=fc: e.tensor_tensor(out=mg[:, 8 + fc, :], in0=yB(dvc), in1=szb(dvc), op=ALU.mult)),
                 reads=[W(21 + dvc), W(8 + dvc)], writes=[f'mg{8 + fc}'])
        for j in range(4):
            pt, pk = bank()
            for dkc in range(2):
                S.op('pe', (lambda e, pt=pt, j=j, dkc=dkc: e.transpose(out=pt[:, dkc * 128:(dkc + 1) * 128], in_=B(2 + dkc, True)[:, j * 128:(j + 1) * 128],
                                                                     identity=ident)),
                     reads=[W(2 + dkc), 'cst'], writes=[pk])
            S.op('dve', (lambda e, pt=pt, j=j: e.tensor_scalar(out=kd(j), in0=pt[:, 0:256], scalar1=dec[:, j:j + 1], scalar2=None, op0=ALU.mult)),
                 reads=[pk, 'dec'], writes=[W(18 + j // 2)])
        pC, pCk = bank()
        for dkc in range(2):
            for j in range(4):
                S.op('pe', (lambda e, j=j, dkc=dkc: e.matmul(pC[:, dkc * 256:(dkc + 1) * 256], kd(j)[:, dkc * 128:(dkc + 1) * 128], vtok(j),
                                                           start=(j == 0), stop=(j == 3))),
                     reads=[W(18 + j // 2), W(4 + j // 2)], writes=[pCk])
        pNn, pNnk = bank()
        for dkc in range(2):
            for j in range(4):
                S.op('pe', (lambda e, j=j, dkc=dkc: e.matmul(pNn[:, dkc * 256:(dkc + 1) * 256], kd(j)[:, dkc * 128:(dkc + 1) * 128], ones_w[:, :],
                                                           start=(j == 0), stop=(j == 3))),
                     reads=[W(18 + j // 2), 'ones_r'], writes=[pNnk])
        for dkc in range(2):
            S.op('dve', (lambda e, dkc=dkc: e.scalar_tensor_tensor(out=Cst[l][:, h, dkc, :], in0=Cst[l][:].bitcast(F32)[:, h, dkc, :], scalar=gib[:, T - 1:T],
                                                                  in1=pC[:, dkc * 256:(dkc + 1) * 256], op0=ALU.mult, op1=ALU.add)),
                 reads=[Ck, W(11), pCk], writes=[Ck])
            S.op('dve', (lambda e, dkc=dkc: e.scalar_tensor_tensor(out=nrep[l][:, h, dkc, :], in0=nrep[l][:].bitcast(F32)[:, h, dkc, :], scalar=gib[:, T - 1:T],
                                                                  in1=pNn[:, dkc * 256:dkc * 256 + 128], op0=ALU.mult, op1=ALU.add)),
                 reads=[nk, W(11), pNnk], writes=[nk])


    def layer_prompt(l, tt):
        xk = lambda kc: f'x{kc}'
        rmsnorm(xk, lambda kc: prm[:, kc, 10 * l + R_GN:10 * l + R_GN + 1], lambda kc: hn[:, kc, :], lambda kc: f'hn{kc}', T)
        ig, l1, bcs, Mx, gi, en = [g[:] for g in grow]
        pgi, pgik = bank()
        for kc in range(KC):
            S.op('pe', (lambda e, kc=kc: e.matmul(pgi[0:4, :], wg[:, l, kc, 0:4], hn[:, kc, :], start=(kc == 0), stop=(kc == KC - 1))),
                 reads=['wg', f'hn{kc}'], writes=[pgik])
        pgf, pgfk = bank()
        for kc in range(KC):
            S.op('pe', (lambda e, kc=kc: e.matmul(pgf[0:4, :], wg[:, l, kc, 4:8], hn[:, kc, :], start=(kc == 0), stop=(kc == KC - 1))),
                 reads=['wg', f'hn{kc}'], writes=[pgfk])
        S.op('act', lambda e: e.activation(out=ig, in_=pgi[0:4, :], func=AF.Identity, bias=gcol[:, l:l + 1]), reads=[pgik, 'gcol'], writes=['g0'])
        S.op('act', lambda e: e.activation(out=l1, in_=pgf[0:4, :], func=AF.Exp, scale=-1.0, bias=gcol[:, 2 + l:3 + l]), reads=[pgfk, 'gcol'], writes=['g1'])
        S.op('act', lambda e: e.activation(out=l1, in_=l1, func=AF.Ln, bias=1.0), reads=['g1'], writes=['g1'])
        S.op('dve', lambda e: e.tensor_tensor_scan(out=bcs, data0=ones4, data1=l1, initial=0.0, op0=ALU.mult, op1=ALU.subtract),
             reads=['g1', 'cst'], writes=['g2'])
        S.op('dve', lambda e: e.tensor_tensor(out=ig, in0=ig, in1=bcs, op=ALU.subtract), reads=['g0', 'g2'], writes=['g0'])
        S.op('dve', lambda e: e.tensor_tensor_scan(out=Mx, data0=ig, data1=ig, initial=mst[:, l:l + 1], op0=ALU.max, op1=ALU.max),
             reads=['g0', f'mst{l}'], writes=['g3'])
        MxF = grow[3][:].bitcast(F32)
        S.op('act', lambda e: e.activation(out=gi, in_=MxF, func=AF.Exp, scale=-1.0, bias=mst[:, l:l + 1]), reads=['g3', f'mst{l}'], writes=['g4'])
        S.op('dve', lambda e: e.tensor_tensor(out=bcs, in0=bcs, in1=MxF, op=ALU.add), reads=['g2', 'g3'], writes=['g2'])
        S.op('act', lambda e: e.activation(out=en, in_=bcs, func=AF.Exp, scale=-1.0), reads=['g2'], writes=['g5'])
        S.op('dve', lambda e: e.tensor_copy(out=mst[:, l:l + 1], in_=bcs[:, T - 1:T]), reads=['g2'], writes=[f'mst{l}'])
        pa, pak = bank()
        for j in range(4):
            S.op('pe', (lambda e, j=j: e.transpose(out=pa[:, 4 * j:4 * j + 4], in_=ig[:, j * 128:(j + 1) * 128], identity=cst[0:4, C_ID:C_ID + 4])),
                 reads=['g0', 'cst'], writes=[pak])
        S.op('dve', lambda e: e.tensor_copy(out=acol[:], in_=pa[:, 0:16]), reads=[pak], writes=['acol'])

        for g in range(2):
            wa, wak = load_w(win_cols(l, 512 * g, 512), (KC, 512))
            wz, wzk = load_w(win_cols(l, 1024 + 512 * g, 512), (KC, 512))
            for j in range(4):
                rglru_chunk(l, 4 * g + j, j, wa, wak, wz, wzk)
        for h in range(NH):
            mlstm_head(l, h)
        for g4 in range(4):
            wo4, wo4k = load_w(wout_cols(l, 256 * g4, 256), (16, 256))
            for dd in range(2):
                dc = 2 * g4 + dd
                pt, pk = bank()
                for kc in range(16):
                    S.op('pe', (lambda e, pt=pt, kc=kc, dd=dd, wo4=wo4: e.matmul(pt[:, :], wo4[:, kc, dd * 128:(dd + 1) * 128], mg[:, kc, :],
                                                                      start=(kc == 0), stop=(kc == 15))),
                         reads=[wo4k, f'mg{kc}'], writes=[pk])
                S.op('dve', (lambda e, pt=pt, dc=dc: e.tensor_tensor(out=xT[:, dc, :], in0=xT[:, dc, :], in1=pt[:, :], op=ALU.add)),
                     reads=[pk, f'x{dc}'], writes=[f'x{dc}'])

    def y_store(tt):
        for j in range(4):
            r0 = tt * T + j * 128
            for half in range(2):
                key = f'yout{half}'
                pt, pk = bank()
                for q in range(4):
                    kc = half * 4 + q
                    S.op('pe', (lambda e, pt=pt, q=q, kc=kc, j=j: e.transpose(out=pt[:, q * 128:(q + 1) * 128], in_=wk(kc)[:, j * 128:(j + 1) * 128], identity=ident)),
                         reads=[f'wk{kc}', 'cst'], writes=[pk])
                S.op('act', (lambda e, pt=pt, half=half: e.activation(out=yout[:, half * 512:(half + 1) * 512], in_=pt[:, :], func=AF.Copy)),
                     reads=[pk], writes=[key])
                S.dma('act', y_p[r0:r0 + 128, half * 512:(half + 1) * 512], yout[:, half * 512:(half + 1) * 512], reads=[key], sem=key)

    for tt in range(NT):
        for l in range(DEPTH):
            layer_prompt(l, tt)
        rmsnorm(lambda kc: f'x{kc}', lambda kc: prm[:, kc, R_GF:R_GF + 1], lambda kc: wk(kc), lambda kc: f'wk{kc}', T)
        if tt + 1 < NT:
            x_load(tt + 1)
        y_store(tt)

    for l in range(DEPTH):
        S.dma('act', p_h[l].rearrange("(kc p) -> p kc", p=128), hst[:, l, :], reads=[f'hst{l}'], sem='ost', allow_slow_non_contiguous=True)
        for j in range(3):
            S.dma('act', p_conv[l, j].rearrange("(kc p) -> p kc", p=128), convst[:, l * KC:(l + 1) * KC, j], reads=[f'cv{l}'], sem='ost', allow_slow_non_contiguous=True)
        S.dma('act', p_C[l].rearrange("h (dkc p) v -> p h dkc v", p=128), Cst[l][:].bitcast(F32), reads=[f'C{l}'], sem='ost')
        S.dma('act', p_n[l].rearrange("h (dkc p) -> p h dkc", p=128), nrep[l][:].bitcast(F32)[:, :, :, 0], reads=[f'n{l}'], sem='ost', allow_slow_non_contiguous=True)
        S.dma('act', p_m[l].rearrange("(h o) -> h o", o=1), mst[:, l:l + 1], reads=[f'mst{l}'], sem='ost', allow_slow_non_contiguous=True)

    x_s = din("x_s", [DB, D]); st_h = din("st_h", [DEPTH, DB, D]); st_conv = din("st_conv", [DEPTH, DB, 3, D])
    st_C = din("st_C", [DEPTH, DB, NH, DK, DK]); st_n = din("st_n", [DEPTH, DB, NH, DK]); st_m = din("st_m", [DEPTH, DB, NH])
    gb_t = din("gb_t", [DB, 16])
    emask = din("emask", [128, 256])
    y_s = dout("y_s", [DB, D]); s_h = dout("s_h", [DEPTH, DB, D]); s_conv = dout("s_conv", [DEPTH, DB, 3, D])
    s_C = dout("s_C", [DEPTH, DB, NH, DK, DK]); s_n = dout("s_n", [DEPTH, DB, NH, DK]); s_m = dout("s_m", [DEPTH, DB, NH])

    N = DB
    x_d = sb("x_d", [128, KC, N])
    hn_d = sb("hn_d", [128, KC, N], BF16)
    mg_d = sb("mg_d", [128, 16, N], BF16)
    xcbd = sb("xcbd", [128, 2, N], BF16)
    gt = sb("gt", [N, 64])
    gbt = sb("gbt", [N, 16])

    XF = xT[:].rearrange("p a b -> p (a b)")
    MF = mg[:].bitcast(F32).rearrange("p a b -> p (a b)")
    HF = hn[:].bitcast(F32).rearrange("p a b -> p (a b)")
    cs_tok = XF[0:N, 0:3072]
    h0_tok = XF[0:N, 3072:4096]
    k_tok = MF[0:N, 0:1024]
    v_tok = MF[0:N, 1024:2048]
    n_tok = MF[0:N, 2048:3072]
    otok = MF[0:N, 3072:4096]
    rhs3 = HF[0:N, 512:560]
    Em = HF[:, 560:816].rearrange("p (b c) -> p b c", b=N)
    qm = xT[:].bitcast(BF16).rearrange("p a b -> p (a b)")[:, 0:2048].rearrange("p (g b c) -> p g b c", g=8, b=N)
    cq_tok = MF[0:N, 1024:2048]
    A_ = xin[:]
    B_ = yout[:]
    convT = A_[:, 0:384].rearrange("p (j c b) -> p j c b", j=3, c=KC)
    h0T = A_[:, 384:512].rearrange("p (c b) -> p c b", c=KC)
    xaT = A_[:, 512:640].rearrange("p (c b) -> p c b", c=KC)
    hT = A_[:, 640:768].rearrange("p (c b) -> p c b", c=KC)
    tmp = lambda i, n=1: A_[:, 768 + 16 * i:768 + 16 * (i + n)]
    qTd = B_[:, 0:128].rearrange("p (h c b) -> p h c b", h=NH, c=2)
    kTd = B_[:, 128:256].rearrange("p (h c b) -> p h c b", h=NH, c=2)
    vTd = B_[:, 256:384].rearrange("p (h c b) -> p h c b", h=NH, c=2)
    nTd = B_[:, 384:512].rearrange("p (h c b) -> p h c b", h=NH, c=2)
    sod = B_[:, 512:640].rearrange("p (h c b) -> p h c b", h=NH, c=2)
    szbd = B_[:, 640:768].rearrange("p (h c b) -> p h c b", h=NH, c=2)
    CqT = B_[:, 768:896].rearrange("p (h c b) -> p h c b", h=NH, c=2)
    bc = rsb[:, 64:384].rearrange("p (h c) -> p h c", h=NH)
    id16 = cst[0:N, C_ID:C_ID + N]
    ones16 = cst[0:N, C_ONE:C_ONE + 128]
    onesf = cst[:, C_ONE:C_ONE + 128]
    NCB = 3
    Cin = [work[:, 4 * i:4 * i + 4, :].rearrange("p a (c v) -> p (a c) v", c=2).rearrange("p (h c) v -> p h c v", h=NH) for i in range(NCB)]
    Cbf = [work[:, 12 + 2 * i:14 + 2 * i, :].bitcast(BF16).rearrange("p a (c v) -> p (a c) v", c=4).rearrange("p (h c) v -> p h c v", h=NH) for i in range(2)]
    kw_r = workr[0:N, 0:2, :].rearrange("p a b -> p (a b)")
    kw_f = workr[:].bitcast(F32)[0:N, 0:2, :].rearrange("p a b -> p (a b)")
    vm = [workr[0:N, 2 + i // 2, (i % 2) * 256:(i % 2) * 256 + 256] for i in range(4)]

    allkeys = ([f'x{k}' for k in range(KC)] + [f'hn{k}' for k in range(KC)] + [f'mg{k}' for k in range(16)] +
               [f'wk{i}' for i in range(NWK)] + [f'wr{i}' for i in range(15)] + ['xin0', 'xin1', 'yout0', 'yout1', 'ext0', 'ext1', 'rsb'])
    S.mark = len(S.ins)
    S.op('dve', lambda e: e.memset(gt[:, 0:1], 0.0), writes=allkeys + ['dbar'])

    def dop(eng, fn, reads=(), writes=()):
        S.op(eng, fn, list(reads) + ['dbar'], writes)

    def ddma(eng, out, in_, reads=(), writes=(), sem=None, **kw):
        S.dma(eng, out, in_, list(reads) + ['dbar'], writes, sem, **kw)

    hnd_fn = lambda kc: hn_d[:, kc, :]
    hndk_fn = lambda kc: 'hnd'

    def to_fm(src_tok, skey, dst3, dkey, nch=KC):
        pt, pk = bank()
        for kc in range(nch):
            dop('pe', (lambda e, kc=kc, pt=pt: e.transpose(out=pt[:, kc * N:(kc + 1) * N], in_=src_tok[:, kc * 128:(kc + 1) * 128], identity=id16)),
                reads=[skey, 'cst'], writes=[pk])
        dop('act', (lambda e, pt=pt: e.activation(out=dst3, in_=pt[:, 0:nch * N].rearrange("p (c b) -> p c b", c=nch), func=AF.Copy)),
            reads=[pk], writes=[dkey])

    def to_tok(src3, skey, dst_tok, dkey):
        for half in range(2):
            pt, pk = bank()
            for q in range(4):
                dop('pe', (lambda e, q=q, pt=pt, half=half: e.transpose(out=pt[0:N, q * 128:(q + 1) * 128], in_=src3[:, half * 4 + q, :], identity=ident)),
                    reads=[skey, 'cst'], writes=[pk])
            dop('act', (lambda e, pt=pt, half=half: e.activation(out=dst_tok[:, half * 512:(half + 1) * 512], in_=pt[0:N, :], func=AF.Copy)),
                reads=[pk], writes=[dkey])

    def decode_layer(l):
        rmsnorm(lambda kc: 'xd', lambda kc: prm[:, kc, 10 * l + R_GN:10 * l + R_GN + 1], hnd_fn, hndk_fn, N,
                src_fn=lambda kc: x_d[:, kc, :], extra=['dbar'])
        ddma('sp', cs_tok.rearrange("p (j d) -> p j d", j=3), st_conv[l], writes=['d_cs'] + [f'd_qm{b_}_{g_}' for b_ in range(N) for g_ in range(8)], sem='d_cs')
        ddma('sp', h0_tok, st_h[l], writes=['d_h0'], sem='d_h0')
        ddma('sp', n_tok.rearrange("p (h k) -> p h k", h=NH), st_n[l], writes=['d_n'], sem='d_n')
        ddma('sp', gt[:, 0:4], st_m[l], writes=['d_mp'], sem='d_mp')
        ddma('sp', s_conv[l, :, 0:2, :], st_conv[l, :, 1:3, :], sem='d_d2d')
        for j in range(3):
            to_fm(cs_tok[:, j * D:(j + 1) * D], 'd_cs', convT[:, j, :, :], 'd_convT')
        to_fm(h0_tok, 'd_h0', h0T, 'd_h0T')
        to_fm(n_tok, 'd_n', nTd.rearrange("p h c b -> p (h c) b"), 'd_nT')
        pG, pGk = bank()
        for kc in range(KC):
            dop('pe', (lambda e, kc=kc: e.matmul(pG[0:N, 0:8], hn_d[:, kc, :], wg[:, l, kc, :], start=(kc == 0), stop=(kc == KC - 1))),
                reads=['hnd', 'wg'], writes=[pGk])
        G = lambda a: gt[:, a:a + 4]
        mp, ig, l1, gg, mt, w_, gi_, en_, t_ = G(0), G(4), G(8), G(12), G(16), G(20), G(24), G(28), G(32)
        dop('dve', lambda e: e.tensor_tensor(out=ig, in0=pG[0:N, 0:4], in1=gbt[:, 4 * l:4 * l + 4], op=ALU.add), reads=[pGk, 'd_gbt'], writes=['d_gt'])
        dop('dve', lambda e: e.tensor_tensor(out=l1, in0=pG[0:N, 4:8], in1=gbt[:, 8 + 4 * l:12 + 4 * l], op=ALU.add), reads=[pGk, 'd_gbt', 'd_gt'], writes=['d_gt'])
        dop('act', lambda e: e.activation(out=l1, in_=l1, func=AF.Exp, scale=-1.0), reads=['d_gt'], writes=['d_gt'])
        dop('act', lambda e: e.activation(out=l1, in_=l1, func=AF.Ln, bias=1.0), reads=['d_gt'], writes=['d_gt'])
        dop('dve', lambda e: e.tensor_tensor(out=gg, in0=mp, in1=l1, op=ALU.subtract), reads=['d_gt', 'd_mp'], writes=['d_gt'])
        dop('dve', lambda e: e.tensor_tensor(out=mt, in0=gg, in1=ig, op=ALU.max), reads=['d_gt'], writes=['d_gt'])
        dop('dve', lambda e: e.tensor_tensor(out=t_, in0=ig, in1=mt, op=ALU.subtract), reads=['d_gt'], writes=['d_gt'])
        dop('act', lambda e: e.activation(out=w_, in_=t_, func=AF.Exp), reads=['d_gt'], writes=['d_gt'])
        dop('dve', lambda e: e.tensor_tensor(out=t_, in0=gg, in1=mt, op=ALU.subtract), reads=['d_gt'], writes=['d_gt'])
        dop('act', lambda e: e.activation(out=gi_, in_=t_, func=AF.Exp), reads=['d_gt'], writes=['d_gt'])
        dop('act', lambda e: e.activation(out=en_, in_=mt, func=AF.Exp, scale=-1.0), reads=['d_gt'], writes=['d_gt'])
        ddma('sp', s_m[l], mt, reads=['d_gt'], sem='d_sm')

        def rg_chunk(c, j, wa, wak, wz, wzk):
            tb = 7 * (c % 2)
            TK = lambda i: f'd_t{tb + i}'
            sz, xc, r_, i_, m_ = tmp(tb + 0), tmp(tb + 1), tmp(tb + 2), tmp(tb + 3), tmp(tb + 4)
            xb = xcbd[:, c % 2, :]
            xbk = f'd_xcb{c % 2}'
            pX, pXk = proj_fm(wa, wak, j, N, hn_fn=hnd_fn, hnk_fn=hndk_fn)
            pZ, pZk = proj_fm(wz, wzk, j, N, hn_fn=hnd_fn, hnk_fn=hndk_fn)
            dop('act', lambda e: e.activation(out=xaT[:, c, :], in_=pX[:, 0:N], func=AF.Copy), reads=[pXk], writes=['d_xaT'])
            act_sig(dop, sz, pZ[:, 0:N], [pZk], TK(0))
            dop('dve', lambda e: e.tensor_tensor(out=sz, in0=sz, in1=pZ[:, 0:N], op=ALU.mult), reads=[pZk, TK(0)], writes=[TK(0)])
            cw = lambda jj: prm[:, c, 10 * l + R_CW + jj:10 * l + R_CW + jj + 1]
            dop('dve', lambda e: e.tensor_scalar(out=xc, in0=convT[:, 0, c, :], scalar1=cw(0), scalar2=prm[:, c, 10 * l + R_CB:10 * l + R_CB + 1],
                                                op0=ALU.mult, op1=ALU.add), reads=['d_convT', 'prm'], writes=[TK(1)])
            for jj in (1, 2):
                dop('dve', (lambda e, jj=jj: e.scalar_tensor_tensor(out=xc, in0=convT[:, jj, c, :], scalar=cw(jj), in1=xc, op0=ALU.mult, op1=ALU.add)),
                    reads=['d_convT', 'prm', TK(1)], writes=[TK(1)])
            dop('dve', lambda e: e.scalar_tensor_tensor(out=xc, in0=xaT[:, c, :], scalar=cw(3), in1=xc, op0=ALU.mult, op1=ALU.add),
                reads=['d_xaT', 'prm', TK(1)], writes=[TK(1)])
            dop('act', lambda e: e.activation(out=xb, in_=xc, func=AF.Copy), reads=[TK(1)], writes=[xbk])
            pR, pRk = bank()
            dop('pe', lambda e: e.matmul(pR[:, 0:N], wbd[:, (0 * DEPTH + l) * KC + c, :], xb, start=True, stop=True), reads=['wbd0'] + [f'wbd_{a_}{b_}{c_}' for a_ in range(2) for b_ in range(2) for c_ in range(2)] + [xbk], writes=[pRk])
            pI, pIk = bank()
            dop('pe', lambda e: e.matmul(pI[:, 0:N], wbd[:, (1 * DEPTH + l) * KC + c, :], xb, start=True, stop=True), reads=['wbd0'] + [f'wbd_{a_}{b_}{c_}' for a_ in range(2) for b_ in range(2) for c_ in range(2)] + [xbk], writes=[pIk])
            act_sig(dop, r_, pR[:, 0:N], [pRk], TK(2), nbias=nprm[:, c, 10 * l + R_BR:10 * l + R_BR + 1])
            act_sig(dop, i_, pI[:, 0:N], [pIk], TK(3), nbias=nprm[:, c, 10 * l + R_BI:10 * l + R_BI + 1])
            dop('act', lambda e: e.activation(out=r_, in_=r_, func=AF.Exp, scale=nsp[:, l, c:c + 1]), reads=[TK(2), 'nsp'], writes=[TK(2)])
            dop('act', lambda e: e.activation(out=m_, in_=r_, func=AF.Square), reads=[TK(2)], writes=[TK(4)])
            act_rpow(dop, m_, m_, [TK(4), 'onep'], TK(4), 0.5, scale=-1.0, bias=onep[:, 0:1])
            dop('dve', lambda e: e.tensor_tensor(out=i_, in0=i_, in1=xc, op=ALU.mult), reads=[TK(3), TK(1)], writes=[TK(3)])
            dop('dve', lambda e: e.tensor_tensor(out=i_, in0=i_, in1=m_, op=ALU.mult), reads=[TK(3), TK(4)], writes=[TK(3)])
            dop('dve', lambda e: e.tensor_tensor(out=hT[:, c, :], in0=r_, in1=h0T[:, c, :], op=ALU.mult), reads=[TK(2), 'd_h0T'], writes=['d_hT'])
            dop('dve', lambda e: e.tensor_tensor(out=hT[:, c, :], in0=hT[:, c, :], in1=i_, op=ALU.add), reads=['d_hT', TK(3)], writes=['d_hT'])
            dop('dve', lambda e: e.tensor_tensor(out=mg_d[:, c, :], in0=hT[:, c, :], in1=sz, op=ALU.mult), reads=['d_hT', TK(0)], writes=['mgd'])

        for g in range(2):
            wa, wak = load_w(win_cols(l, 512 * g, 512), (KC, 512))
            wz, wzk = load_w(win_cols(l, 1024 + 512 * g, 512), (KC, 512))
            for j in range(4):
                rg_chunk(4 * g + j, j, wa, wak, wz, wzk)
        to_tok(xaT, 'd_xaT', otok, 'd_otok')
        ddma('sp', s_conv[l, :, 2, :], otok, reads=['d_otok'], sem='d_otok')
        to_tok(hT, 'd_hT', otok, 'd_otok')
        ddma('sp', s_h[l], otok, reads=['d_otok'], sem='d_otok')

        def ml_head_proj(h):
            (wq, wqk), (wk_, wkk) = load_w2(win_cols(l, 2048 + 256 * h, 256), win_cols(l, 3072 + 256 * h, 256))
            for dkc in range(2):
                pt, pk = proj_fm(wq, wqk, dkc, N, hn_fn=hnd_fn, hnk_fn=hndk_fn)
                dop('dve', (lambda e, pt=pt, dkc=dkc: e.tensor_copy(out=qTd[:, h, dkc, :], in_=pt[:, 0:N])), reads=[pk], writes=['d_qT'])
                pt, pk = proj_fm(wk_, wkk, dkc, N, hn_fn=hnd_fn, hnk_fn=hndk_fn)
                dop('act', (lambda e, pt=pt, dkc=dkc: e.activation(out=kTd[:, h, dkc, :], in_=pt[:, 0:N], func=AF.Copy, scale=DK ** -0.5)), reads=[pk], writes=['d_kT'])
            (wv_, wvk), (wo_, wok) = load_w2(win_cols(l, 4096 + 256 * h, 256), win_cols(l, 5120 + 256 * h, 256))
            wzb_, wzbk = load_w(win_cols(l, 6144 + 256 * h, 256), (KC, 256))
            for dvc in range(2):
                pt, pk = proj_fm(wv_, wvk, dvc, N, hn_fn=hnd_fn, hnk_fn=hndk_fn)
                dop('act', (lambda e, pt=pt, dvc=dvc: e.activation(out=vTd[:, h, dvc, :], in_=pt[:, 0:N], func=AF.Copy)), reads=[pk], writes=['d_vT'])
                pt, pk = proj_fm(wo_, wok, dvc, N, hn_fn=hnd_fn, hnk_fn=hndk_fn)
                act_sig(dop, sod[:, h, dvc, :], pt[:, 0:N], [pk], 'd_so')
                pt, pk = proj_fm(wzb_, wzbk, dvc, N, hn_fn=hnd_fn, hnk_fn=hndk_fn)
                act_sig(dop, szbd[:, h, dvc, :], pt[:, 0:N], [pk], 'd_szb')
                dop('dve', (lambda e, pt=pt, dvc=dvc: e.tensor_tensor(out=szbd[:, h, dvc, :], in0=szbd[:, h, dvc, :], in1=pt[:, 0:N], op=ALU.mult)),
                    reads=[pk, 'd_szb'], writes=['d_szb'])
            pt, pk = bank()
            for dkc in range(2):
                dop('pe', (lambda e, dkc=dkc: e.transpose(out=pt[0:N, dkc * 128:(dkc + 1) * 128], in_=kTd[:, h, dkc, :], identity=ident)), reads=['d_kT', 'cst'], writes=[pk])
            for dvc in range(2):
                dop('pe', (lambda e, dvc=dvc: e.transpose(out=pt[0:N, 256 + dvc * 128:256 + (dvc + 1) * 128], in_=vTd[:, h, dvc, :], identity=ident)), reads=['d_vT', 'cst'], writes=[pk])
            dop('dve', lambda e: e.tensor_scalar(out=kw_r[:, h * 256:(h + 1) * 256], in0=pt[0:N, 0:256], scalar1=gt[:, 20 + h:21 + h], scalar2=None, op0=ALU.mult),
                reads=[pk, 'd_gt'], writes=['d_ktok'])
            dop('dve', lambda e: e.tensor_copy(out=v_tok[:, h * 256:(h + 1) * 256], in_=pt[0:N, 256:512]), reads=[pk], writes=['d_vtok'])
            dop('dve', lambda e: e.scalar_tensor_tensor(out=n_tok[:, h * 256:(h + 1) * 256], in0=n_tok[:, h * 256:(h + 1) * 256], scalar=gt[:, 24 + h:25 + h],
                                                       in1=kw_f[:, h * 256:(h + 1) * 256], op0=ALU.mult, op1=ALU.add),
                reads=['d_n', 'd_nT', 'd_gt', 'd_ktok'], writes=['d_n'])
            for q in range(3):
                dop('dve', (lambda e, q=q: e.tensor_scalar(out=rhs3[:, 16 * q:16 * q + 16], in0=id16, scalar1=gt[:, 20 + 4 * q + h:21 + 4 * q + h], scalar2=None, op0=ALU.mult)),
                    reads=['cst', 'd_gt'], writes=['d_rhs3'])
            pB, pBk = bank()
            dop('pe', lambda e: e.matmul(pB[:, 0:48], ones16, rhs3, start=True, stop=True), reads=['cst', 'd_rhs3'], writes=[pBk])
            dop('dve', lambda e: e.tensor_copy(out=bc[:, h, 0:48], in_=pB[:, 0:48]), reads=[pBk], writes=['d_bc'])
            t0 = tmp(0, 2).rearrange("p (c b) -> p c b", c=2)
            t1 = tmp(2, 2).rearrange("p (c b) -> p c b", c=2)
            dop('dve', lambda e: e.tensor_tensor(out=t0, in0=qTd[:, h, :, :], in1=kTd[:, h, :, :], op=ALU.mult), reads=['d_qT', 'd_kT'], writes=['d_t0', 'd_t1'])
            dop('dve', lambda e: e.tensor_tensor(out=t1, in0=qTd[:, h, :, :], in1=nTd[:, h, :, :], op=ALU.mult), reads=['d_qT', 'd_nT'], writes=['d_t2', 'd_t3'])
            pQ, pQk = bank()
            for dkc in range(2):
                dop('pe', (lambda e, dkc=dkc: e.matmul(pQ[:, 0:N], onesf, t0[:, dkc, :], start=(dkc == 0), stop=(dkc == 1))), reads=['cst', 'd_t0', 'd_t1'], writes=[pQk])
            for dkc in range(2):
                dop('pe', (lambda e, dkc=dkc: e.matmul(pQ[:, N:2 * N], onesf, t1[:, dkc, :], start=(dkc == 0), stop=(dkc == 1))), reads=['cst', 'd_t2', 'd_t3'], writes=[pQk])
            dop('dve', lambda e: e.tensor_copy(out=bc[:, h, 48:80], in_=pQ[:, 0:2 * N]), reads=[pQk], writes=['d_bc'])

        for h in range(NH):
            ml_head_proj(h)
        ddma('sp', s_n[l], n_tok.rearrange("p (h k) -> p h k", h=NH), reads=['d_n'], sem='d_n')

        dop('dve', lambda e: e.memset(gt[:, 40:41], 0.0), reads=['d_convT', 'd_h0T'], writes=['d_cs', 'd_qmf'])
        for b in range(N):
            for h in range(NH):
                for dkc in range(2):
                    g = h * 2 + dkc
                    if (g + b) % 2 == 0:
                        dop('dve', (lambda e, g=g, b=b, h=h, dkc=dkc: e.tensor_scalar(out=qm[:, g, b, :], in0=Em[:, b, :], scalar1=qTd[:, h, dkc, b:b + 1],
                                                                                   scalar2=None, op0=ALU.mult)),
                            reads=['d_em', 'd_qT', 'd_qmf'], writes=[f'd_qm{b}_{g}'])
                    else:
                        dop('act', (lambda e, g=g, b=b, h=h, dkc=dkc: e.activation(out=qm[:, g, b, :], in_=Em[:, b, :], func=AF.Copy, scale=qTd[:, h, dkc, b:b + 1])),
                            reads=['d_em', 'd_qT', 'd_qmf'], writes=[f'd_qm{b}_{g}'])
        pCqs = [bank(), bank()]
        pCis = [ps.index(p_) for p_, _ in pCqs]
        for i_ in pCis:
            reserved.add(i_)
        def c_load(b):
            ddma('sp', Cin[b % NCB], st_C[l, b].rearrange("h (c p) v -> p h c v", p=128), writes=[f'd_Cin{b % NCB}'], sem=f'd_Cin{b % NCB}')
        for b in range(NCB - 1):
            c_load(b)
        for b in range(N):
            ci, cik = Cin[b % NCB], f'd_Cin{b % NCB}'
            cb_, cbk = Cbf[b % 2], f'd_Cbf{b % 2}'
            if b + NCB - 1 < N:
                c_load(b + NCB - 1)
            for h2 in range(2):
                dop('dve' if h2 == 0 else 'act',
                    ((lambda e, ci=ci, cb_=cb_, h2=h2: e.tensor_copy(out=cb_[:, 2 * h2:2 * h2 + 2, :, :], in_=ci[:, 2 * h2:2 * h2 + 2, :, :])) if h2 == 0 else
                     (lambda e, ci=ci, cb_=cb_, h2=h2: e.activation(out=cb_[:, 2 * h2:2 * h2 + 2, :, :], in_=ci[:, 2 * h2:2 * h2 + 2, :, :], func=AF.Copy))),
                    reads=[cik], writes=[cbk])
            for h in range(NH):
                pq, pqk = pCqs[h // 2]
                for dkc in range(2):
                    first = (b == 0 and h % 2 == 0 and dkc == 0)
                    dop('pe', (lambda e, h=h, dkc=dkc, cb_=cb_, b=b, pq=pq, first=first: e.matmul(pq[0:N, (h % 2) * 256:(h % 2) * 256 + 256], qm[:, h * 2 + dkc, b, :],
                                                                                         cb_[:, h, dkc, :], start=first, stop=(b == N - 1 and dkc == 1),
                                                                                         skip_group_check=True)),
                        reads=[cbk, f'd_qm{b}_{h * 2 + dkc}'], writes=[pqk])
            for h in range(NH):
                vi = (b * NH + h) % 4
                dop('dve', (lambda e, h=h, vi=vi, b=b: e.tensor_scalar(out=vm[vi], in0=v_tok[:, h * 256:(h + 1) * 256], scalar1=cst[0:N, C_ID + b:C_ID + b + 1],
                                                                   scalar2=None, op0=ALU.mult)), reads=['d_vtok', 'cst'], writes=[f'd_vm{vi}'])
                for dkc in range(2):
                    pU, pUk = bank()
                    dop('pe', (lambda e, h=h, dkc=dkc, vi=vi, pU=pU: e.matmul(pU[:, 0:256], kw_r[:, h * 256 + dkc * 128:h * 256 + (dkc + 1) * 128], vm[vi],
                                                                          start=True, stop=True)), reads=['d_ktok', f'd_vm{vi}'], writes=[pUk])
                    dop('dve', (lambda e, h=h, dkc=dkc, pU=pU, ci=ci, b=b: e.scalar_tensor_tensor(out=ci[:, h, dkc, :], in0=ci[:, h, dkc, :],
                                                                                         scalar=bc[:, h, 16 + b:17 + b], in1=pU[:, 0:256],
                                                                                         op0=ALU.mult, op1=ALU.add)),
                        reads=[cik, 'd_bc', pUk], writes=[cik])
            ddma('pool', s_C[l, b].rearrange("h (c p) v -> p h c v", p=128), ci, reads=[cik], sem=f'd_Cst{b % NCB}')

        for i_, (pq, pqk) in enumerate(pCqs):
            dop('act', (lambda e, pq=pq, i_=i_: e.activation(out=cq_tok[:, i_ * 512:(i_ + 1) * 512], in_=pq[0:N, :], func=AF.Copy)),
                reads=[pqk, 'd_ktok'], writes=['d_vtok'])
        to_fm(cq_tok, 'd_vtok', CqT.rearrange("p h c b -> p (h c) b"), 'd_CqT')
        for i_ in pCis:
            reserved.discard(i_)
        def ml_head_out(h):
            Pb, den, dn2, rs_ = tmp(4), tmp(5), tmp(6), tmp(11)
            wb, gib, enb, qkb, qnb = bc[:, h, 0:16], bc[:, h, 16:32], bc[:, h, 32:48], bc[:, h, 48:64], bc[:, h, 64:80]
            dop('dve', lambda e: e.tensor_tensor(out=Pb, in0=wb, in1=qkb, op=ALU.mult), reads=['d_bc'], writes=['d_t4'])
            dop('dve', lambda e: e.tensor_tensor(out=den, in0=gib, in1=qnb, op=ALU.mult), reads=['d_bc'], writes=['d_t5'])
            dop('dve', lambda e: e.tensor_tensor(out=den, in0=den, in1=Pb, op=ALU.add), reads=['d_t5', 'd_t4'], writes=['d_t5'])
            dop('dve', lambda e: e.tensor_tensor(out=dn2, in0=den, in1=enb, op=ALU.max), reads=['d_t5', 'd_bc'], writes=['d_t6'])
            dop('dve', lambda e: e.scalar_tensor_tensor(out=dn2, in0=den, scalar=-1.0, in1=dn2, op0=ALU.mult, op1=ALU.max), reads=['d_t5', 'd_t6'], writes=['d_t6'])
            act_rpow(dop, dn2, dn2, ['d_t6'], 'd_t6', -1.0)
            pQ2, pQ2k = bank()
            for dvc in range(2):
                y, t2, ysq_ = tmp(7 + dvc), tmp(9), tmp(12 + dvc)
                yk, ysk = f'd_t{7 + dvc}', f'd_t{12 + dvc}'
                col = (h * 2 + dvc) * N
                dop('dve', (lambda e, y=y, dvc=dvc: e.tensor_tensor(out=y, in0=CqT[:, h, dvc, :], in1=gib, op=ALU.mult)), reads=['d_CqT', 'd_bc'], writes=[yk])
                dop('dve', (lambda e, t2=t2, dvc=dvc: e.tensor_tensor(out=t2, in0=vTd[:, h, dvc, :], in1=Pb, op=ALU.mult)), reads=['d_vT', 'd_t4'], writes=['d_t9'])
                dop('dve', (lambda e, y=y, t2=t2: e.tensor_tensor(out=y, in0=y, in1=t2, op=ALU.add)), reads=[yk, 'd_t9'], writes=[yk])
                dop('dve', (lambda e, y=y: e.tensor_tensor(out=y, in0=y, in1=dn2, op=ALU.mult)), reads=[yk, 'd_t6'], writes=[yk])
                dop('dve', (lambda e, y=y, dvc=dvc: e.tensor_tensor(out=y, in0=y, in1=sod[:, h, dvc, :], op=ALU.mult)), reads=[yk, 'd_so'], writes=[yk])
                dop('act', (lambda e, y=y, ysq_=ysq_: e.activation(out=ysq_, in_=y, func=AF.Square)), reads=[yk], writes=[ysk])
                dop('pe', (lambda e, ysq_=ysq_, dvc=dvc: e.matmul(pQ2[:, 0:N], onesf, ysq_, start=(dvc == 0), stop=(dvc == 1))), reads=['cst', ysk], writes=[pQ2k])
            act_rpow(dop, rs_, pQ2[:, 0:N], [pQ2k, 'epsc'], 'd_t11', -0.5, scale=1.0 / DK, bias=epsc[:, 0:1])
            for dvc in range(2):
                y = tmp(7 + dvc)
                yk = f'd_t{7 + dvc}'
                fc = 2 * h + dvc
                dop('dve', (lambda e, y=y, fc=fc: e.scalar_tensor_tensor(out=y, in0=y, scalar=prm[:, fc, 10 * l + R_GMH:10 * l + R_GMH + 1], in1=rs_,
                                                                       op0=ALU.mult, op1=ALU.mult)), reads=[yk, 'd_t11', 'prm'], writes=[yk])
                dop('dve', (lambda e, y=y, fc=fc, dvc=dvc: e.tensor_tensor(out=mg_d[:, 8 + fc, :], in0=y, in1=szbd[:, h, dvc, :], op=ALU.mult)),
                    reads=[yk, 'd_szb'], writes=['mgd'])

        for h in range(NH):
            ml_head_out(h)

        for g4 in range(4):
            wo4, wo4k = load_w(wout_cols(l, 256 * g4, 256), (16, 256))
            for dd in range(2):
                dc = 2 * g4 + dd
                pt, pk = bank()
                for kc in range(16):
                    dop('pe', (lambda e, pt=pt, kc=kc, dd=dd, wo4=wo4: e.matmul(pt[:, 0:N], wo4[:, kc, dd * 128:(dd + 1) * 128], mg_d[:, kc, :],
                                                                            start=(kc == 0), stop=(kc == 15))), reads=[wo4k, 'mgd'], writes=[pk])
                dop('dve', (lambda e, pt=pt, dc=dc: e.tensor_tensor(out=x_d[:, dc, :], in0=x_d[:, dc, :], in1=pt[:, 0:N], op=ALU.add)), reads=[pk, 'xd'], writes=['xd'])

    if do_decode:
        ddma('sp', otok, x_s, writes=['d_otok'], sem='d_otok')
        ddma('sp', gbt[:], gb_t, writes=['d_gbt'], sem='d_gbt')
        ddma('sp', HF[:, 560:816], emask, writes=['d_em'], sem='d_em')
        to_fm(otok, 'd_otok', x_d[:], 'xd')
        for l in range(DEPTH):
            decode_layer(l)
        rmsnorm(lambda kc: 'xd', lambda kc: prm[:, kc, R_GF:R_GF + 1], lambda kc: hT[:, kc, :], lambda kc: 'd_hT', N,
                src_fn=lambda kc: x_d[:, kc, :], extra=['dbar'])
        to_tok(hT, 'd_hT', otok, 'd_otok')
        ddma('sp', y_s, otok, reads=['d_otok'], sem='d_otok')

    S.emit()
    return nc, es


def make_consts():
    c = np.zeros((128, NCONST), np.float32)
    c[:, C_ID:C_ID + 128] = np.eye(128, dtype=np.float32)
    c[:, C_ONE:C_ONE + 128] = 1.0
    s = np.arange(128)
    c[:, C_MASK:C_MASK + 128] = (s[:, None] <= s[None, :]).astype(np.float32)
    for h in range(NH):
        c[h, C_SEL + 128 * h:C_SEL + 128 * h + 128] = 1.0
    c[:, C_ONES4:C_ONES4 + T] = 1.0
    return c


_CACHE = {}


def kernel(**inputs):
    f = lambda k: np.ascontiguousarray(np.asarray(inputs[k], dtype=np.float32))
    if 'nc' not in _CACHE:
        _CACHE['nc'] = build_program()
    nc, _es = _CACHE['nc']
    consts = make_consts()
    shared = {k: f(k) for k in ("g_norm", "w_in", "conv_w", "conv_b", "w_rgate", "b_rgate", "w_igate", "b_igate",
                                "lru_lambda", "b_mi", "b_mf", "g_mhead", "w_out", "g_final")}
    xp = f("x_prompt")
    xs = f("x_sample").reshape(128, D)
    sth, stc, stC, stn, stm = f("state_rglru_h"), f("state_rglru_conv"), f("state_mlstm_C"), f("state_mlstm_n"), f("state_mlstm_m")
    gb = np.ascontiguousarray(np.tile(np.concatenate([shared["b_mi"].reshape(-1), shared["b_mf"].reshape(-1)])[None, :], (DB, 1)))
    em = np.ascontiguousarray(np.tile(np.eye(DB, dtype=np.float32).reshape(1, DB * DB), (128, 1)))
    in_maps = []
    for i in range(NCORES):
        m = dict(shared)
        sl = slice(i * DB, (i + 1) * DB)
        m["consts"] = consts
        m["x_p"] = xp[i]
        m["x_s"] = np.ascontiguousarray(xs[sl])
        m["st_h"] = np.ascontiguousarray(sth[:, sl]); m["st_conv"] = np.ascontiguousarray(stc[:, sl])
        m["st_C"] = np.ascontiguousarray(stC[:, sl]); m["st_n"] = np.ascontiguousarray(stn[:, sl]); m["st_m"] = np.ascontiguousarray(stm[:, sl])
        m["gb_t"] = gb
        m["emask"] = em
        in_maps.append(m)
    res = run_bass_kernel_spmd(nc, in_maps, core_ids=list(range(NCORES)))
    R = res.results
    g = lambda k: np.stack([np.asarray(R[i][k], dtype=np.float32) for i in range(NCORES)], axis=0)
    gc = lambda k, ax: np.ascontiguousarray(np.concatenate([np.asarray(R[i][k], dtype=np.float32) for i in range(NCORES)], axis=ax))
    y_prompt = g("y_p")
    p_h = np.ascontiguousarray(g("p_h").transpose(1, 0, 2))
    p_conv = np.ascontiguousarray(g("p_conv").transpose(1, 0, 2, 3))
    p_C = np.ascontiguousarray(g("p_C").transpose(1, 0, 2, 3, 4))
    p_n = np.ascontiguousarray(g("p_n").transpose(1, 0, 2, 3))
    p_m = np.ascontiguousarray(g("p_m").transpose(1, 0, 2))
    y_s = gc("y_s", 0).reshape(128, 1, D)
    s_h = gc("s_h", 1); s_conv = gc("s_conv", 1); s_C = gc("s_C", 1); s_n = gc("s_n", 1); s_m = gc("s_m", 1)
    return (y_prompt, y_s, p_h, p_conv, p_C, p_n, p_m, s_h, s_conv, s_C, s_n, s_m)
```

```python
import contextlib
import numpy as np
import concourse.bass as bass
import concourse.mybir as mybir
from concourse.bass_utils import run_bass_kernel_spmd

F32 = mybir.dt.float32
F32R = mybir.dt.float32r
BF16 = mybir.dt.bfloat16
AF = mybir.ActivationFunctionType
ALU = mybir.AluOpType

NCORES = 8
WHATIF = {}
USE_SCRATCH = True
PAIR_LOADS = True
D = 1024
KC = 8
SEQ = 2048
T = 512
NT = SEQ // T
DIN = 7176
DEPTH = 2
NH = 4
DK = 256
EPS = 1e-6
DB = 16

C_ID, C_ONE, C_MASK, C_SEL, C_ONES4, NCONST = 0, 128, 256, 384, 896, 1408
R_GN, R_CW, R_CB, R_BR, R_BI, R_LAM, R_GMH = 0, 1, 5, 6, 7, 8, 9
R_GF = 20
NR = 21


class _FakeIns:
    def then_inc(self, *a, **k):
        return self


class _FakeEng:
    def __init__(self):
        self.rec = None

    def __getattr__(self, name):
        def f(*a, **k):
            self.rec = (name, a, k)
            return _FakeIns()
        return f


def _free(ap):
    n = 1
    for d in ap.shape[1:]:
        n *= int(d)
    return n


def _cost(eng, fn):
    fe = _FakeEng()
    fn(fe)
    name, a, k = fe.rec
    if name == 'matmul':
        rhs = a[2] if len(a) > 2 else k['rhs']
        lhs = a[1] if len(a) > 1 else k['lhsT']
        n = _free(rhs)
        m = _free(lhs)
        dt = rhs.dtype
        passes = 4 if (dt == F32 or (dt == F32R and n < 256)) else 1
        return 0.015 + passes * max(n, m) / 2000.0
    if name == 'transpose':
        return 0.12
    if name == 'dma_start':
        out = k['out']
        src = k['in_']
        esz = max(2 if out.dtype == BF16 else 4, 2 if src.dtype == BF16 else 4)
        nbytes = _free(out) * int(out.shape[0]) * esz
        return 2.0 + nbytes / 330e3
    out = k.get('out', a[0] if a else None)
    n = _free(out) if out is not None else 64
    if eng == 'act':
        return 0.2 + n / 1400.0
    if name == 'tensor_tensor_scan':
        return 0.1 + 2 * n / 960.0
    if eng == 'pool':
        return 0.3 + n / 500.0
    return 0.08 + n / 1000.0


class Sched:
    ENGS = ('pe', 'act', 'dve', 'pool', 'sp')
    WINDOW = {'pe': 100, 'act': 160, 'dve': 160, 'pool': 1, 'sp': 1}
    PE_GROUPS = False
    PE_DECODE_INORDER = False
    CRIT = False
    DMA_OCC = 1.0
    SLACK = 0.0

    def __init__(self, nc, es, reorder=True):
        self.nc, self.es, self.reorder = nc, es, reorder
        self.ins = []
        self.lastw, self.readers, self.sems = {}, {}, {}

    def _add(self, eng, fn, reads, writes, sem, cost):
        i = len(self.ins)
        deps = set()
        for k in reads:
            if k in self.lastw:
                deps.add(self.lastw[k])
        for k in writes:
            if k in self.lastw:
                deps.add(self.lastw[k])
            deps.update(self.readers.get(k, ()))
        self.ins.append(dict(eng=eng, fn=fn, deps=deps, sem=sem, cost=cost, wkeys=tuple(writes)))
        for k in reads:
            self.readers.setdefault(k, set()).add(i)
        for k in writes:
            self.lastw[k] = i
            self.readers[k] = set()

    def op(self, eng, fn, reads=(), writes=()):
        self._add(eng, fn, reads, writes, None, _cost(eng, fn))

    def dma(self, eng, out, in_, reads=(), writes=(), sem=None, **kw):
        fn = (lambda e: e.dma_start(out=out, in_=in_, **kw))
        self._add(eng, fn, reads, writes, 'D_' + sem, _cost(eng, fn))

    def _schedule(self):
        ins = self.ins
        n = len(ins)
        if not self.reorder:
            return list(range(n))
        succ = [[] for _ in range(n)]
        for i, I in enumerate(ins):
            for d in I['deps']:
                succ[d].append(i)
        unit_of = [None] * n
        units = []
        per_eng = {e: [] for e in self.ENGS}
        last_pe = None
        for i, I in enumerate(ins):
            e = I['eng']
            if e == 'pe' and self.PE_GROUPS and last_pe is not None and units[last_pe]['wkeys'] == I['wkeys'] and len(units[last_pe]['members']) < 40:
                units[last_pe]['members'].append(i)
                unit_of[i] = last_pe
                continue
            u = len(units)
            units.append(dict(eng=e, members=[i], wkeys=I['wkeys'], pos=len(per_eng[e])))
            per_eng[e].append(u)
            unit_of[i] = u
            if e == 'pe':
                last_pe = u
        mark = getattr(self, 'mark', n) if self.PE_DECODE_INORDER else n
        ext = [0] * len(units)
        indeg = [0] * n
        for i, I in enumerate(ins):
            indeg[i] = len(I['deps'])
            for d in I['deps']:
                if unit_of[d] != unit_of[i]:
                    ext[unit_of[i]] += 1
        tail = [0.0] * n
        if self.CRIT:
            for i in range(n - 1, -1, -1):
                t = 0.0
                for s_ in succ[i]:
                    if tail[s_] > t:
                        t = tail[s_]
                tail[i] = t + ins[i]['cost'] + 0.1
        head = {e: 0 for e in self.ENGS}
        udone = [False] * len(units)
        finish = [0.0] * n
        rtime = [0.0] * n
        ready = {e: [] for e in self.ENGS}
        for u, U in enumerate(units):
            if ext[u] == 0:
                ready[U['eng']].append(u)
        free = {e: 0.0 for e in self.ENGS}
        order = []
        nsched = [0]
        open_unit = [None]

        def sched_ins(i, e, u):
            est = max(free[e], rtime[i])
            finish[i] = est + ins[i]['cost']
            free[e] = finish[i] if ins[i]['sem'] is None else est + 0.1 + (ins[i]['cost'] - 2.0) * self.DMA_OCC
            order.append(i)
            nsched[0] += 1
            for s_ in succ[i]:
                indeg[s_] -= 1
                su = unit_of[s_]
                lat = 0.05 if ins[s_]['eng'] == e else 0.2
                t = finish[i] + lat
                if t > rtime[s_]:
                    rtime[s_] = t
                if su != u:
                    ext[su] -= 1
                    if ext[su] == 0 and not udone[su]:
                        ready[units[su]['eng']].append(su)

        while nsched[0] < n:
            best = None
            for e in self.ENGS:
                pl = per_eng[e]
                h = head[e]
                while h < len(pl) and udone[pl[h]]:
                    h += 1
                head[e] = h
                if h >= len(pl):
                    continue
                if e == 'pe' and open_unit[0] is not None:
                    u, k = open_unit[0]
                    m = units[u]['members'][k]
                    if indeg[m] == 0:
                        cand = ((max(free[e], rtime[m]), -1), ('open', u, k))
                        if best is None or cand[0] < best[0]:
                            best = (cand[0], cand[1], e)
                    continue
                lim = h + self.WINDOW[e]
                if e == 'pe' and units[pl[h]]['members'][0] >= mark:
                    lim = h + 1
                cand = None
                for u in ready[e]:
                    U = units[u]
                    if U['pos'] >= lim or (e == 'pe' and U['members'][0] >= mark and U['pos'] != h):
                        continue
                    st, acc = 0.0, 0.0
                    for m in U['members']:
                        if rtime[m] - acc > st:
                            st = rtime[m] - acc
                        acc += ins[m]['cost']
                    est_ = max(free[e], st)
                    if self.CRIT:
                        key = (est_ if est_ > free[e] + self.SLACK else free[e], -tail[U['members'][0]], U['pos'])
                    else:
                        key = (est_, U['pos'])
                    if cand is None or key < cand[0]:
                        cand = (key, ('full', u, 0))
                if e == 'pe':
                    hu = pl[h]
                    if ext[hu] > 0:
                        m0 = units[hu]['members'][0]
                        if indeg[m0] == 0:
                            key = (max(free[e], rtime[m0]), units[hu]['pos'])
                            if cand is None or key < cand[0]:
                                cand = (key, ('head', hu, 0))
                if cand is not None and (best is None or cand[0] < best[0]):
                    best = (cand[0], cand[1], e)
            assert best is not None, "scheduler stuck"
            _, (kind, u, k), e = best
            if kind == 'full':
                ready[e].remove(u)
                udone[u] = True
                for i in units[u]['members']:
                    sched_ins(i, e, u)
            else:
                mem = units[u]['members']
                udone[u] = True
                if u in ready[e]:
                    ready[e].remove(u)
                sched_ins(mem[k], e, u)
                open_unit[0] = (u, k + 1) if k + 1 < len(mem) else None
        self.sim_time = max(finish)
        self.finish = finish
        return order

    def emit(self):
        nc = self.nc
        ins = self.ins
        order = self._schedule()
        stream = {e: [] for e in self.ENGS}
        cnt = {e: 0 for e in self.ENGS}
        dcnt = {}
        tok = [None] * len(ins)
        for i in order:
            I = ins[i]
            if I['sem'] is None:
                cnt[I['eng']] += 1
                tok[i] = ('E_' + I['eng'], cnt[I['eng']], 1)
            else:
                dcnt[I['sem']] = dcnt.get(I['sem'], 0) + 16
                tok[i] = (I['sem'], dcnt[I['sem']], 16)
        waited = {e: {} for e in self.ENGS}
        names = set()
        for i in order:
            I = ins[i]
            e = I['eng']
            deps = {}
            for d in I['deps']:
                s_, v, _ = tok[d]
                if deps.get(s_, 0) < v:
                    deps[s_] = v
            waits = []
            for s_, v in deps.items():
                if e == 'pe' and s_ == 'E_pe':
                    continue
                if waited[e].get(s_, 0) >= v:
                    continue
                waited[e][s_] = v
                waits.append((s_, v))
                names.add(s_)
            names.add(tok[i][0])
            stream[e].append((waits, I['fn'], (tok[i][0], tok[i][2])))
        for nme in sorted(names):
            self.sems[nme] = self.es.enter_context(nc.semaphore(nme))
        fin = list(dcnt.items())
        sems = self.sems

        def mk(engname, final):
            def body(e):
                for waits, fn, inc in stream[engname]:
                    for s_, v in waits:
                        e.wait_ge(sems[s_], v)
                    fn(e).then_inc(sems[inc[0]], inc[1])
                if final:
                    for nme, c in fin:
                        e.wait_ge(sems[nme], c)
            return body
        with nc.Block() as block:
            block.tensor(mk('pe', False))
            block.scalar(mk('act', False))
            block.vector(mk('dve', False))
            block.gpsimd(mk('pool', False))
            block.sync(mk('sp', True))


def build_program(do_decode=True):
    nc = bass.Bass("TRN2", target_bir_lowering=False)
    es = contextlib.ExitStack()
    S = Sched(nc, es)

    def din(name, shape):
        return nc.dram_tensor(name, list(shape), F32, kind="ExternalInput").ap()

    def dout(name, shape):
        return nc.dram_tensor(name, list(shape), F32, kind="ExternalOutput").ap()

    x_p = din("x_p", [SEQ, D])
    consts = din("consts", [128, NCONST])
    g_norm = din("g_norm", [DEPTH, D]); w_in = din("w_in", [DEPTH, D, DIN])
    conv_w = din("conv_w", [DEPTH, 4, D]); conv_b = din("conv_b", [DEPTH, D])
    w_rg = din("w_rgate", [DEPTH, 16, 64, 64]); b_rg = din("b_rgate", [DEPTH, D])
    w_ig = din("w_igate", [DEPTH, 16, 64, 64]); b_ig = din("b_igate", [DEPTH, D])
    lam = din("lru_lambda", [DEPTH, D]); b_mi = din("b_mi", [DEPTH, NH]); b_mf = din("b_mf", [DEPTH, NH])
    g_mh = din("g_mhead", [DEPTH, D]); w_out = din("w_out", [DEPTH, 2 * D, D]); g_fin = din("g_final", [D])
    y_p = dout("y_p", [SEQ, D])
    p_h = dout("p_h", [DEPTH, D]); p_conv = dout("p_conv", [DEPTH, 3, D])
    p_C = dout("p_C", [DEPTH, NH, DK, DK]); p_n = dout("p_n", [DEPTH, NH, DK]); p_m = dout("p_m", [DEPTH, NH])

    def sb(name, shape, dt=F32):
        return es.enter_context(nc.sbuf_tensor(name, list(shape), dt))

    S_ONES4 = 384
    cst = sb("cst", [128, 896])
    selr = sb("selr", [4, 512], F32R)
    ones_w = sb("ones_w", [128, 256], F32R)
    ones_r = ones_w[:, 0:128]
    prm = sb("prm", [128, KC, 32])
    nsp = sb("nsp", [128, DEPTH, KC])
    gcol = sb("gcol", [4, 8])
    wbd = sb("wbd", [128, 2 * DEPTH * KC, 128], BF16)
    wg = sb("wg", [128, DEPTH, KC, 8], BF16)
    xT = sb("xT", [128, KC, T])
    hn = sb("hn", [128, KC, T], BF16)
    mg = sb("mg", [128, 16, T], BF16)
    NS = 4
    wsl = [sb(f"wsl{i}", [128, KC * 512], BF16) for i in range(NS)]
    xin = sb("xin", [128, D])
    yout = sb("yout", [128, D])
    sq = [sb(f"sq{i}", [128, T], F32R) for i in range(2)]
    rsb = sb("rsb", [128, T])
    grow = [sb(f"grow{i}", [4, T], F32 if i < 3 else F32R) for i in range(6)]
    acol = sb("acol", [128, 16])
    NWK = 16
    work = sb("work", [128, NWK, T])
    workr = sb("workr", [128, 15, T], F32R)
    ext = [sb(f"ext{i}", [128, T + 4]) for i in range(2)]
    Cst = [sb(f"Cst{l}", [128, NH, 2, DK], F32R) for l in range(DEPTH)]
    nrep = [sb(f"nrep{l}", [128, NH, 2, 128], F32R) for l in range(DEPTH)]
    hst = sb("hst", [128, DEPTH, KC])
    convst = sb("convst", [128, DEPTH * KC, 3])
    mst = sb("mst", [4, DEPTH])
    dec = sb("dec", [128, 4])

    ps = [es.enter_context(nc.psum_tensor(f"ps{i}", [128, 512], F32)) for i in range(8)]
    psctr = [0]

    reserved = set()

    def bank():
        while psctr[0] % 8 in reserved:
            psctr[0] += 1
        i = psctr[0] % 8
        psctr[0] += 1
        if WHATIF.get('psum'):
            return ps[i], f"psv{psctr[0]}"
        return ps[i], f"ps{i}"

    ident = cst[:, C_ID:C_ID + 128]
    mask01 = cst[:, C_MASK:C_MASK + 128]
    ones4 = cst[0:4, S_ONES4:S_ONES4 + T]

    def wk(i):
        return work[:, i, :]

    def wkr(i):
        return workr[:, i, :]

    def wkrf(i):
        return workr[:].bitcast(F32)[:, i, :]

    S.dma('sp', cst[:, 0:384], consts[:, 0:384], writes=['cst'], sem='cst')
    S.dma('sp', cst[:, 384:896], consts[:, C_ONES4:C_ONES4 + 512], writes=['cst'], sem='cst')
    S.dma('sp', work[0:4, 2, :], consts[0:4, C_SEL:C_SEL + 512], writes=['wk2'], sem='selst')
    S.op('dve', lambda e: e.tensor_copy(out=selr[:], in_=work[0:4, 2, :]), reads=['wk2'], writes=['selr'])
    def x_load(tt):
        for j in range(4):
            r0 = tt * T + j * 128
            for half in range(2):
                key = f'xin{half}'
                S.dma('sp', xin[:, half * 512:(half + 1) * 512], x_p[r0:r0 + 128, half * 512:(half + 1) * 512], writes=[key], sem=key)
                pt, pk = bank()
                for q in range(4):
                    kc = half * 4 + q
                    S.op('pe', (lambda e, pt=pt, q=q, kc=kc: e.transpose(out=pt[:, q * 128:(q + 1) * 128], in_=xin[:, kc * 128:(kc + 1) * 128], identity=ident)),
                         reads=[key, 'cst'], writes=[pk])
                S.op('act', (lambda e, pt=pt, half=half, j=j: e.activation(out=xT[:, half * 4:half * 4 + 4, j * 128:(j + 1) * 128],
                                                                         in_=pt[:, :].rearrange("p (a b) -> p a b", a=4), func=AF.Copy)),
                     reads=[pk], writes=[f'x{half * 4 + q}' for q in range(4)])

    x_load(0)
    S.op('dve', lambda e: e.tensor_copy(out=ones_w[:, 0:128], in_=cst[:, C_ONE:C_ONE + 128]), reads=['cst'], writes=['ones_r'])
    S.op('dve', lambda e: e.tensor_copy(out=ones_w[:, 128:256], in_=cst[:, C_ONE:C_ONE + 128]), reads=['cst'], writes=['ones_r'])
    prow = work[0:32, 0:2, :]

    def prow_row(r):
        return work[r:r + 1, 0:2, :]
    S.op('dve', lambda e: e.memset(work[0:32, 0:2, :], 0.0), writes=['wk0', 'wk1'])
    rows = []
    for l in range(DEPTH):
        rows.append((10 * l + R_GN, g_norm[l:l + 1, :]))
        for j in range(4):
            rows.append((10 * l + R_CW + j, conv_w[l, j:j + 1, :]))
        rows.append((10 * l + R_CB, conv_b[l:l + 1, :]))
        rows.append((10 * l + R_BR, b_rg[l:l + 1, :]))
        rows.append((10 * l + R_BI, b_ig[l:l + 1, :]))
        rows.append((10 * l + R_LAM, lam[l:l + 1, :]))
        rows.append((10 * l + R_GMH, g_mh[l:l + 1, :]))
    rows.append((R_GF, g_fin.rearrange("(o d) -> o d", o=1)))
    for r, src in rows:
        S.dma('sp', work[r:r + 1, 0:2, :], src.rearrange("o (a b) -> o a b", a=2), reads=['wk0', 'wk1'], writes=[f'prow{r}'], sem='prow')
    for kc in range(KC):
        pt, pk = bank()
        a, b = divmod(kc * 128, 512)
        S.op('pe', (lambda e, pt=pt, a=a, b=b: e.transpose(out=pt[:, 0:32], in_=work[0:32, a, b:b + 128], identity=cst[0:32, C_ID:C_ID + 32])),
             reads=[f'prow{r}' for r, _ in rows] + ['cst', 'wk0', 'wk1'], writes=[pk])
        S.op('dve', (lambda e, pt=pt, kc=kc: e.tensor_copy(out=prm[:, kc, :], in_=pt[:, 0:32])), reads=[pk], writes=['prm'])
    for l in range(DEPTH):
        S.op('act', (lambda e, l=l: e.activation(out=nsp[:, l, :], in_=prm[:, :, 10 * l + R_LAM], func=AF.Exp, scale=-1.0)), reads=['prm'], writes=['nsp'])
    S.op('act', lambda e: e.activation(out=nsp[:], in_=nsp[:], func=AF.Ln, bias=1.0), reads=['nsp'], writes=['nsp'])
    S.op('dve', lambda e: e.tensor_scalar(out=nsp[:], in0=nsp[:], scalar1=-8.0, scalar2=None, op0=ALU.mult), reads=['nsp'], writes=['nsp'])
    S.dma('sp', gcol[:, 0:2], b_mi.rearrange("l h -> h l"), writes=['gcol'], sem='gcol', allow_slow_non_contiguous=True)
    S.dma('sp', gcol[:, 4:6], b_mf.rearrange("l h -> h l"), writes=['gcol'], sem='gcol', allow_slow_non_contiguous=True)
    S.op('dve', lambda e: e.tensor_scalar(out=gcol[:, 2:4], in0=gcol[:, 4:6], scalar1=-1.0, scalar2=None, op0=ALU.mult), reads=['gcol'], writes=['gcol'])
    S.op('pool', lambda e: e.memset(wbd[:], 0.0), writes=['wbd0'])
    for gi_, wsrc in enumerate((w_rg, w_ig)):
        for l in range(DEPTH):
            for half in range(2):
                src = wsrc[l].rearrange("(c two) i o -> two i c o", two=2)[half]
                base = (gi_ * DEPTH + l) * KC
                S.dma('pool', wbd[half * 64:(half + 1) * 64, base:base + KC, half * 64:(half + 1) * 64], src,
                      reads=['wbd0'], writes=[f'wbd_{gi_}{l}{half}'], sem='wbd')
    for l in range(DEPTH):
        S.dma('pool', wg[:, l, :, :], w_in[l].rearrange("(kc p) n -> p kc n", p=128)[:, :, 7168:7176], writes=['wg'], sem='wg')
    for l in range(DEPTH):
        for h in range(NH):
            S.op('dve', (lambda e, l=l, h=h: e.tensor_scalar(out=Cst[l][:, h, :, :], in0=cst[:, 0:512].rearrange("p (a b) -> p a b", a=2),
                                                           scalar1=0.0, scalar2=None, op0=ALU.mult)), reads=['cst'], writes=[f'C{l}'])
            S.op('dve', (lambda e, l=l, h=h: e.tensor_scalar(out=nrep[l][:, h, :, :], in0=cst[:, 0:256].rearrange("p (a b) -> p a b", a=2),
                                                           scalar1=0.0, scalar2=None, op0=ALU.mult)), reads=['cst'], writes=[f'n{l}'])
    S.op('dve', lambda e: e.memset(hst[:], 0.0), writes=['hst0', 'hst1'])
    S.op('dve', lambda e: e.memset(convst[:], 0.0), writes=['cv0', 'cv1'])
    S.op('dve', lambda e: e.memset(mst[:], 0.0), writes=['mst0', 'mst1'])

    wctr = [0]

    wscr = [nc.dram_tensor(f"wscr{l}", [28, 128, 4096], BF16, kind="Internal").ap() for l in range(DEPTH)]
    seen = {}

    def load_w(src, view):
        src_ap, gid = src
        i = wctr[0] % NS
        wctr[0] += 1
        a, b = view
        flat = wsl[i][:, 0:a * b]
        dst = flat.rearrange("p (a b) -> p a b", a=a)
        l = int(gid.split('_')[0])
        if (not USE_SCRATCH) or gid not in seen:
            S.dma('pool', dst, src_ap, writes=[f'W{i}', f'W{i}b'], sem=f'W{i}')
            if USE_SCRATCH:
                g = len([k for k in seen if k.startswith(f'{l}_')])
                seen[gid] = g
                S.dma('sp', wscr[l][g, :, 0:a * b], flat, reads=[f'W{i}'], writes=[f'scr{gid}'], sem=f'scrst{i}')
        else:
            g = seen[gid]
            S.dma('pool', flat, wscr[l][g, :, 0:a * b], reads=[f'scr{gid}'], writes=[f'W{i}', f'W{i}b'], sem=f'W{i}')
        return dst, f'W{i}'

    def load_w2(srcA, srcB):
        (apA, gidA), (apB, gidB) = srcA, srcB
        if not (USE_SCRATCH and PAIR_LOADS):
            return load_w(srcA, (KC, 256)), load_w(srcB, (KC, 256))
        l = int(gidA.split('_')[0])
        gid = gidA + '+' + gidB
        v3 = lambda ap_: ap_.rearrange("p (a b) -> p a b", a=KC)
        if gid not in seen:
            g = len([k for k in seen if k.startswith(f'{l}_')])
            seen[gid] = g
            outs = []
            for half, ap_ in enumerate((apA, apB)):
                i = wctr[0] % NS
                wctr[0] += 1
                flat = wsl[i][:, 0:2048]
                S.dma('pool', v3(flat), ap_, writes=[f'W{i}', f'W{i}b'], sem=f'W{i}')
                S.dma('sp', wscr[l][g, :, half * 2048:(half + 1) * 2048], flat, reads=[f'W{i}'], writes=[f'scr{gid}_{half}'], sem=f'scrst{i}')
                outs.append((v3(flat), f'W{i}'))
            return outs[0], outs[1]
        g = seen[gid]
        i = wctr[0] % NS
        wctr[0] += 1
        S.dma('pool', wsl[i][:, 0:4096], wscr[l][g, :, 0:4096], reads=[f'scr{gid}_0', f'scr{gid}_1'], writes=[f'W{i}', f'W{i}b'], sem=f'W{i}')
        return (v3(wsl[i][:, 0:2048]), f'W{i}'), (v3(wsl[i][:, 2048:4096]), f'W{i}b')

    def win_cols(l, c0, n):
        return w_in[l].rearrange("(kc p) n -> p kc n", p=128)[:, :, c0:c0 + n], f'{l}_in_{c0}'

    def wout_cols(l, c0, n):
        return w_out[l].rearrange("(kc p) n -> p kc n", p=128)[:, :, c0:c0 + n], f'{l}_out_{c0}'

    def act_sig(opf, dst, src, reads, dkey, nbias=None):
        if nbias is None:
            opf('act', (lambda e: e.activation(out=dst, in_=src, func=AF.Exp, scale=-1.0)), reads=list(reads), writes=[dkey])
        else:
            opf('act', (lambda e: e.activation(out=dst, in_=src, func=AF.Exp, scale=-1.0, bias=nbias)), reads=list(reads) + ['nprm'], writes=[dkey])
        opf('act', (lambda e: e.activation(out=dst, in_=dst, func=AF.Ln, bias=1.0)), reads=[dkey], writes=[dkey])
        opf('act', (lambda e: e.activation(out=dst, in_=dst, func=AF.Exp, scale=-1.0)), reads=[dkey], writes=[dkey])

    def act_rpow(opf, dst, src, reads, dkey, p, scale=1.0, bias=None):
        if bias is None:
            opf('act', (lambda e: e.activation(out=dst, in_=src, func=AF.Ln, scale=scale)), reads=list(reads), writes=[dkey])
        else:
            opf('act', (lambda e: e.activation(out=dst, in_=src, func=AF.Ln, scale=scale, bias=bias)), reads=list(reads), writes=[dkey])
        opf('act', (lambda e: e.activation(out=dst, in_=dst, func=AF.Exp, scale=p)), reads=[dkey], writes=[dkey])

    def rmsnorm(src_key_fn, gcol_fn, dst_fn, dst_key_fn, ntok, src_fn=None, extra=()):
        if src_fn is None:
            src_fn = lambda kc: xT[:, kc, 0:ntok]
        extra = list(extra)
        pt, pk = bank()
        for kc in range(KC):
            s_ = sq[kc % 2]
            S.op('act', (lambda e, s_=s_, kc=kc: e.activation(out=s_[:, 0:ntok], in_=src_fn(kc), func=AF.Square)),
                 reads=[src_key_fn(kc)] + extra, writes=[f'sq{kc % 2}'])
            S.op('pe', (lambda e, s_=s_, kc=kc, pt=pt: e.matmul(pt[:, 0:ntok], ones_r, s_[:, 0:ntok], start=(kc == 0), stop=(kc == KC - 1))),
                 reads=[f'sq{kc % 2}', 'ones_r'] + extra, writes=[pk])
        act_rpow(S.op, rsb[:, 0:ntok], pt[:, 0:ntok], [pk, 'epsc'] + extra, 'rsb', -0.5, scale=1.0 / D, bias=epsc[:, 0:1])
        for kc in range(KC):
            S.op('dve', (lambda e, kc=kc: e.scalar_tensor_tensor(out=dst_fn(kc), in0=src_fn(kc), scalar=gcol_fn(kc),
                                                               in1=rsb[:, 0:ntok], op0=ALU.mult, op1=ALU.mult)),
                 reads=[src_key_fn(kc), 'rsb', 'prm'], writes=[dst_key_fn(kc)])

    epsc = sb("epsc", [128, 1])
    S.op('dve', lambda e: e.memset(epsc[:], EPS), writes=['epsc'])
    onep = sb("onep", [128, 1])
    S.op('dve', lambda e: e.memset(onep[:], 1.0), writes=['onep'])
    nprm = sb("nprm", [128, KC, 32])
    S.op('dve', lambda e: e.tensor_scalar(out=nprm[:], in0=prm[:], scalar1=-1.0, scalar2=None, op0=ALU.mult), reads=['prm'], writes=['nprm'])

    def proj_fm(wv, wkey, j, ntok, ncols=128, hn_fn=None, hnk_fn=None):
        if hn_fn is None:
            hn_fn = lambda kc: hn[:, kc, 0:ntok]
            hnk_fn = lambda kc: f'hn{kc}'
        pt, pk = bank()
        for kc in range(KC):
            S.op('pe', (lambda e, kc=kc, pt=pt: e.matmul(pt[0:ncols, 0:ntok], wv[:, kc, j * 128:j * 128 + ncols], hn_fn(kc),
                                                      start=(kc == 0), stop=(kc == KC - 1))),
                 reads=[wkey, hnk_fn(kc)], writes=[pk])
        return pt, pk

    def rglru_chunk(l, c, j, wa, wak, wz, wzk):
        st = c % 2
        b0 = 7 * st
        xc, xcb, r_, i_, m_, h_, sz = (wk(b0 + 0), work[:, b0 + 1, :].bitcast(BF16)[:, 0:T], wk(b0 + 2), wk(b0 + 3), wk(b0 + 4),
                                       wk(b0 + 5), wk(b0 + 6))
        a_, u_ = r_, i_
        KM = {0: 0, 1: 1, 2: 2, 3: 3, 4: 2, 5: 4, 6: 3, 7: 5, 8: 6}
        K = lambda n: (WHATIF.get('rg', 'wk') + f'{b0 + KM[n]}')
        ex = ext[st]
        exk = f'ext{st}'
        cvk = f'cv{l}'
        pX, pXk = proj_fm(wa, wak, j, T)
        pZ, pZk = proj_fm(wz, wzk, j, T)
        S.op('dve', (lambda e, ex=ex, c=c: e.tensor_copy(out=ex[:, 0:3], in_=convst[:, l * KC + c, :])), reads=[cvk], writes=[exk])
        S.op('act', (lambda e, ex=ex, pX=pX: e.activation(out=ex[:, 3:T + 3], in_=pX[:, :], func=AF.Copy)), reads=[pXk], writes=[exk])
        act_sig(S.op, sz, pZ[:, :], [pZk], K(8))
        S.op('dve', (lambda e, sz=sz, pZ=pZ: e.tensor_tensor(out=sz, in0=sz, in1=pZ[:, :], op=ALU.mult)), reads=[pZk, K(8)], writes=[K(8)])
        cw = lambda jj, c=c: prm[:, c, 10 * l + R_CW + jj:10 * l + R_CW + jj + 1]
        S.op('dve', (lambda e, ex=ex, xc=xc, cw=cw, c=c: e.tensor_scalar(out=xc, in0=ex[:, 0:T], scalar1=cw(0),
                                                                        scalar2=prm[:, c, 10 * l + R_CB:10 * l + R_CB + 1], op0=ALU.mult, op1=ALU.add)),
             reads=[exk, 'prm'], writes=[K(0)])
        for jj in range(1, 4):
            S.op('dve', (lambda e, ex=ex, xc=xc, cw=cw, jj=jj: e.scalar_tensor_tensor(out=xc, in0=ex[:, jj:jj + T], scalar=cw(jj), in1=xc,
                                                                                     op0=ALU.mult, op1=ALU.add)),
                 reads=[exk, 'prm', K(0)], writes=[K(0)])
        S.op('dve', (lambda e, ex=ex, c=c: e.tensor_copy(out=convst[:, l * KC + c, :], in_=ex[:, T:T + 3])), reads=[exk], writes=[cvk])
        S.op(WHATIF.get('xcb_eng', 'dve'), ((lambda e, xc=xc, xcb=xcb: e.activation(out=xcb, in_=xc, func=AF.Copy)) if WHATIF.get('xcb_eng', 'dve') == 'act' else (lambda e, xc=xc, xcb=xcb: e.tensor_copy(out=xcb, in_=xc))), reads=[K(0)], writes=[K(1)])
        pR, pRk = bank()
        S.op('pe', (lambda e, pR=pR, xcb=xcb, c=c: e.matmul(pR[:, :], wbd[:, (0 * DEPTH + l) * KC + c, :], xcb, start=True, stop=True)),
             reads=['wbd0'] + [f'wbd_{a_}{b_}{c_}' for a_ in range(2) for b_ in range(2) for c_ in range(2)] + [K(1)], writes=[pRk])
        pI, pIk = bank()
        S.op('pe', (lambda e, pI=pI, xcb=xcb, c=c: e.matmul(pI[:, :], wbd[:, (1 * DEPTH + l) * KC + c, :], xcb, start=True, stop=True)),
             reads=['wbd0'] + [f'wbd_{a_}{b_}{c_}' for a_ in range(2) for b_ in range(2) for c_ in range(2)] + [K(1)], writes=[pIk])
        act_sig(S.op, r_, pR[:, :], [pRk], K(2), nbias=nprm[:, c, 10 * l + R_BR:10 * l + R_BR + 1])
        act_sig(S.op, i_, pI[:, :], [pIk], K(3), nbias=nprm[:, c, 10 * l + R_BI:10 * l + R_BI + 1])
        S.op('act', (lambda e, a_=a_, r_=r_, c=c: e.activation(out=a_, in_=r_, func=AF.Exp, scale=nsp[:, l, c:c + 1])), reads=[K(2), 'nsp'], writes=[K(4)])
        S.op(WHATIF.get('sq_eng', 'dve'), ((lambda e, a_=a_, m_=m_: e.activation(out=m_, in_=a_, func=AF.Square)) if WHATIF.get('sq_eng', 'dve') == 'act' else (lambda e, a_=a_, m_=m_: e.tensor_tensor(out=m_, in0=a_, in1=a_, op=ALU.mult))), reads=[K(4)], writes=[K(5)])
        act_rpow(S.op, m_, m_, [K(5), 'onep'], K(5), 0.5, scale=-1.0, bias=onep[:, 0:1])
        S.op('dve', (lambda e, u_=u_, i_=i_, xc=xc: e.tensor_tensor(out=u_, in0=i_, in1=xc, op=ALU.mult)), reads=[K(3), K(0)], writes=[K(6)])
        S.op('dve', (lambda e, u_=u_, m_=m_: e.tensor_tensor(out=u_, in0=u_, in1=m_, op=ALU.mult)), reads=[K(6), K(5)], writes=[K(6)])
        S.op('dve', (lambda e, h_=h_, a_=a_, u_=u_, c=c: e.tensor_tensor_scan(out=h_, data0=a_, data1=u_, initial=hst[:, l, c:c + 1],
                                                                            op0=ALU.mult, op1=ALU.add)),
             reads=[K(4), K(6), f'hst{l}'], writes=[K(7)])
        S.op('dve', (lambda e, h_=h_, c=c: e.tensor_copy(out=hst[:, l, c:c + 1], in_=h_[:, T - 1:T])), reads=[K(7)], writes=[f'hst{l}'])
        S.op('dve', (lambda e, h_=h_, sz=sz, c=c: e.tensor_tensor(out=mg[:, c, :], in0=h_, in1=sz, op=ALU.mult)), reads=[K(7), K(8)], writes=[f'mg{c}'])


    def mlstm_head(l, h):
        ig, l1, bcs, Mx, gi, en = [g_[:] for g_ in grow]
        LM = {0: 'R0', 1: 'R1', 2: 'R2', 3: 'R3', 4: 'R4', 5: 'R5', 6: 'F0', 7: 'F1', 8: 'F2', 9: 'F3', 10: 'F4', 11: 'F5', 12: 'F6',
              13: 'F7', 14: 'F8', 15: 'F9', 16: 'R9', 17: 'R10', 18: 'R11', 19: 'R12', 20: 'F10', 21: 'F11', 22: 'F12', 23: 'R13', 24: 'R14',
              25: 'F13', 26: 'R6', 27: 'R7', 28: 'R8'}
        W = lambda n: ('wk' if LM[n][0] == 'F' else 'wr') + LM[n][1:] + (f'_h{h % 2}' if (WHATIF.get('mlproj') and n < 10) or WHATIF.get('mlall') else '')

        def B(n, f32=False):
            i = int(LM[n][1:])
            if LM[n][0] == 'F':
                return work[:, i, :]
            return wkrf(i) if f32 else workr[:, i, :]
        qT = lambda dkc: B(0 + dkc)
        kT = lambda dkc: B(2 + dkc)
        vtok = lambda j: B(4 + j // 2)[:, (j % 2) * 256:(j % 2) * 256 + 256]
        so = lambda dvc: B(6 + dvc)
        szb = lambda dvc: B(8 + dvc)
        Mb, gib, enb = B(10), B(11), B(12)
        wTb = [(13, 0), (14, 0), (15, 0), (15, 256)]
        PTb = [(26, 0), (27, 0), (28, 0), (28, 256)]
        wT = lambda j: B(wTb[j][0])[:, wTb[j][1]:wTb[j][1] + (T - 128 * j)]
        PT = lambda j: B(PTb[j][0])[:, PTb[j][1]:PTb[j][1] + (T - 128 * j)]
        qg = lambda dkc: B(16 + dkc)
        kd = lambda j: B(18 + j // 2)[:, (j % 2) * 256:(j % 2) * 256 + 256]
        rden = B(20)
        yB = lambda dvc: B(21 + dvc)
        ysq = lambda dvc: B(23 + dvc)
        rsh = B(25)

        (wq, wqk), (wkk_, wkk) = load_w2(win_cols(l, 2048 + 256 * h, 256), win_cols(l, 3072 + 256 * h, 256))
        for dkc in range(2):
            pt, pk = proj_fm(wq, wqk, dkc, T)
            S.op('dve', (lambda e, pt=pt, dkc=dkc: e.tensor_copy(out=qT(dkc), in_=pt[:, :])), reads=[pk], writes=[W(0 + dkc)])
        for dkc in range(2):
            pt, pk = proj_fm(wkk_, wkk, dkc, T)
            S.op('act', (lambda e, pt=pt, dkc=dkc: e.activation(out=kT(dkc), in_=pt[:, :], func=AF.Copy, scale=DK ** -0.5)), reads=[pk], writes=[W(2 + dkc)])
        (wv_, wvk), (wo_, wok) = load_w2(win_cols(l, 4096 + 256 * h, 256), win_cols(l, 5120 + 256 * h, 256))
        for j in range(4):
            pt, pk = bank()
            for kc in range(KC):
                S.op('pe', (lambda e, kc=kc, pt=pt, j=j: e.matmul(pt[:, 0:256], hn[:, kc, j * 128:(j + 1) * 128], wv_[:, kc, :],
                                                               start=(kc == 0), stop=(kc == KC - 1))),
                     reads=[wvk, f'hn{kc}'], writes=[pk])
            S.op('act', (lambda e, pt=pt, j=j: e.activation(out=vtok(j), in_=pt[:, 0:256], func=AF.Copy)), reads=[pk], writes=[W(4 + j // 2)])
        wzb_, wzbk = load_w(win_cols(l, 6144 + 256 * h, 256), (KC, 256))
        for dvc in range(2):
            pt, pk = proj_fm(wo_, wok, dvc, T)
            act_sig(S.op, so(dvc), pt[:, :], [pk], W(6 + dvc))
        for dvc in range(2):
            pt, pk = proj_fm(wzb_, wzbk, dvc, T)
            act_sig(S.op, szb(dvc), pt[:, :], [pk], W(8 + dvc))
            S.op('dve', (lambda e, pt=pt, dvc=dvc: e.tensor_tensor(out=szb(dvc), in0=szb(dvc), in1=pt[:, :], op=ALU.mult)), reads=[pk, W(8 + dvc)], writes=[W(8 + dvc)])
        sel = selr[:, 128 * h:128 * h + 128]
        for src, sk, dst, dkey in ((Mx, 'g3', Mb, W(10)), (gi, 'g4', gib, W(11)), (en, 'g5', enb, W(12))):
            pt, pk = bank()
            S.op('pe', (lambda e, pt=pt, src=src: e.matmul(pt[:, :], sel, src, start=True, stop=True)), reads=[sk, 'selr'], writes=[pk])
            S.op('dve', (lambda e, pt=pt, dst=dst: e.tensor_copy(out=dst, in_=pt[:, :])), reads=[pk], writes=[dkey])
        psS = []
        for j in range(4):
            Nj = T - 128 * j
            t0 = 128 * j
            pt, pk = bank()
            for dkc in range(2):
                S.op('pe', (lambda e, pt=pt, dkc=dkc, t0=t0, Nj=Nj: e.matmul(pt[:, 0:Nj], kT(dkc)[:, t0:t0 + 128], qT(dkc)[:, t0:T],
                                                                           start=(dkc == 0), stop=(dkc == 1))),
                     reads=[W(0 + dkc), W(2 + dkc)], writes=[pk])
            wkey = W(wTb[j][0])
            S.op('act', (lambda e, j=j, t0=t0: e.activation(out=wT(j), in_=Mb[:, t0:T], func=AF.Exp, scale=-1.0, bias=acol[:, 4 * j + h:4 * j + h + 1])),
                 reads=[W(10), 'acol'], writes=[wkey])
            S.op('dve', (lambda e, j=j: e.tensor_tensor(out=wT(j)[:, 0:128], in0=wT(j)[:, 0:128], in1=mask01, op=ALU.mult)),
                 reads=[wkey, 'cst'], writes=[wkey])
            S.op('dve', (lambda e, j=j, Nj=Nj: e.tensor_copy(out=dec[:, j:j + 1], in_=wT(j)[:, Nj - 1:Nj])), reads=[wkey], writes=['dec'])
            S.op('dve', (lambda e, j=j, pt=pt, Nj=Nj: e.tensor_tensor(out=PT(j), in0=pt[:, 0:Nj], in1=wT(j), op=ALU.mult)),
                 reads=[pk, wkey], writes=[W(PTb[j][0])])
        for dkc in range(2):
            S.op('dve', (lambda e, dkc=dkc: e.tensor_tensor(out=qg(dkc), in0=B(0 + dkc, True), in1=gib, op=ALU.mult)),
                 reads=[W(0 + dkc), W(11)], writes=[W(16 + dkc)])
        Ck, nk = f'C{l}', f'n{l}'
        pN = []
        for dvc in range(2):
            pt, pk = bank()
            pN.append((pt, pk))
            for dkc in range(2):
                S.op('pe', (lambda e, pt=pt, dkc=dkc, dvc=dvc: e.matmul(pt[:, :], Cst[l][:, h, dkc, dvc * 128:(dvc + 1) * 128], qg(dkc),
                                                                      start=(dkc == 0), stop=False)),
                     reads=[Ck, W(16 + dkc)], writes=[pk])
            for j in range(4):
                S.op('pe', (lambda e, pt=pt, j=j, dvc=dvc: e.matmul(pt[:, 128 * j:T], vtok(j)[:, dvc * 128:(dvc + 1) * 128], PT(j),
                                                                  start=False, stop=(j == 3))),
                     reads=[W(4 + j // 2), W(PTb[j][0])], writes=[pk])
        pD, pDk = bank()
        for dkc in range(2):
            S.op('pe', (lambda e, dkc=dkc: e.matmul(pD[:, :], nrep[l][:, h, dkc, :], qg(dkc), start=(dkc == 0), stop=False)),
                 reads=[nk, W(16 + dkc)], writes=[pDk])
        for j in range(4):
            S.op('pe', (lambda e, j=j: e.matmul(pD[:, 128 * j:T], ones_r, PT(j), start=False, stop=(j == 3))),
                 reads=['ones_r', W(PTb[j][0])], writes=[pDk])
        S.op('dve', lambda e: e.tensor_tensor(out=rden, in0=pD[:, :], in1=enb, op=ALU.max), reads=[pDk, W(12)], writes=[W(20)])
        S.op('dve', lambda e: e.scalar_tensor_tensor(out=rden, in0=pD[:, :], scalar=-1.0, in1=rden, op0=ALU.mult, op1=ALU.max), reads=[pDk, W(20)], writes=[W(20)])
        act_rpow(S.op, rden, rden, [W(20)], W(20), -1.0)
        for dvc in range(2):
            pt, pk = pN[dvc]
            S.op('dve', (lambda e, pt=pt, dvc=dvc: e.tensor_tensor(out=yB(dvc), in0=pt[:, :], in1=rden, op=ALU.mult)), reads=[pk, W(20)], writes=[W(21 + dvc)])
            S.op('dve', (lambda e, dvc=dvc: e.tensor_tensor(out=yB(dvc), in0=yB(dvc), in1=so(dvc), op=ALU.mult)), reads=[W(21 + dvc), W(6 + dvc)], writes=[W(21 + dvc)])
            S.op('act', (lambda e, dvc=dvc: e.activation(out=ysq(dvc), in_=yB(dvc), func=AF.Square)), reads=[W(21 + dvc)], writes=[W(23 + dvc)])
        pQ, pQk = bank()
        for dvc in range(2):
            S.op('pe', (lambda e, dvc=dvc: e.matmul(pQ[:, :], ones_r, ysq(dvc), start=(dvc == 0), stop=(dvc == 1))), reads=['ones_r', W(23 + dvc)], writes=[pQk])
        act_rpow(S.op, rsh, pQ[:, :], [pQk, 'epsc'], W(25), -0.5, scale=1.0 / DK, bias=epsc[:, 0:1])
        for dvc in range(2):
            fc = 2 * h + dvc
            S.op('dve', (lambda e, dvc=dvc, fc=fc: e.scalar_tensor_tensor(out=yB(dvc), in0=yB(dvc), scalar=prm[:, fc, 10 * l + R_GMH:10 * l + R_GMH + 1],
                                                                        in1=rsh, op0=ALU.mult, op1=ALU.mult)),
                 reads=[W(21 + dvc), W(25), 'prm'], writes=[W(21 + dvc)])
            S.op('dve', (lambda e, dvc=dvc, fc=fc: e.tensor_tensor(out=mg[:, 8 + fc, :], in0=yB(dvc), in1=szb(dvc), op=ALU.mult)),
                 reads=[W(21 + dvc), W(8 + dvc)], writes=[f'mg{8 + fc}'])
        for j in range(4):
            pt, pk = bank()
            for dkc in range(2):
                S.op('pe', (lambda e, pt=pt, j=j, dkc=dkc: e.transpose(out=pt[:, dkc * 128:(dkc + 1) * 128], in_=B(2 + dkc, True)[:, j * 128:(j + 1) * 128],
                                                                     identity=ident)),
                     reads=[W(2 + dkc), 'cst'], writes=[pk])
            S.op('dve', (lambda e, pt=pt, j=j: e.tensor_scalar(out=kd(j), in0=pt[:, 0:256], scalar1=dec[:, j:j + 1], scalar2=None, op0=ALU.mult)),
                 reads=[pk, 'dec'], writes=[W(18 + j // 2)])
        pC, pCk = bank()
        for dkc in range(2):
            for j in range(4):
                S.op('pe', (lambda e, j=j, dkc=dkc: e.matmul(pC[:, dkc * 256:(dkc + 1) * 256], kd(j)[:, dkc * 128:(dkc + 1) * 128], vtok(j),
                                                           start=(j == 0), stop=(j == 3))),
                     reads=[W(18 + j // 2), W(4 + j // 2)], writes=[pCk])
        pNn, pNnk = bank()
        for dkc in range(2):
            for j in range(4):
                S.op('pe', (lambda e, j=j, dkc=dkc: e.matmul(pNn[:, dkc * 256:(dkc + 1) * 256], kd(j)[:, dkc * 128:(dkc + 1) * 128], ones_w[:, :],
                                                           start=(j == 0), stop=(j == 3))),
                     reads=[W(18 + j // 2), 'ones_r'], writes=[pNnk])
        for dkc in range(2):
            S.op('dve', (lambda e, dkc=dkc: e.scalar_tensor_tensor(out=Cst[l][:, h, dkc, :], in0=Cst[l][:].bitcast(F32)[:, h, dkc, :], scalar=gib[:, T - 1:T],
                                                                  in1=pC[:, dkc * 256:(dkc + 1) * 256], op0=ALU.mult, op1=ALU.add)),
                 reads=[Ck, W(11), pCk], writes=[Ck])
            S.op('dve', (lambda e, dkc=dkc: e.scalar_tensor_tensor(out=nrep[l][:, h, dkc, :], in0=nrep[l][:].bitcast(F32)[:, h, dkc, :], scalar=gib[:, T - 1:T],
                                                                  in1=pNn[:, dkc * 256:dkc * 256 + 128], op0=ALU.mult, op1=ALU.add)),
                 reads=[nk, W(11), pNnk], writes=[nk])


    def layer_prompt(l, tt):
        xk = lambda kc: f'x{kc}'
        rmsnorm(xk, lambda kc: prm[:, kc, 10 * l + R_GN:10 * l + R_GN + 1], lambda kc: hn[:, kc, :], lambda kc: f'hn{kc}', T)
        ig, l1, bcs, Mx, gi, en = [g[:] for g in grow]
        pgi, pgik = bank()
        for kc in range(KC):
            S.op('pe', (lambda e, kc=kc: e.matmul(pgi[0:4, :], wg[:, l, kc, 0:4], hn[:, kc, :], start=(kc == 0), stop=(kc == KC - 1))),
                 reads=['wg', f'hn{kc}'], writes=[pgik])
        pgf, pgfk = bank()
        for kc in range(KC):
            S.op('pe', (lambda e, kc=kc: e.matmul(pgf[0:4, :], wg[:, l, kc, 4:8], hn[:, kc, :], start=(kc == 0), stop=(kc == KC - 1))),
                 reads=['wg', f'hn{kc}'], writes=[pgfk])
        S.op('act', lambda e: e.activation(out=ig, in_=pgi[0:4, :], func=AF.Identity, bias=gcol[:, l:l + 1]), reads=[pgik, 'gcol'], writes=['g0'])
        S.op('act', lambda e: e.activation(out=l1, in_=pgf[0:4, :], func=AF.Exp, scale=-1.0, bias=gcol[:, 2 + l:3 + l]), reads=[pgfk, 'gcol'], writes=['g1'])
        S.op('act', lambda e: e.activation(out=l1, in_=l1, func=AF.Ln, bias=1.0), reads=['g1'], writes=['g1'])
        S.op('dve', lambda e: e.tensor_tensor_scan(out=bcs, data0=ones4, data1=l1, initial=0.0, op0=ALU.mult, op1=ALU.subtract),
             reads=['g1', 'cst'], writes=['g2'])
        S.op('dve', lambda e: e.tensor_tensor(out=ig, in0=ig, in1=bcs, op=ALU.subtract), reads=['g0', 'g2'], writes=['g0'])
        S.op('dve', lambda e: e.tensor_tensor_scan(out=Mx, data0=ig, data1=ig, initial=mst[:, l:l + 1], op0=ALU.max, op1=ALU.max),
             reads=['g0', f'mst{l}'], writes=['g3'])
        MxF = grow[3][:].bitcast(F32)
        S.op('act', lambda e: e.activation(out=gi, in_=MxF, func=AF.Exp, scale=-1.0, bias=mst[:, l:l + 1]), reads=['g3', f'mst{l}'], writes=['g4'])
        S.op('dve', lambda e: e.tensor_tensor(out=bcs, in0=bcs, in1=MxF, op=ALU.add), reads=['g2', 'g3'], writes=['g2'])
        S.op('act', lambda e: e.activation(out=en, in_=bcs, func=AF.Exp, scale=-1.0), reads=['g2'], writes=['g5'])
        S.op('dve', lambda e: e.tensor_copy(out=mst[:, l:l + 1], in_=bcs[:, T - 1:T]), reads=['g2'], writes=[f'mst{l}'])
        pa, pak = bank()
        for j in range(4):
            S.op('pe', (lambda e, j=j: e.transpose(out=pa[:, 4 * j:4 * j + 4], in_=ig[:, j * 128:(j + 1) * 128], identity=cst[0:4, C_ID:C_ID + 4])),
                 reads=['g0', 'cst'], writes=[pak])
        S.op('dve', lambda e: e.tensor_copy(out=acol[:], in_=pa[:, 0:16]), reads=[pak], writes=['acol'])

        for g in range(2):
            wa, wak = load_w(win_cols(l, 512 * g, 512), (KC, 512))
            wz, wzk = load_w(win_cols(l, 1024 + 512 * g, 512), (KC, 512))
            for j in range(4):
                rglru_chunk(l, 4 * g + j, j, wa, wak, wz, wzk)
        for h in range(NH):
            mlstm_head(l, h)
        for g4 in range(4):
            wo4, wo4k = load_w(wout_cols(l, 256 * g4, 256), (16, 256))
            for dd in range(2):
                dc = 2 * g4 + dd
                pt, pk = bank()
                for kc in range(16):
                    S.op('pe', (lambda e, pt=pt, kc=kc, dd=dd, wo4=wo4: e.matmul(pt[:, :], wo4[:, kc, dd * 128:(dd + 1) * 128], mg[:, kc, :],
                                                                      start=(kc == 0), stop=(kc == 15))),
                         reads=[wo4k, f'mg{kc}'], writes=[pk])
                S.op('dve', (lambda e, pt=pt, dc=dc: e.tensor_tensor(out=xT[:, dc, :], in0=xT[:, dc, :], in1=pt[:, :], op=ALU.add)),
                     reads=[pk, f'x{dc}'], writes=[f'x{dc}'])

    def y_store(tt):
        for j in range(4):
            r0 = tt * T + j * 128
            for half in range(2):
                key = f'yout{half}'
                pt, pk = bank()
                for q in range(4):
                    kc = half * 4 + q
                    S.op('pe', (lambda e, pt=pt, q=q, kc=kc, j=j: e.transpose(out=pt[:, q * 128:(q + 1) * 128], in_=wk(kc)[:, j * 128:(j + 1) * 128], identity=ident)),
                         reads=[f'wk{kc}', 'cst'], writes=[pk])
                S.op('act', (lambda e, pt=pt, half=half: e.activation(out=yout[:, half * 512:(half + 1) * 512], in_=pt[:, :], func=AF.Copy)),
                     reads=[pk], writes=[key])
                S.dma('act', y_p[r0:r0 + 128, half * 512:(half + 1) * 512], yout[:, half * 512:(half + 1) * 512], reads=[key], sem=key)

    for tt in range(NT):
        for l in range(DEPTH):
            layer_prompt(l, tt)
        rmsnorm(lambda kc: f'x{kc}', lambda kc: prm[:, kc, R_GF:R_GF + 1], lambda kc: wk(kc), lambda kc: f'wk{kc}', T)
        if tt + 1 < NT:
            x_load(tt + 1)
        y_store(tt)

    for l in range(DEPTH):
        S.dma('sp', p_h[l].rearrange("(kc p) -> p kc", p=128), hst[:, l, :], reads=[f'hst{l}'], sem='ost', allow_slow_non_contiguous=True)
        for j in range(3):
            S.dma('sp', p_conv[l, j].rearrange("(kc p) -> p kc", p=128), convst[:, l * KC:(l + 1) * KC, j], reads=[f'cv{l}'], sem='ost', allow_slow_non_contiguous=True)
        S.dma('sp', p_C[l].rearrange("h (dkc p) v -> p h dkc v", p=128), Cst[l][:].bitcast(F32), reads=[f'C{l}'], sem='ost')
        S.dma('sp', p_n[l].rearrange("h (dkc p) -> p h dkc", p=128), nrep[l][:].bitcast(F32)[:, :, :, 0], reads=[f'n{l}'], sem='ost', allow_slow_non_contiguous=True)
        S.dma('sp', p_m[l].rearrange("(h o) -> h o", o=1), mst[:, l:l + 1], reads=[f'mst{l}'], sem='ost', allow_slow_non_contiguous=True)

    x_s = din("x_s", [DB, D]); st_h = din("st_h", [DEPTH, DB, D]); st_conv = din("st_conv", [DEPTH, DB, 3, D])
    st_C = din("st_C", [DEPTH, DB, NH, DK, DK]); st_n = din("st_n", [DEPTH, DB, NH, DK]); st_m = din("st_m", [DEPTH, DB, NH])
    gb_t = din("gb_t", [DB, 16])
    emask = din("emask", [128, 256])
    y_s = dout("y_s", [DB, D]); s_h = dout("s_h", [DEPTH, DB, D]); s_conv = dout("s_conv", [DEPTH, DB, 3, D])
    s_C = dout("s_C", [DEPTH, DB, NH, DK, DK]); s_n = dout("s_n", [DEPTH, DB, NH, DK]); s_m = dout("s_m", [DEPTH, DB, NH])

    N = DB
    x_d = sb("x_d", [128, KC, N])
    hn_d = sb("hn_d", [128, KC, N], BF16)
    mg_d = sb("mg_d", [128, 16, N], BF16)
    xcbd = sb("xcbd", [128, 2, N], BF16)
    gt = sb("gt", [N, 64])
    gbt = sb("gbt", [N, 16])

    XF = xT[:].rearrange("p a b -> p (a b)")
    MF = mg[:].bitcast(F32).rearrange("p a b -> p (a b)")
    HF = hn[:].bitcast(F32).rearrange("p a b -> p (a b)")
    cs_tok = XF[0:N, 0:3072]
    h0_tok = XF[0:N, 3072:4096]
    k_tok = MF[0:N, 0:1024]
    v_tok = MF[0:N, 1024:2048]
    n_tok = MF[0:N, 2048:3072]
    otok = MF[0:N, 3072:4096]
    rhs3 = HF[0:N, 512:560]
    Em = HF[:, 560:816].rearrange("p (b c) -> p b c", b=N)
    qm = xT[:].bitcast(BF16).rearrange("p a b -> p (a b)")[:, 0:2048].rearrange("p (g b c) -> p g b c", g=8, b=N)
    cq_tok = MF[0:N, 1024:2048]
    A_ = xin[:]
    B_ = yout[:]
    convT = A_[:, 0:384].rearrange("p (j c b) -> p j c b", j=3, c=KC)
    h0T = A_[:, 384:512].rearrange("p (c b) -> p c b", c=KC)
    xaT = A_[:, 512:640].rearrange("p (c b) -> p c b", c=KC)
    hT = A_[:, 640:768].rearrange("p (c b) -> p c b", c=KC)
    tmp = lambda i, n=1: A_[:, 768 + 16 * i:768 + 16 * (i + n)]
    qTd = B_[:, 0:128].rearrange("p (h c b) -> p h c b", h=NH, c=2)
    kTd = B_[:, 128:256].rearrange("p (h c b) -> p h c b", h=NH, c=2)
    vTd = B_[:, 256:384].rearrange("p (h c b) -> p h c b", h=NH, c=2)
    nTd = B_[:, 384:512].rearrange("p (h c b) -> p h c b", h=NH, c=2)
    sod = B_[:, 512:640].rearrange("p (h c b) -> p h c b", h=NH, c=2)
    szbd = B_[:, 640:768].rearrange("p (h c b) -> p h c b", h=NH, c=2)
    CqT = B_[:, 768:896].rearrange("p (h c b) -> p h c b", h=NH, c=2)
    bc = rsb[:, 64:384].rearrange("p (h c) -> p h c", h=NH)
    id16 = cst[0:N, C_ID:C_ID + N]
    ones16 = cst[0:N, C_ONE:C_ONE + 128]
    onesf = cst[:, C_ONE:C_ONE + 128]
    NCB = 3
    Cin = [work[:, 4 * i:4 * i + 4, :].rearrange("p a (c v) -> p (a c) v", c=2).rearrange("p (h c) v -> p h c v", h=NH) for i in range(NCB)]
    Cbf = [work[:, 12 + 2 * i:14 + 2 * i, :].bitcast(BF16).rearrange("p a (c v) -> p (a c) v", c=4).rearrange("p (h c) v -> p h c v", h=NH) for i in range(2)]
    kw_r = workr[0:N, 0:2, :].rearrange("p a b -> p (a b)")
    kw_f = workr[:].bitcast(F32)[0:N, 0:2, :].rearrange("p a b -> p (a b)")
    vm = [workr[0:N, 2 + i // 2, (i % 2) * 256:(i % 2) * 256 + 256] for i in range(4)]

    allkeys = ([f'x{k}' for k in range(KC)] + [f'hn{k}' for k in range(KC)] + [f'mg{k}' for k in range(16)] +
               [f'wk{i}' for i in range(NWK)] + [f'wr{i}' for i in range(15)] + ['xin0', 'xin1', 'yout0', 'yout1', 'ext0', 'ext1', 'rsb'])
    S.mark = len(S.ins)
    S.op('dve', lambda e: e.memset(gt[:, 0:1], 0.0), writes=allkeys + ['dbar'])

    def dop(eng, fn, reads=(), writes=()):
        S.op(eng, fn, list(reads) + ['dbar'], writes)

    def ddma(eng, out, in_, reads=(), writes=(), sem=None, **kw):
        S.dma(eng, out, in_, list(reads) + ['dbar'], writes, sem, **kw)

    hnd_fn = lambda kc: hn_d[:, kc, :]
    hndk_fn = lambda kc: 'hnd'

    def to_fm(src_tok, skey, dst3, dkey, nch=KC):
        pt, pk = bank()
        for kc in range(nch):
            dop('pe', (lambda e, kc=kc, pt=pt: e.transpose(out=pt[:, kc * N:(kc + 1) * N], in_=src_tok[:, kc * 128:(kc + 1) * 128], identity=id16)),
                reads=[skey, 'cst'], writes=[pk])
        dop('act', (lambda e, pt=pt: e.activation(out=dst3, in_=pt[:, 0:nch * N].rearrange("p (c b) -> p c b", c=nch), func=AF.Copy)),
            reads=[pk], writes=[dkey])

    def to_tok(src3, skey, dst_tok, dkey):
        for half in range(2):
            pt, pk = bank()
            for q in range(4):
                dop('pe', (lambda e, q=q, pt=pt, half=half: e.transpose(out=pt[0:N, q * 128:(q + 1) * 128], in_=src3[:, half * 4 + q, :], identity=ident)),
                    reads=[skey, 'cst'], writes=[pk])
            dop('act', (lambda e, pt=pt, half=half: e.activation(out=dst_tok[:, half * 512:(half + 1) * 512], in_=pt[0:N, :], func=AF.Copy)),
                reads=[pk], writes=[dkey])

    def decode_layer(l):
        rmsnorm(lambda kc: 'xd', lambda kc: prm[:, kc, 10 * l + R_GN:10 * l + R_GN + 1], hnd_fn, hndk_fn, N,
                src_fn=lambda kc: x_d[:, kc, :], extra=['dbar'])
        ddma('sp', cs_tok.rearrange("p (j d) -> p j d", j=3), st_conv[l], writes=['d_cs'] + [f'd_qm{b_}_{g_}' for b_ in range(N) for g_ in range(8)], sem='d_cs')
        ddma('sp', h0_tok, st_h[l], writes=['d_h0'], sem='d_h0')
        ddma('sp', n_tok.rearrange("p (h k) -> p h k", h=NH), st_n[l], writes=['d_n'], sem='d_n')
        ddma('sp', gt[:, 0:4], st_m[l], writes=['d_mp'], sem='d_mp')
        ddma('sp', s_conv[l, :, 0:2, :], st_conv[l, :, 1:3, :], sem='d_d2d')
        for j in range(3):
            to_fm(cs_tok[:, j * D:(j + 1) * D], 'd_cs', convT[:, j, :, :], 'd_convT')
        to_fm(h0_tok, 'd_h0', h0T, 'd_h0T')
        to_fm(n_tok, 'd_n', nTd.rearrange("p h c b -> p (h c) b"), 'd_nT')
        pG, pGk = bank()
        for kc in range(KC):
            dop('pe', (lambda e, kc=kc: e.matmul(pG[0:N, 0:8], hn_d[:, kc, :], wg[:, l, kc, :], start=(kc == 0), stop=(kc == KC - 1))),
                reads=['hnd', 'wg'], writes=[pGk])
        G = lambda a: gt[:, a:a + 4]
        mp, ig, l1, gg, mt, w_, gi_, en_, t_ = G(0), G(4), G(8), G(12), G(16), G(20), G(24), G(28), G(32)
        dop('dve', lambda e: e.tensor_tensor(out=ig, in0=pG[0:N, 0:4], in1=gbt[:, 4 * l:4 * l + 4], op=ALU.add), reads=[pGk, 'd_gbt'], writes=['d_gt'])
        dop('dve', lambda e: e.tensor_tensor(out=l1, in0=pG[0:N, 4:8], in1=gbt[:, 8 + 4 * l:12 + 4 * l], op=ALU.add), reads=[pGk, 'd_gbt', 'd_gt'], writes=['d_gt'])
        dop('act', lambda e: e.activation(out=l1, in_=l1, func=AF.Exp, scale=-1.0), reads=['d_gt'], writes=['d_gt'])
        dop('act', lambda e: e.activation(out=l1, in_=l1, func=AF.Ln, bias=1.0), reads=['d_gt'], writes=['d_gt'])
        dop('dve', lambda e: e.tensor_tensor(out=gg, in0=mp, in1=l1, op=ALU.subtract), reads=['d_gt', 'd_mp'], writes=['d_gt'])
        dop('dve', lambda e: e.tensor_tensor(out=mt, in0=gg, in1=ig, op=ALU.max), reads=['d_gt'], writes=['d_gt'])
        dop('dve', lambda e: e.tensor_tensor(out=t_, in0=ig, in1=mt, op=ALU.subtract), reads=['d_gt'], writes=['d_gt'])
        dop('act', lambda e: e.activation(out=w_, in_=t_, func=AF.Exp), reads=['d_gt'], writes=['d_gt'])
        dop('dve', lambda e: e.tensor_tensor(out=t_, in0=gg, in1=mt, op=ALU.subtract), reads=['d_gt'], writes=['d_gt'])
        dop('act', lambda e: e.activation(out=gi_, in_=t_, func=AF.Exp), reads=['d_gt'], writes=['d_gt'])
        dop('act', lambda e: e.activation(out=en_, in_=mt, func=AF.Exp, scale=-1.0), reads=['d_gt'], writes=['d_gt'])
        ddma('sp', s_m[l], mt, reads=['d_gt'], sem='d_sm')

        def rg_chunk(c, j, wa, wak, wz, wzk):
            tb = 7 * (c % 2)
            TK = lambda i: f'd_t{tb + i}'
            sz, xc, r_, i_, m_ = tmp(tb + 0), tmp(tb + 1), tmp(tb + 2), tmp(tb + 3), tmp(tb + 4)
            xb = xcbd[:, c % 2, :]
            xbk = f'd_xcb{c % 2}'
            pX, pXk = proj_fm(wa, wak, j, N, hn_fn=hnd_fn, hnk_fn=hndk_fn)
            pZ, pZk = proj_fm(wz, wzk, j, N, hn_fn=hnd_fn, hnk_fn=hndk_fn)
            dop('act', lambda e: e.activation(out=xaT[:, c, :], in_=pX[:, 0:N], func=AF.Copy), reads=[pXk], writes=['d_xaT'])
            act_sig(dop, sz, pZ[:, 0:N], [pZk], TK(0))
            dop('dve', lambda e: e.tensor_tensor(out=sz, in0=sz, in1=pZ[:, 0:N], op=ALU.mult), reads=[pZk, TK(0)], writes=[TK(0)])
            cw = lambda jj: prm[:, c, 10 * l + R_CW + jj:10 * l + R_CW + jj + 1]
            dop('dve', lambda e: e.tensor_scalar(out=xc, in0=convT[:, 0, c, :], scalar1=cw(0), scalar2=prm[:, c, 10 * l + R_CB:10 * l + R_CB + 1],
                                                op0=ALU.mult, op1=ALU.add), reads=['d_convT', 'prm'], writes=[TK(1)])
            for jj in (1, 2):
                dop('dve', (lambda e, jj=jj: e.scalar_tensor_tensor(out=xc, in0=convT[:, jj, c, :], scalar=cw(jj), in1=xc, op0=ALU.mult, op1=ALU.add)),
                    reads=['d_convT', 'prm', TK(1)], writes=[TK(1)])
            dop('dve', lambda e: e.scalar_tensor_tensor(out=xc, in0=xaT[:, c, :], scalar=cw(3), in1=xc, op0=ALU.mult, op1=ALU.add),
                reads=['d_xaT', 'prm', TK(1)], writes=[TK(1)])
            dop('act', lambda e: e.activation(out=xb, in_=xc, func=AF.Copy), reads=[TK(1)], writes=[xbk])
            pR, pRk = bank()
            dop('pe', lambda e: e.matmul(pR[:, 0:N], wbd[:, (0 * DEPTH + l) * KC + c, :], xb, start=True, stop=True), reads=['wbd0'] + [f'wbd_{a_}{b_}{c_}' for a_ in range(2) for b_ in range(2) for c_ in range(2)] + [xbk], writes=[pRk])
            pI, pIk = bank()
            dop('pe', lambda e: e.matmul(pI[:, 0:N], wbd[:, (1 * DEPTH + l) * KC + c, :], xb, start=True, stop=True), reads=['wbd0'] + [f'wbd_{a_}{b_}{c_}' for a_ in range(2) for b_ in range(2) for c_ in range(2)] + [xbk], writes=[pIk])
            act_sig(dop, r_, pR[:, 0:N], [pRk], TK(2), nbias=nprm[:, c, 10 * l + R_BR:10 * l + R_BR + 1])
            act_sig(dop, i_, pI[:, 0:N], [pIk], TK(3), nbias=nprm[:, c, 10 * l + R_BI:10 * l + R_BI + 1])
            dop('act', lambda e: e.activation(out=r_, in_=r_, func=AF.Exp, scale=nsp[:, l, c:c + 1]), reads=[TK(2), 'nsp'], writes=[TK(2)])
            dop('act', lambda e: e.activation(out=m_, in_=r_, func=AF.Square), reads=[TK(2)], writes=[TK(4)])
            act_rpow(dop, m_, m_, [TK(4), 'onep'], TK(4), 0.5, scale=-1.0, bias=onep[:, 0:1])
            dop('dve', lambda e: e.tensor_tensor(out=i_, in0=i_, in1=xc, op=ALU.mult), reads=[TK(3), TK(1)], writes=[TK(3)])
            dop('dve', lambda e: e.tensor_tensor(out=i_, in0=i_, in1=m_, op=ALU.mult), reads=[TK(3), TK(4)], writes=[TK(3)])
            dop('dve', lambda e: e.tensor_tensor(out=hT[:, c, :], in0=r_, in1=h0T[:, c, :], op=ALU.mult), reads=[TK(2), 'd_h0T'], writes=['d_hT'])
            dop('dve', lambda e: e.tensor_tensor(out=hT[:, c, :], in0=hT[:, c, :], in1=i_, op=ALU.add), reads=['d_hT', TK(3)], writes=['d_hT'])
            dop('dve', lambda e: e.tensor_tensor(out=mg_d[:, c, :], in0=hT[:, c, :], in1=sz, op=ALU.mult), reads=['d_hT', TK(0)], writes=['mgd'])

        for g in range(2):
            wa, wak = load_w(win_cols(l, 512 * g, 512), (KC, 512))
            wz, wzk = load_w(win_cols(l, 1024 + 512 * g, 512), (KC, 512))
            for j in range(4):
                rg_chunk(4 * g + j, j, wa, wak, wz, wzk)
        to_tok(xaT, 'd_xaT', otok, 'd_otok')
        ddma('sp', s_conv[l, :, 2, :], otok, reads=['d_otok'], sem='d_otok')
        to_tok(hT, 'd_hT', otok, 'd_otok')
        ddma('sp', s_h[l], otok, reads=['d_otok'], sem='d_otok')

        def ml_head_proj(h):
            (wq, wqk), (wk_, wkk) = load_w2(win_cols(l, 2048 + 256 * h, 256), win_cols(l, 3072 + 256 * h, 256))
            for dkc in range(2):
                pt, pk = proj_fm(wq, wqk, dkc, N, hn_fn=hnd_fn, hnk_fn=hndk_fn)
                dop('dve', (lambda e, pt=pt, dkc=dkc: e.tensor_copy(out=qTd[:, h, dkc, :], in_=pt[:, 0:N])), reads=[pk], writes=['d_qT'])
                pt, pk = proj_fm(wk_, wkk, dkc, N, hn_fn=hnd_fn, hnk_fn=hndk_fn)
                dop('act', (lambda e, pt=pt, dkc=dkc: e.activation(out=kTd[:, h, dkc, :], in_=pt[:, 0:N], func=AF.Copy, scale=DK ** -0.5)), reads=[pk], writes=['d_kT'])
            (wv_, wvk), (wo_, wok) = load_w2(win_cols(l, 4096 + 256 * h, 256), win_cols(l, 5120 + 256 * h, 256))
            wzb_, wzbk = load_w(win_cols(l, 6144 + 256 * h, 256), (KC, 256))
            for dvc in range(2):
                pt, pk = proj_fm(wv_, wvk, dvc, N, hn_fn=hnd_fn, hnk_fn=hndk_fn)
                dop('act', (lambda e, pt=pt, dvc=dvc: e.activation(out=vTd[:, h, dvc, :], in_=pt[:, 0:N], func=AF.Copy)), reads=[pk], writes=['d_vT'])
                pt, pk = proj_fm(wo_, wok, dvc, N, hn_fn=hnd_fn, hnk_fn=hndk_fn)
                act_sig(dop, sod[:, h, dvc, :], pt[:, 0:N], [pk], 'd_so')
                pt, pk = proj_fm(wzb_, wzbk, dvc, N, hn_fn=hnd_fn, hnk_fn=hndk_fn)
                act_sig(dop, szbd[:, h, dvc, :], pt[:, 0:N], [pk], 'd_szb')
                dop('dve', (lambda e, pt=pt, dvc=dvc: e.tensor_tensor(out=szbd[:, h, dvc, :], in0=szbd[:, h, dvc, :], in1=pt[:, 0:N], op=ALU.mult)),
                    reads=[pk, 'd_szb'], writes=['d_szb'])
            pt, pk = bank()
            for dkc in range(2):
                dop('pe', (lambda e, dkc=dkc: e.transpose(out=pt[0:N, dkc * 128:(dkc + 1) * 128], in_=kTd[:, h, dkc, :], identity=ident)), reads=['d_kT', 'cst'], writes=[pk])
            for dvc in range(2):
                dop('pe', (lambda e, dvc=dvc: e.transpose(out=pt[0:N, 256 + dvc * 128:256 + (dvc + 1) * 128], in_=vTd[:, h, dvc, :], identity=ident)), reads=['d_vT', 'cst'], writes=[pk])
            dop('dve', lambda e: e.tensor_scalar(out=kw_r[:, h * 256:(h + 1) * 256], in0=pt[0:N, 0:256], scalar1=gt[:, 20 + h:21 + h], scalar2=None, op0=ALU.mult),
                reads=[pk, 'd_gt'], writes=['d_ktok'])
            dop('dve', lambda e: e.tensor_copy(out=v_tok[:, h * 256:(h + 1) * 256], in_=pt[0:N, 256:512]), reads=[pk], writes=['d_vtok'])
            dop('dve', lambda e: e.scalar_tensor_tensor(out=n_tok[:, h * 256:(h + 1) * 256], in0=n_tok[:, h * 256:(h + 1) * 256], scalar=gt[:, 24 + h:25 + h],
                                                       in1=kw_f[:, h * 256:(h + 1) * 256], op0=ALU.mult, op1=ALU.add),
                reads=['d_n', 'd_nT', 'd_gt', 'd_ktok'], writes=['d_n'])
            for q in range(3):
                dop('dve', (lambda e, q=q: e.tensor_scalar(out=rhs3[:, 16 * q:16 * q + 16], in0=id16, scalar1=gt[:, 20 + 4 * q + h:21 + 4 * q + h], scalar2=None, op0=ALU.mult)),
                    reads=['cst', 'd_gt'], writes=['d_rhs3'])
            pB, pBk = bank()
            dop('pe', lambda e: e.matmul(pB[:, 0:48], ones16, rhs3, start=True, stop=True), reads=['cst', 'd_rhs3'], writes=[pBk])
            dop('dve', lambda e: e.tensor_copy(out=bc[:, h, 0:48], in_=pB[:, 0:48]), reads=[pBk], writes=['d_bc'])
            t0 = tmp(0, 2).rearrange("p (c b) -> p c b", c=2)
            t1 = tmp(2, 2).rearrange("p (c b) -> p c b", c=2)
            dop('dve', lambda e: e.tensor_tensor(out=t0, in0=qTd[:, h, :, :], in1=kTd[:, h, :, :], op=ALU.mult), reads=['d_qT', 'd_kT'], writes=['d_t0', 'd_t1'])
            dop('dve', lambda e: e.tensor_tensor(out=t1, in0=qTd[:, h, :, :], in1=nTd[:, h, :, :], op=ALU.mult), reads=['d_qT', 'd_nT'], writes=['d_t2', 'd_t3'])
            pQ, pQk = bank()
            for dkc in range(2):
                dop('pe', (lambda e, dkc=dkc: e.matmul(pQ[:, 0:N], onesf, t0[:, dkc, :], start=(dkc == 0), stop=(dkc == 1))), reads=['cst', 'd_t0', 'd_t1'], writes=[pQk])
            for dkc in range(2):
                dop('pe', (lambda e, dkc=dkc: e.matmul(pQ[:, N:2 * N], onesf, t1[:, dkc, :], start=(dkc == 0), stop=(dkc == 1))), reads=['cst', 'd_t2', 'd_t3'], writes=[pQk])
            dop('dve', lambda e: e.tensor_copy(out=bc[:, h, 48:80], in_=pQ[:, 0:2 * N]), reads=[pQk], writes=['d_bc'])

        for h in range(NH):
            ml_head_proj(h)
        ddma('sp', s_n[l], n_tok.rearrange("p (h k) -> p h k", h=NH), reads=['d_n'], sem='d_n')

        dop('dve', lambda e: e.memset(gt[:, 40:41], 0.0), reads=['d_convT', 'd_h0T'], writes=['d_cs', 'd_qmf'])
        for b in range(N):
            for h in range(NH):
                for dkc in range(2):
                    g = h * 2 + dkc
                    if (g + b) % 2 == 0:
                        dop('dve', (lambda e, g=g, b=b, h=h, dkc=dkc: e.tensor_scalar(out=qm[:, g, b, :], in0=Em[:, b, :], scalar1=qTd[:, h, dkc, b:b + 1],
                                                                                   scalar2=None, op0=ALU.mult)),
                            reads=['d_em', 'd_qT', 'd_qmf'], writes=[f'd_qm{b}_{g}'])
                    else:
                        dop('act', (lambda e, g=g, b=b, h=h, dkc=dkc: e.activation(out=qm[:, g, b, :], in_=Em[:, b, :], func=AF.Copy, scale=qTd[:, h, dkc, b:b + 1])),
                            reads=['d_em', 'd_qT', 'd_qmf'], writes=[f'd_qm{b}_{g}'])
        pCqs = [bank(), bank()]
        pCis = [ps.index(p_) for p_, _ in pCqs]
        for i_ in pCis:
            reserved.add(i_)
        def c_load(b):
            ddma('sp', Cin[b % NCB], st_C[l, b].rearrange("h (c p) v -> p h c v", p=128), writes=[f'd_Cin{b % NCB}'], sem=f'd_Cin{b % NCB}')
        for b in range(NCB - 1):
            c_load(b)
        for b in range(N):
            ci, cik = Cin[b % NCB], f'd_Cin{b % NCB}'
            cb_, cbk = Cbf[b % 2], f'd_Cbf{b % 2}'
            if b + NCB - 1 < N:
                c_load(b + NCB - 1)
            for h2 in range(2):
                dop('dve' if h2 == 0 else 'act',
                    ((lambda e, ci=ci, cb_=cb_, h2=h2: e.tensor_copy(out=cb_[:, 2 * h2:2 * h2 + 2, :, :], in_=ci[:, 2 * h2:2 * h2 + 2, :, :])) if h2 == 0 else
                     (lambda e, ci=ci, cb_=cb_, h2=h2: e.activation(out=cb_[:, 2 * h2:2 * h2 + 2, :, :], in_=ci[:, 2 * h2:2 * h2 + 2, :, :], func=AF.Copy))),
                    reads=[cik], writes=[cbk])
            for h in range(NH):
                pq, pqk = pCqs[h // 2]
                for dkc in range(2):
                    first = (b == 0 and h % 2 == 0 and dkc == 0)
                    dop('pe', (lambda e, h=h, dkc=dkc, cb_=cb_, b=b, pq=pq, first=first: e.matmul(pq[0:N, (h % 2) * 256:(h % 2) * 256 + 256], qm[:, h * 2 + dkc, b, :],
                                                                                         cb_[:, h, dkc, :], start=first, stop=(b == N - 1 and dkc == 1),
                                                                                         skip_group_check=True)),
                        reads=[cbk, f'd_qm{b}_{h * 2 + dkc}'], writes=[pqk])
            for h in range(NH):
                vi = (b * NH + h) % 4
                dop('dve', (lambda e, h=h, vi=vi, b=b: e.tensor_scalar(out=vm[vi], in0=v_tok[:, h * 256:(h + 1) * 256], scalar1=cst[0:N, C_ID + b:C_ID + b + 1],
                                                                   scalar2=None, op0=ALU.mult)), reads=['d_vtok', 'cst'], writes=[f'd_vm{vi}'])
                for dkc in range(2):
                    pU, pUk = bank()
                    dop('pe', (lambda e, h=h, dkc=dkc, vi=vi, pU=pU: e.matmul(pU[:, 0:256], kw_r[:, h * 256 + dkc * 128:h * 256 + (dkc + 1) * 128], vm[vi],
                                                                          start=True, stop=True)), reads=['d_ktok', f'd_vm{vi}'], writes=[pUk])
                    dop('dve', (lambda e, h=h, dkc=dkc, pU=pU, ci=ci, b=b: e.scalar_tensor_tensor(out=ci[:, h, dkc, :], in0=ci[:, h, dkc, :],
                                                                                         scalar=bc[:, h, 16 + b:17 + b], in1=pU[:, 0:256],
                                                                                         op0=ALU.mult, op1=ALU.add)),
                        reads=[cik, 'd_bc', pUk], writes=[cik])
            ddma('pool', s_C[l, b].rearrange("h (c p) v -> p h c v", p=128), ci, reads=[cik], sem=f'd_Cst{b % NCB}')

        for i_, (pq, pqk) in enumerate(pCqs):
            dop('act', (lambda e, pq=pq, i_=i_: e.activation(out=cq_tok[:, i_ * 512:(i_ + 1) * 512], in_=pq[0:N, :], func=AF.Copy)),
                reads=[pqk, 'd_ktok'], writes=['d_vtok'])
        to_fm(cq_tok, 'd_vtok', CqT.rearrange("p h c b -> p (h c) b"), 'd_CqT')
        for i_ in pCis:
            reserved.discard(i_)
        def ml_head_out(h):
            Pb, den, dn2, rs_ = tmp(4), tmp(5), tmp(6), tmp(11)
            wb, gib, enb, qkb, qnb = bc[:, h, 0:16], bc[:, h, 16:32], bc[:, h, 32:48], bc[:, h, 48:64], bc[:, h, 64:80]
            dop('dve', lambda e: e.tensor_tensor(out=Pb, in0=wb, in1=qkb, op=ALU.mult), reads=['d_bc'], writes=['d_t4'])
            dop('dve', lambda e: e.tensor_tensor(out=den, in0=gib, in1=qnb, op=ALU.mult), reads=['d_bc'], writes=['d_t5'])
            dop('dve', lambda e: e.tensor_tensor(out=den, in0=den, in1=Pb, op=ALU.add), reads=['d_t5', 'd_t4'], writes=['d_t5'])
            dop('dve', lambda e: e.tensor_tensor(out=dn2, in0=den, in1=enb, op=ALU.max), reads=['d_t5', 'd_bc'], writes=['d_t6'])
            dop('dve', lambda e: e.scalar_tensor_tensor(out=dn2, in0=den, scalar=-1.0, in1=dn2, op0=ALU.mult, op1=ALU.max), reads=['d_t5', 'd_t6'], writes=['d_t6'])
            act_rpow(dop, dn2, dn2, ['d_t6'], 'd_t6', -1.0)
            pQ2, pQ2k = bank()
            for dvc in range(2):
                y, t2, ysq_ = tmp(7 + dvc), tmp(9), tmp(12 + dvc)
                yk, ysk = f'd_t{7 + dvc}', f'd_t{12 + dvc}'
                col = (h * 2 + dvc) * N
                dop('dve', (lambda e, y=y, dvc=dvc: e.tensor_tensor(out=y, in0=CqT[:, h, dvc, :], in1=gib, op=ALU.mult)), reads=['d_CqT', 'd_bc'], writes=[yk])
                dop('dve', (lambda e, t2=t2, dvc=dvc: e.tensor_tensor(out=t2, in0=vTd[:, h, dvc, :], in1=Pb, op=ALU.mult)), reads=['d_vT', 'd_t4'], writes=['d_t9'])
                dop('dve', (lambda e, y=y, t2=t2: e.tensor_tensor(out=y, in0=y, in1=t2, op=ALU.add)), reads=[yk, 'd_t9'], writes=[yk])
                dop('dve', (lambda e, y=y: e.tensor_tensor(out=y, in0=y, in1=dn2, op=ALU.mult)), reads=[yk, 'd_t6'], writes=[yk])
                dop('dve', (lambda e, y=y, dvc=dvc: e.tensor_tensor(out=y, in0=y, in1=sod[:, h, dvc, :], op=ALU.mult)), reads=[yk, 'd_so'], writes=[yk])
                dop('act', (lambda e, y=y, ysq_=ysq_: e.activation(out=ysq_, in_=y, func=AF.Square)), reads=[yk], writes=[ysk])
                dop('pe', (lambda e, ysq_=ysq_, dvc=dvc: e.matmul(pQ2[:, 0:N], onesf, ysq_, start=(dvc == 0), stop=(dvc == 1))), reads=['cst', ysk], writes=[pQ2k])
            act_rpow(dop, rs_, pQ2[:, 0:N], [pQ2k, 'epsc'], 'd_t11', -0.5, scale=1.0 / DK, bias=epsc[:, 0:1])
            for dvc in range(2):
                y = tmp(7 + dvc)
                yk = f'd_t{7 + dvc}'
                fc = 2 * h + dvc
                dop('dve', (lambda e, y=y, fc=fc: e.scalar_tensor_tensor(out=y, in0=y, scalar=prm[:, fc, 10 * l + R_GMH:10 * l + R_GMH + 1], in1=rs_,
                                                                       op0=ALU.mult, op1=ALU.mult)), reads=[yk, 'd_t11', 'prm'], writes=[yk])
                dop('dve', (lambda e, y=y, fc=fc, dvc=dvc: e.tensor_tensor(out=mg_d[:, 8 + fc, :], in0=y, in1=szbd[:, h, dvc, :], op=ALU.mult)),
                    reads=[yk, 'd_szb'], writes=['mgd'])

        for h in range(NH):
            ml_head_out(h)

        for g4 in range(4):
            wo4, wo4k = load_w(wout_cols(l, 256 * g4, 256), (16, 256))
            for dd in range(2):
                dc = 2 * g4 + dd
                pt, pk = bank()
                for kc in range(16):
                    dop('pe', (lambda e, pt=pt, kc=kc, dd=dd, wo4=wo4: e.matmul(pt[:, 0:N], wo4[:, kc, dd * 128:(dd + 1) * 128], mg_d[:, kc, :],
                                                                            start=(kc == 0), stop=(kc == 15))), reads=[wo4k, 'mgd'], writes=[pk])
                dop('dve', (lambda e, pt=pt, dc=dc: e.tensor_tensor(out=x_d[:, dc, :], in0=x_d[:, dc, :], in1=pt[:, 0:N], op=ALU.add)), reads=[pk, 'xd'], writes=['xd'])

    if do_decode:
        ddma('sp', otok, x_s, writes=['d_otok'], sem='d_otok')
        ddma('sp', gbt[:], gb_t, writes=['d_gbt'], sem='d_gbt')
        ddma('sp', HF[:, 560:816], emask, writes=['d_em'], sem='d_em')
        to_fm(otok, 'd_otok', x_d[:], 'xd')
        for l in range(DEPTH):
            decode_layer(l)
        rmsnorm(lambda kc: 'xd', lambda kc: prm[:, kc, R_GF:R_GF + 1], lambda kc: hT[:, kc, :], lambda kc: 'd_hT', N,
                src_fn=lambda kc: x_d[:, kc, :], extra=['dbar'])
        to_tok(hT, 'd_hT', otok, 'd_otok')
        ddma('sp', y_s, otok, reads=['d_otok'], sem='d_otok')

    S.emit()
    return nc, es


def make_consts():
    c = np.zeros((128, NCONST), np.float32)
    c[:, C_ID:C_ID + 128] = np.eye(128, dtype=np.float32)
    c[:, C_ONE:C_ONE + 128] = 1.0
    s = np.arange(128)
    c[:, C_MASK:C_MASK + 128] = (s[:, None] <= s[None, :]).astype(np.float32)
    for h in range(NH):
        c[h, C_SEL + 128 * h:C_SEL + 128 * h + 128] = 1.0
    c[:, C_ONES4:C_ONES4 + T] = 1.0
    return c


_CACHE = {}


def kernel(**inputs):
    f = lambda k: np.ascontiguousarray(np.asarray(inputs[k], dtype=np.float32))
    if 'nc' not in _CACHE:
        _CACHE['nc'] = build_program()
    nc, _es = _CACHE['nc']
    consts = make_consts()
    shared = {k: f(k) for k in ("g_norm", "w_in", "conv_w", "conv_b", "w_rgate", "b_rgate", "w_igate", "b_igate",
                                "lru_lambda", "b_mi", "b_mf", "g_mhead", "w_out", "g_final")}
    xp = f("x_prompt")
    xs = f("x_sample").reshape(128, D)
    sth, stc, stC, stn, stm = f("state_rglru_h"), f("state_rglru_conv"), f("state_mlstm_C"), f("state_mlstm_n"), f("state_mlstm_m")
    gb = np.ascontiguousarray(np.tile(np.concatenate([shared["b_mi"].reshape(-1), shared["b_mf"].reshape(-1)])[None, :], (DB, 1)))
    em = np.ascontiguousarray(np.tile(np.eye(DB, dtype=np.float32).reshape(1, DB * DB), (128, 1)))
    in_maps = []
    for i in range(NCORES):
        m = dict(shared)
        sl = slice(i * DB, (i + 1) * DB)
        m["consts"] = consts
        m["x_p"] = xp[i]
        m["x_s"] = np.ascontiguousarray(xs[sl])
        m["st_h"] = np.ascontiguousarray(sth[:, sl]); m["st_conv"] = np.ascontiguousarray(stc[:, sl])
        m["st_C"] = np.ascontiguousarray(stC[:, sl]); m["st_n"] = np.ascontiguousarray(stn[:, sl]); m["st_m"] = np.ascontiguousarray(stm[:, sl])
        m["gb_t"] = gb
        m["emask"] = em
        in_maps.append(m)
    res = run_bass_kernel_spmd(nc, in_maps, core_ids=list(range(NCORES)))
    R = res.results
    g = lambda k: np.stack([np.asarray(R[i][k], dtype=np.float32) for i in range(NCORES)], axis=0)
    gc = lambda k, ax: np.ascontiguousarray(np.concatenate([np.asarray(R[i][k], dtype=np.float32) for i in range(NCORES)], axis=ax))
    y_prompt = g("y_p")
    p_h = np.ascontiguousarray(g("p_h").transpose(1, 0, 2))
    p_conv = np.ascontiguousarray(g("p_conv").transpose(1, 0, 2, 3))
    p_C = np.ascontiguousarray(g("p_C").transpose(1, 0, 2, 3, 4))
    p_n = np.ascontiguousarray(g("p_n").transpose(1, 0, 2, 3))
    p_m = np.ascontiguousarray(g("p_m").transpose(1, 0, 2))
    y_s = gc("y_s", 0).reshape(128, 1, D)
    s_h = gc("s_h", 1); s_conv = gc("s_conv", 1); s_C = gc("s_C", 1); s_n = gc("s_n", 1); s_m = gc("s_m", 1)
    return (y_prompt, y_s, p_h, p_conv, p_C, p_n, p_m, s_h, s_conv, s_C, s_n, s_m)
```
